# Optimizing a Trainium2 kernel written in Bass

```python
import math
import jax
import jax.numpy as jnp
from jax import lax
import numpy as np

D_MODEL = 1024
BATCH = 32
SEQ = 256
DEPTH = 4
DEC_BATCH = 8
DEC_SEQ = 4096
PAST_LEN = 256

GRID_W = 64
W_A = D_MODEL // 2
W_B = D_MODEL // 2
W_C = D_MODEL // 2
W_D = D_MODEL // 2
W_MIX = W_A + W_B
SC_WIDTH = 3
LRU_CONV = 4
LRU_HEADS = 8
LRU_BLK = W_B // LRU_HEADS
RG_C = 8.0
HY_CONV = 3
HY_ORDER = 2
HY_BANDS = 16
HY_POS_DIM = 1 + 2 * HY_BANDS
HY_FFN = 64
HY_TARGET = 1e-2
HY_SHORT_PCT = 0.3
HY_LONG_PCT = 1.5
RW_HEAD = 64
RW_HEADS = W_D // RW_HEAD
RW_LORA_W = 64
RW_LORA_A = 64
D_FF = 4 * D_MODEL
N_AB = (DEPTH + 1) // 2
N_CD = DEPTH // 2
DN_ALPHA = (2 * DEPTH) ** 0.25
DN_BETA = (8 * DEPTH) ** -0.25
LN_EPS = 1e-5
GN_EPS = 64e-5

kernel_name = 'hybrid_diffusion_conv_lru_hyena_rwkv7_step'


def _layer_norm(x, g, b):
    xf = x.astype(jnp.float32)
    xc = xf - jnp.mean(xf, -1, keepdims=True)
    var = jnp.mean(xc * xc, -1, keepdims=True)
    return (xc * lax.rsqrt(var + LN_EPS) * g.astype(jnp.float32) + b.astype(jnp.float32)).astype(x.dtype)


def _dwconv(x, w, pad_left, line):
    K, L = w.shape[0], x.shape[1]
    xp = jnp.pad(x, ((0, 0), (pad_left, K - 1 - pad_left), (0, 0)))
    pos = None if line is None else jnp.arange(L) % line
    out = 0.0
    for k in range(K):
        off = k - pad_left
        term = xp[:, k:k + L] * w[k]
        if pos is not None and off != 0:
            valid = ((pos + off >= 0) & (pos + off < line))[None, :, None]
            term = jnp.where(valid, term, jnp.zeros_like(term))
        out = out + term
    return out


def _tshift(x, line):
    w = jnp.array([0.5, 0.0, 0.5], x.dtype)[:, None]
    return _dwconv(x, w, 1, line)


def _to_colmajor(t, rows):
    b, L, ch = t.shape
    return t.reshape(b, rows, GRID_W, ch).transpose(0, 2, 1, 3).reshape(b, L, ch)


def _from_colmajor(t, rows):
    b, L, ch = t.shape
    return t.reshape(b, GRID_W, rows, ch).transpose(0, 2, 1, 3).reshape(b, L, ch)


def _lin_combine(e1, e2):
    a1, b1 = e1
    a2, b2 = e2
    return a1 * a2, a2 * b1 + b2


def _rglru_dir(xc, wa, ba, wi, bi, lam, h0):
    b, L, ch = xc.shape
    xf = xc.astype(jnp.float32)
    xb = xf.reshape(b, L, LRU_HEADS, LRU_BLK)
    r = jax.nn.sigmoid(jnp.einsum('blhi,hij->blhj', xb, wa).reshape(b, L, ch) + ba)
    i = jax.nn.sigmoid(jnp.einsum('blhi,hij->blhj', xb, wi).reshape(b, L, ch) + bi)
    log_a = -RG_C * r * jax.nn.softplus(-lam.astype(jnp.float32))
    a = jnp.exp(log_a)
    u = jnp.sqrt(-jnp.expm1(2.0 * log_a)) * (i * xf)
    u = u.at[:, 0].add(a[:, 0] * h0.astype(jnp.float32))
    _, h = lax.associative_scan(_lin_combine, (a, u), axis=1)
    return h, h[:, -1]


def _rwkv_dir(r, w, k, v, kk, a, s0):
    xs = tuple(jnp.moveaxis(t, 1, 0) for t in (r, w, k, v, kk, a))

    def step(S, inp):
        r_t, w_t, k_t, v_t, kk_t, a_t = inp
        sa = jnp.einsum('bhvk,bhk->bhv', S, kk_t)
        S = (S * w_t[:, :, None, :] - sa[..., None] * (kk_t * a_t)[:, :, None, :]
             + v_t[..., None] * k_t[:, :, None, :])
        return S, jnp.einsum('bhvk,bhk->bhv', S, r_t)

    s_fin, o = lax.scan(step, s0, xs)
    return jnp.moveaxis(o, 0, 1), s_fin


def _hyena_filters(L, w1, b1, w2, b2, w3, freq):
    f32 = jnp.float32
    tn = jnp.linspace(0.0, 1.0, L, dtype=f32)
    tr = jnp.arange(L, dtype=f32)
    bands = jnp.linspace(1e-4, HY_BANDS - 1, HY_BANDS, dtype=f32)
    ang = (2.0 * math.pi / L) * tr[:, None] * bands[None, :]
    feats = jnp.concatenate([tn[:, None], jnp.cos(ang), -jnp.sin(ang)], -1)
    fr = freq.astype(f32)
    hid = jnp.sin(fr * (jnp.dot(feats, w1.astype(f32)) + b1))
    hid = jnp.sin(fr * (jnp.dot(hid, w2.astype(f32)) + b2))
    raw = jnp.dot(hid, w3.astype(f32)).reshape(L, HY_ORDER, 2, W_C)
    deltas = jnp.abs(jnp.linspace(math.log(HY_TARGET) / HY_LONG_PCT, math.log(HY_TARGET) / HY_SHORT_PCT, W_C, dtype=f32))
    decay = jnp.exp(-tn[:, None] * deltas[None, :])
    filt = raw * decay[:, None, None, :]
    circ = jnp.concatenate([filt[:, :, 0], jnp.zeros((1, HY_ORDER, W_C), f32), filt[:0:-1, :, 1]], 0)
    return circ * lax.rsqrt(jnp.sum(circ * circ, 0, keepdims=True) + 1e-6)


def _fft_conv(z, f):
    L = z.shape[1]
    n = 2 * L
    zf = jnp.fft.rfft(z.astype(jnp.float32), n=n, axis=1)
    ff = jnp.fft.rfft(f, n=n, axis=0)
    return jnp.fft.irfft(zf * ff[None], n=n, axis=1)[:, :L]


def _mixer_ab(h, j, h0, line, p):
    proj = jnp.dot(h, p['ab_w_in'][j])
    s_b, s_c, s_v, g_lru, x_lru = jnp.split(proj, [W_A, 2 * W_A, 3 * W_A, 3 * W_A + W_B], axis=-1)
    y_a = s_b * _dwconv(s_c * s_v, p['sc_conv'][j], 1, line)
    xc = _dwconv(x_lru, p['lru_conv'][j], 2, line) + p['lru_conv_b'][j]
    hf, sf = _rglru_dir(xc, p['lru_wa'][j, 0], p['lru_ba'][j, 0], p['lru_wi'][j, 0], p['lru_bi'][j, 0], p['lru_lambda'][j, 0], h0[:, 0])
    hb, sb = _rglru_dir(jnp.flip(xc, 1), p['lru_wa'][j, 1], p['lru_ba'][j, 1], p['lru_wi'][j, 1], p['lru_bi'][j, 1], p['lru_lambda'][j, 1], h0[:, 1])
    y_b = jax.nn.gelu(g_lru) * (hf + jnp.flip(hb, 1)).astype(h.dtype)
    return jnp.concatenate([y_a, y_b], -1), jnp.stack([sf, sb], 1)


def _mixer_cd(h, j, s0, line, p):
    f32 = jnp.float32
    bsz, L, _ = h.shape
    proj = jnp.dot(h, p['cd_w_in'][j])
    u = _dwconv(proj[..., :3 * W_C], p['hy_conv'][j], 1, line)
    hv, hx1, hx2 = jnp.split(u.astype(f32), 3, axis=-1)
    filt = _hyena_filters(L, p['hy_w1'][j], p['hy_b1'][j], p['hy_w2'][j], p['hy_b2'][j], p['hy_w3'][j], p['hy_freq'][j])
    bias = p['hy_bias'][j].astype(f32)
    z = hx1 * (_fft_conv(hv, filt[:, 0]) + bias[0] * hv)
    y_c = hx2 * (_fft_conv(z, filt[:, 1]) + bias[1] * z)

    mu = p['rw_mu'][j]
    r, k, v, g = [t + (_tshift(t, line) - t) * mu[n] for n, t in enumerate(jnp.split(proj[..., 3 * W_C:], 4, axis=-1))]
    dh = _tshift(h, line) - h
    xw = h + dh * p['rw_mu_x'][j, 0]
    xa = h + dh * p['rw_mu_x'][j, 1]

    def heads(t):
        return t.astype(f32).reshape(bsz, L, RW_HEADS, RW_HEAD)

    rh, vh = heads(r), heads(v)
    kk = heads(k * p['rw_kk'][j])
    kk = kk * lax.rsqrt(jnp.sum(kk * kk, -1, keepdims=True) + 1e-12)
    rk = p['rw_rk'][j].astype(f32)
    o_sum = 0.0
    bonus = 0.0
    finals = []
    for d in range(2):
        w_raw = -jax.nn.softplus(-(p['rw_w0'][j, d] + jnp.dot(jnp.tanh(jnp.dot(xw, p['rw_w1'][j, d])), p['rw_w2'][j, d]))) - 0.5
        decay = heads(jnp.exp(-jnp.exp(w_raw.astype(f32))))
        a = jax.nn.sigmoid(p['rw_a0'][j, d] + jnp.dot(jnp.dot(xa, p['rw_a1'][j, d]), p['rw_a2'][j, d]))
        kd = k * (1.0 + (a - 1.0) * p['rw_ka'][j])
        ah, kdh = heads(a), heads(kd)
        seq = (rh, decay, kdh, vh, kk, ah)
        if d == 1:
            seq = tuple(jnp.flip(t, 1) for t in seq)
        o, s_fin = _rwkv_dir(seq[0], seq[1], seq[2], seq[3], seq[4], seq[5], s0[:, d].astype(f32))
        if d == 1:
            o = jnp.flip(o, 1)
        o_sum = o_sum + o
        bonus = bonus + jnp.sum(rh * kdh * rk, -1, keepdims=True) * vh
        finals.append(s_fin)
    oc = o_sum - jnp.mean(o_sum, -1, keepdims=True)
    on = oc * lax.rsqrt(jnp.mean(oc * oc, -1, keepdims=True) + GN_EPS)
    on = on.reshape(bsz, L, W_D) * p['rw_gn_g'][j].astype(f32) + p['rw_gn_b'][j].astype(f32)
    y_d = (on + bonus.reshape(bsz, L, W_D)) * jax.nn.sigmoid(g.astype(f32))
    return jnp.concatenate([y_c, y_d], -1).astype(h.dtype), jnp.stack(finals, 1)


def _trunk(x, cvec, init_lru, init_rwkv, rows, p):
    bsz = x.shape[0]
    if init_lru is None:
        init_lru = jnp.zeros((bsz, N_AB, 2, W_B), jnp.float32)
        init_rwkv = jnp.zeros((bsz, N_CD, 2, RW_HEADS, RW_HEAD, RW_HEAD), jnp.float32)
    new_lru = []
    new_rwkv = []
    for l in range(DEPTH):
        j = l // 2
        mods = jnp.dot(jax.nn.silu(cvec), p['w_mod'][l]) + p['b_mod'][l]
        sh1, sc1, g1, sh2, sc2, g2 = [m[:, None, :] for m in jnp.split(mods, 6, axis=-1)]
        h = x * (1.0 + sc1) + sh1
        if l % 2 == 0:
            line = None if rows is None else GRID_W
            mix, st = _mixer_ab(h, j, init_lru[:, j], line, p)
            y = jnp.dot(mix, p['w_out'][l])
            new_lru.append(st)
        else:
            if rows is None:
                mix, st = _mixer_cd(h, j, init_rwkv[:, j], None, p)
                y = jnp.dot(mix, p['w_out'][l])
            else:
                mix, st = _mixer_cd(_to_colmajor(h, rows), j, init_rwkv[:, j], rows, p)
                y = _from_colmajor(jnp.dot(mix, p['w_out'][l]), rows)
            new_rwkv.append(st)
        x = _layer_norm(DN_ALPHA * x + g1 * y, p['ln1_g'][l], p['ln1_b'][l])
        h = x * (1.0 + sc2) + sh2
        y = jnp.dot(jnp.square(jax.nn.relu(jnp.dot(h, p['mlp_w1'][l]))), p['mlp_w2'][l])
        x = _layer_norm(DN_ALPHA * x + g2 * y, p['ln2_g'][l], p['ln2_b'][l])
    return x, jnp.stack(new_lru, 1), jnp.stack(new_rwkv, 1)


def setup_inputs(seed: int = 0) -> dict:
    key = jax.random.key(seed)
    ks = iter(jax.random.split(key, 64))
    f32 = jnp.float32

    def nrm(shape, s):
        return jax.random.normal(next(ks), shape, f32) * s

    def uni(shape, lo, hi):
        return jax.random.uniform(next(ks), shape, f32, lo, hi)

    lam_a = uni((N_AB, 2, W_B), 0.9, 0.999) ** (1.0 / RG_C)
    return {
        'x_prompt': nrm((BATCH, SEQ, D_MODEL), 1.0),
        'x_sample': nrm((DEC_BATCH, DEC_SEQ, D_MODEL), 1.0),
        'state_lru': nrm((DEC_BATCH, N_AB, 2, W_B), 0.5),
        'state_rwkv': nrm((DEC_BATCH, N_CD, 2, RW_HEADS, RW_HEAD, RW_HEAD), 0.5),
        'c': nrm((DEC_BATCH, D_MODEL), 1.0),
        'c_ctx': nrm((D_MODEL,), 1.0),
        'w_mod': nrm((DEPTH, D_MODEL, 6 * D_MODEL), 0.5 * D_MODEL ** -0.5),
        'b_mod': nrm((DEPTH, 6 * D_MODEL), 0.02),
        'ln1_g': 1.0 + nrm((DEPTH, D_MODEL), 0.02),
        'ln1_b': nrm((DEPTH, D_MODEL), 0.01),
        'ln2_g': 1.0 + nrm((DEPTH, D_MODEL), 0.02),
        'ln2_b': nrm((DEPTH, D_MODEL), 0.01),
        'mlp_w1': nrm((DEPTH, D_MODEL, D_FF), D_MODEL ** -0.5),
        'mlp_w2': nrm((DEPTH, D_FF, D_MODEL), DN_BETA * D_FF ** -0.5),
        'w_out': nrm((DEPTH, W_MIX, D_MODEL), DN_BETA * W_MIX ** -0.5),
        'ab_w_in': nrm((N_AB, D_MODEL, 3 * W_A + 2 * W_B), D_MODEL ** -0.5),
        'sc_conv': nrm((N_AB, SC_WIDTH, W_A), SC_WIDTH ** -0.5),
        'lru_conv': nrm((N_AB, LRU_CONV, W_B), LRU_CONV ** -0.5),
        'lru_conv_b': nrm((N_AB, W_B), 0.01),
        'lru_wa': nrm((N_AB, 2, LRU_HEADS, LRU_BLK, LRU_BLK), LRU_BLK ** -0.5),
        'lru_ba': nrm((N_AB, 2, W_B), 0.01),
        'lru_wi': nrm((N_AB, 2, LRU_HEADS, LRU_BLK, LRU_BLK), LRU_BLK ** -0.5),
        'lru_bi': nrm((N_AB, 2, W_B), 0.01),
        'lru_lambda': jnp.log(lam_a) - jnp.log1p(-lam_a),
        'cd_w_in': nrm((N_CD, D_MODEL, 3 * W_C + 4 * W_D), D_MODEL ** -0.5),
        'hy_conv': nrm((N_CD, HY_CONV, 3 * W_C), HY_CONV ** -0.5),
        'hy_w1': nrm((N_CD, HY_POS_DIM, HY_FFN), HY_POS_DIM ** -0.5),
        'hy_b1': nrm((N_CD, HY_FFN), 0.01),
        'hy_w2': nrm((N_CD, HY_FFN, HY_FFN), HY_FFN ** -0.5),
        'hy_b2': nrm((N_CD, HY_FFN), 0.01),
        'hy_w3': nrm((N_CD, HY_FFN, HY_ORDER * 2 * W_C), HY_FFN ** -0.5),
        'hy_freq': 1.0 + nrm((N_CD, HY_FFN), 0.1),
        'hy_bias': nrm((N_CD, HY_ORDER, W_C), 0.1),
        'rw_mu': uni((N_CD, 4, W_D), 0.0, 1.0),
        'rw_mu_x': uni((N_CD, 2, D_MODEL), 0.0, 1.0),
        'rw_w0': uni((N_CD, 2, W_D), -6.0, -1.0),
        'rw_w1': nrm((N_CD, 2, D_MODEL, RW_LORA_W), D_MODEL ** -0.5),
        'rw_w2': nrm((N_CD, 2, RW_LORA_W, W_D), 0.1 * RW_LORA_W ** -0.5),
        'rw_a0': nrm((N_CD, 2, W_D), 0.1),
        'rw_a1': nrm((N_CD, 2, D_MODEL, RW_LORA_A), D_MODEL ** -0.5),
        'rw_a2': nrm((N_CD, 2, RW_LORA_A, W_D), 0.1 * RW_LORA_A ** -0.5),
        'rw_kk': 0.85 + nrm((N_CD, W_D), 0.05),
        'rw_ka': 1.0 + nrm((N_CD, W_D), 0.05),
        'rw_rk': nrm((N_CD, RW_HEADS, RW_HEAD), 0.1),
        'rw_gn_g': 1.0 + nrm((N_CD, W_D), 0.02),
        'rw_gn_b': nrm((N_CD, W_D), 0.01),
    }


def reference(x_prompt, x_sample, state_lru, state_rwkv, c, c_ctx, w_mod, b_mod, ln1_g, ln1_b, ln2_g, ln2_b,
              mlp_w1, mlp_w2, w_out, ab_w_in, sc_conv, lru_conv, lru_conv_b, lru_wa, lru_ba, lru_wi, lru_bi,
              lru_lambda, cd_w_in, hy_conv, hy_w1, hy_b1, hy_w2, hy_b2, hy_w3, hy_freq, hy_bias, rw_mu, rw_mu_x,
              rw_w0, rw_w1, rw_w2, rw_a0, rw_a1, rw_a2, rw_kk, rw_ka, rw_rk, rw_gn_g, rw_gn_b):
    p = dict(w_mod=w_mod, b_mod=b_mod, ln1_g=ln1_g, ln1_b=ln1_b, ln2_g=ln2_g, ln2_b=ln2_b,
             mlp_w1=mlp_w1, mlp_w2=mlp_w2, w_out=w_out, ab_w_in=ab_w_in, sc_conv=sc_conv,
             lru_conv=lru_conv, lru_conv_b=lru_conv_b, lru_wa=lru_wa, lru_ba=lru_ba, lru_wi=lru_wi,
             lru_bi=lru_bi, lru_lambda=lru_lambda, cd_w_in=cd_w_in, hy_conv=hy_conv, hy_w1=hy_w1,
             hy_b1=hy_b1, hy_w2=hy_w2, hy_b2=hy_b2, hy_w3=hy_w3, hy_freq=hy_freq, hy_bias=hy_bias,
             rw_mu=rw_mu, rw_mu_x=rw_mu_x, rw_w0=rw_w0, rw_w1=rw_w1, rw_w2=rw_w2, rw_a0=rw_a0,
             rw_a1=rw_a1, rw_a2=rw_a2, rw_kk=rw_kk, rw_ka=rw_ka, rw_rk=rw_rk, rw_gn_g=rw_gn_g,
             rw_gn_b=rw_gn_b)
    y_prompt, new_state_lru, new_state_rwkv = _trunk(x_prompt, c_ctx[None, :], None, None, None, p)
    rows = x_sample.shape[1] // GRID_W
    y_sample, _, _ = _trunk(x_sample, c, state_lru, state_rwkv, rows, p)
    return (y_prompt, y_sample, new_state_lru, new_state_rwkv)
```

```python
import numpy as np
import ml_dtypes
import concourse.bass as bass
import concourse.mybir as mybir
from concourse.bass_utils import run_bass_kernel_spmd

F32 = mybir.dt.float32
BF16 = mybir.dt.bfloat16
AF = mybir.ActivationFunctionType
ALU = mybir.AluOpType

D = 1024
KC = 8
NCORES = 8
TT = 512
ALPHA = 8.0 ** 0.25
LN_EPS = 1e-5


class _Rec:
    def __init__(self):
        self.calls = []

    def __getattr__(self, name):
        def f(*args, **kwargs):
            self.calls.append((name, args, kwargs))
            return self
        return f


class Prog:
    ENGS = ("pe", "act", "dve", "pool", "sp")

    def __init__(self, nc):
        self.nc = nc
        self.stream = {e: [] for e in self.ENGS}
        self.sem = {}
        self.cnt = {}
        self.known = {e: {} for e in self.ENGS}
        self.last_w = {}
        self.readers = {}
        self.dma_ring = {}
        self.ndma_sems = 8
        self.ninstr = 0

    def setup(self, stack):
        for e in ("pe", "act", "dve", "pool"):
            self.sem["c_" + e] = stack.enter_context(self.nc.semaphore("c_" + e))
            self.cnt["c_" + e] = 0
        for q in ("sp", "pool", "act"):
            names = []
            for i in range(self.ndma_sems):
                n = f"d_{q}{i}"
                self.sem[n] = stack.enter_context(self.nc.semaphore(n))
                self.cnt[n] = 0
                names.append(n)
            self.dma_ring[q] = [names, 0]

    def _need(self, eng, tickets):
        best = {}
        for t in tickets:
            if t is None:
                continue
            s, v = t
            if eng == "pe" and s == "c_pe":
                continue
            if self.known[eng].get(s, 0) >= v:
                continue
            if best.get(s, 0) < v:
                best[s] = v
        for s, v in best.items():
            self.known[eng][s] = v
            self.stream[eng].append(("wait", (s, v)))

    GLOBAL_KEYS = {"rwc", "rwcb", "ident", "identb", "rwprm", "rmask", "onesb"}
    GLOBAL_PREF = {"ps", "rwS", "mixT", "nrwkv", "xT", "hyS", "tokS", "FS"}

    def _ns(self, keys):
        ns = getattr(self, "ns", None)
        if ns is None:
            return keys
        out = []
        for k in keys:
            if (isinstance(k, str) and k in self.GLOBAL_KEYS) or (isinstance(k, tuple) and k[0] in self.GLOBAL_PREF):
                out.append(k)
            else:
                out.append(("ns", ns, k))
        return out

    def _deps(self, reads, writes):
        ts = []
        for k in reads:
            ts.append(self.last_w.get(k))
        for k in writes:
            ts.append(self.last_w.get(k))
            ts.extend(self.readers.get(k, ()))
        return ts

    def _commit(self, ticket, reads, writes):
        for k in reads:
            self.readers.setdefault(k, []).append(ticket)
        for k in writes:
            self.last_w[k] = ticket
            self.readers[k] = []

    def op(self, eng, fns, reads=(), writes=(), serial=False):
        if callable(fns):
            fns = [fns]
        calls = []
        for f in fns:
            rec = _Rec()
            f(rec)
            assert len(rec.calls) == 1
            calls.append(rec.calls[0])
        fns = calls
        reads, writes = self._ns(reads), self._ns(writes)
        self._need(eng, self._deps(reads, writes))
        if serial and self.cnt["c_" + eng] > 0:
            sv = ("c_" + eng, self.cnt["c_" + eng])
            if self.known[eng].get(sv[0], 0) < sv[1]:
                self.known[eng][sv[0]] = sv[1]
                self.stream[eng].append(("wait", sv))
        s = "c_" + eng
        self.cnt[s] += 1
        ticket = (s, self.cnt[s])
        self.stream[eng].append(("ops", (fns, ticket)))
        self.ninstr += len(fns)
        self._commit(ticket, reads, writes)
        return ticket

    def dma(self, q, out, in_, reads=(), writes=(), **kw):
        names, idx = self.dma_ring[q]
        s = names[idx % len(names)]
        self.dma_ring[q][1] = idx + 1
        prev = (s, self.cnt[s]) if self.cnt[s] > 0 else None
        reads, writes = self._ns(reads), self._ns(writes)
        self._need(q, self._deps(reads, writes) + [prev])
        self.cnt[s] += 16
        ticket = (s, self.cnt[s])
        self.stream[q].append(("dma", (out, in_, kw, ticket)))
        self.ninstr += 1
        self._commit(ticket, reads, writes)
        return ticket

    def barrier(self):
        for e in self.ENGS:
            self._need(e, [(s, v) for s, v in self.cnt.items() if v > 0])
        self.last_w.clear()
        self.readers.clear()

    def finish(self):
        self._need("sp", [(s, v) for s, v in self.cnt.items() if v > 0])

    def emit(self, block):
        nc = self.nc
        sem = self.sem

        def run(eng_name):
            def body(e):
                for kind, pl in self.stream[eng_name]:
                    if kind == "wait":
                        e.wait_ge(sem[pl[0]], pl[1])
                    elif kind == "ops":
                        fns, (s, v) = pl
                        ins = None
                        for (name, args, kwargs) in fns:
                            ins = getattr(e, name)(*args, **kwargs)
                        ins.then_inc(sem[s], 1)
                    else:
                        out, in_, kw, (s, v) = pl
                        e.dma_start(out=out, in_=in_, **kw).then_inc(sem[s], 16)
            return body

        block.tensor(run("pe"))
        block.scalar(run("act"))
        block.vector(run("dve"))
        block.gpsimd(run("pool"))
        block.sync(run("sp"))


class Group:
    def __init__(self, name, tok0, ntok, nseq, L, line):
        self.name, self.tok0, self.ntok, self.nseq, self.L, self.line = name, tok0, ntok, nseq, L, line
        self.ntile = ntok // TT


class Builder:
    def __init__(self, depth=4, groups=("s", "p"), dbg=False, flags=()):
        self.flags = set(flags)
        self.depth = depth
        self.dbg = dbg
        self.nc = bass.Bass("TRN2", target_bir_lowering=False)
        self.groups = []
        if "s" in groups:
            self.groups.append(Group("s", 0, 4096, 1, 4096, 64))
        if "p" in groups:
            self.groups.append(Group("p", 4096, 1024, 4, 256, 256))
        self.NT = 5120
        self.uid = 0

    def din(self, name, shape, dt=F32):
        return self.nc.dram_tensor(name, list(shape), dt, kind="ExternalInput").ap()

    def dout(self, name, shape, dt=F32):
        return self.nc.dram_tensor(name, list(shape), dt, kind="ExternalOutput").ap()

    def dscr(self, name, shape, dt=F32):
        return self.nc.dram_tensor(name, list(shape), dt).ap()

    def sb(self, stack, name, shape, dt=F32):
        self.uid += 1
        return stack.enter_context(self.nc.sbuf_tensor(f"{name}_{self.uid}", list(shape), dt))

    def ring(self, stack, name, n, shape, dt=F32):
        return [self.sb(stack, f"{name}{i}", shape, dt) for i in range(n)]

    def dump(self, name, ap, reads, dt=F32):
        if not self.dbg:
            return
        shape = list(ap.shape)
        o = self.dout("dbg_" + name, shape, dt)
        self.P.dma("sp", o, ap, reads=reads, writes=[("dbg", name)])

    def next_ps(self):
        i = self.ps_i % (len(self.ps) - 1)
        self.ps_i += 1
        return self.ps[i], ("ps", i)

    def build(self):
        from contextlib import ExitStack
        nc = self.nc
        P = self.P = Prog(nc)
        I = self.I = {}
        O = self.O = {}
        I["xs"] = self.din("xs", [4096, D])
        I["xp"] = self.din("xp", [1024, D])
        I["st_lru"] = self.din("st_lru", [2, 2, 512])
        I["st_rwkv"] = self.din("st_rwkv", [2, 2, 8, 64, 64])
        I["cvecs"] = self.din("cvecs", [2, D])
        I["ident"] = self.din("ident", [128, 128])
        for nm, shp in WEIGHT_SHAPES.items():
            I[nm] = self.din(nm, shp)
        O["ys"] = self.dout("ys", [4096, D])
        O["yp"] = self.dout("yp", [1024, D])
        O["nlru"] = self.dout("nlru", [4, 2, 2, 512])
        O["nrwkv"] = self.dout("nrwkv", [4, 2, 2, 8, 64, 64])
        self.xT = self.dscr("xT_scr", [D, self.NT])
        self.mixT = self.dscr("mixT_scr", [D, self.NT], BF16)
        self.hyS = self.dscr("hyS_scr", [4, 512, self.NT])
        self.rwS = self.dscr("rwS_scr", [4, 11, 128, self.NT])
        self.tokS = self.dscr("tokS_scr", [128, self.NT // 128, 512], BF16)
        self.FS = self.dscr("FS_scr", [2, 34, 128, 2, 512])
        for nm, shp, dt in CONST_SPECS:
            I[nm] = self.din(nm, shp, dt)

        with ExitStack() as top:
            P.setup(top)
            self.ps = [top.enter_context(nc.psum_tensor(f"ps{i}", [128, 512], F32)) for i in range(8)]
            self.ps_i = 0
            self.ident = self.sb(top, "ident", [128, 128])
            P.dma("sp", self.ident[:], I["ident"][:, :], writes=["ident"])
            self.identb = self.sb(top, "identb", [128, 128], BF16)
            P.op("dve", lambda e: e.tensor_copy(out=self.identb[:], in_=self.ident[:]), reads=["ident"], writes=["identb"])
            self.onesb = self.sb(top, "onesb", [128, 128], BF16)
            P.op("dve", lambda e: e.memset(self.onesb[:], 1.0 / 1024.0), writes=["onesb"])
            self.modsT = self.sb(top, "modsT", [128, 4, 2, 48])
            self.lnp = self.sb(top, "lnp", [128, 4, 4, 8])
            for i, nm in enumerate(("ln1_g", "ln1_b", "ln2_g", "ln2_b")):
                for l in range(4):
                    P.dma("sp", self.lnp[:, l, i, :], I[nm][l].rearrange("(kc p) -> p kc", p=128),
                          writes=["lnp"], allow_slow_non_contiguous=True)

            self.epsT = self.sb(top, "epsT", [128, 1])
            P.op("dve", lambda e: e.memset(self.epsT[:], LN_EPS / (ALPHA * ALPHA)), writes=["epsT"])
            self.mods2 = self.sb(top, "mods2", [128, 4, 2, 32])
            self.phase_mods()
            self.phase_prepass()
            for l in range(self.depth):
                for g in self.groups:
                    if l % 2 == 0:
                        self.phase_ab(l, g)
                    else:
                        self.phase_cd(l, g)
                self.phase_dense(l, last=(l == self.depth - 1))
            P.finish()
            with nc.Block() as block:
                P.emit(block)
        return nc

    def phase_mods(self):
        from contextlib import ExitStack
        P, I = self.P, self.I
        with ExitStack() as st:
            cv = self.sb(st, "cv", [128, 2, 8])
            sl = self.sb(st, "sl", [128, 8, 2])
            bm = self.sb(st, "bm", [128, 4, 48])
            P.dma("sp", cv[:], I["cvecs"].rearrange("g (kc p) -> p g kc", p=128), writes=["cv"],
                  allow_slow_non_contiguous=True)
            P.dma("sp", bm[:], I["b_mod"].rearrange("l (m p) -> p l m", p=128), writes=["bm"],
                  allow_slow_non_contiguous=True)
            P.op("act", lambda e: e.activation(out=sl[:].rearrange("p kc g -> p g kc"), in_=cv[:], func=AF.Silu),
                 reads=["cv"], writes=["sl"])
            wbuf = self.ring(st, "wmod", 2, [128, 8, 1536])
            n = 0
            for l in range(self.depth):
                for s4 in range(4):
                    wb = wbuf[n % 2]
                    key = ("wmod", n % 2)
                    P.dma("sp", wb[:], I["w_mod"][l, :, s4 * 1536:(s4 + 1) * 1536].rearrange("(kc p) n -> p kc n", p=128),
                          writes=[key])
                    pt, pk = self.next_ps()
                    fns = []
                    for m in range(12):
                        for kc in range(8):
                            fns.append(lambda e, m=m, kc=kc, wb=wb, pt=pt: e.matmul(
                                pt[:, m * 2:m * 2 + 2], lhsT=wb[:, kc, m * 128:(m + 1) * 128], rhs=sl[:, kc, :],
                                start=(kc == 0), stop=(kc == 7)))
                    P.op("pe", fns, reads=[key, "sl"], writes=[pk])
                    for g in range(2):
                        P.op("dve", lambda e, g=g, l=l, s4=s4, pt=pt: e.tensor_tensor(
                            out=self.modsT[:, l, g, s4 * 12:(s4 + 1) * 12],
                            in0=pt[:, 0:24].rearrange("p (m g) -> p g m", g=2)[:, g, :],
                            in1=bm[:, l, s4 * 12:(s4 + 1) * 12], op=ALU.add),
                            reads=[pk, "bm"], writes=[("modsT", l, g, s4)])
                    n += 1
            allm = [("modsT", l, g, s4) for l in range(self.depth) for g in range(2) for s4 in range(4)]
            m2 = self.mods2
            dl = slice(0, self.depth)
            P.op("dve", lambda e: e.tensor_scalar_add(out=m2[:, dl, :, 0:8], in0=self.modsT[:, dl, :, 8:16], scalar1=1.0),
                 reads=allm, writes=["mods2a"])
            P.op("dve", lambda e: e.tensor_scalar_mul(out=m2[:, dl, :, 8:16], in0=self.modsT[:, dl, :, 16:24], scalar1=1.0 / ALPHA),
                 reads=allm, writes=["mods2b"])
            P.op("dve", lambda e: e.tensor_scalar_add(out=m2[:, dl, :, 16:24], in0=self.modsT[:, dl, :, 32:40], scalar1=1.0),
                 reads=allm, writes=["mods2c"])
            P.op("dve", lambda e: e.tensor_scalar_mul(out=m2[:, dl, :, 24:32], in0=self.modsT[:, dl, :, 40:48], scalar1=1.0 / ALPHA),
                 reads=allm, writes=["mods2d"])
            P.barrier()

    def xT_tile(self, gt0):
        return self.xT.rearrange("(kc p) t -> p kc t", p=128)[:, :, gt0:gt0 + TT]

    def mixT_tile(self, gt0):
        return self.mixT.rearrange("(kc p) t -> p kc t", p=128)[:, :, gt0:gt0 + TT]

    def phase_prepass(self):
        from contextlib import ExitStack
        P, I = self.P, self.I
        with ExitStack() as st:
            xin = self.ring(st, "xin", 2, [128, 4, D])
            xst = self.ring(st, "xst", 2, [128, 8, TT])
            n = 0
            for g in self.groups:
                src = I["xs"] if g.name == "s" else I["xp"]
                for t in range(g.ntile):
                    xi, xk = xin[n % 2], ("xin", n % 2)
                    xo, ok = xst[n % 2], ("xst", n % 2)
                    P.dma("sp", xi[:], src[t * TT:(t + 1) * TT, :].rearrange("(b p) d -> p b d", p=128), writes=[xk])
                    for kc in range(8):
                        pt, pk = self.next_ps()
                        P.op("pe", [lambda e, b=b, kc=kc, xi=xi, pt=pt: e.transpose(
                            pt[:, b * 128:(b + 1) * 128], xi[:, b, kc * 128:(kc + 1) * 128], self.ident[:]) for b in range(4)],
                            reads=[xk, "ident"], writes=[pk])
                        if kc % 2 == 0:
                            P.op("act", lambda e, kc=kc, xo=xo, pt=pt: e.copy(out=xo[:, kc, :], in_=pt[:]),
                                 reads=[pk], writes=[ok + (kc,)])
                        else:
                            P.op("dve", lambda e, kc=kc, xo=xo, pt=pt: e.tensor_copy(out=xo[:, kc, :], in_=pt[:]),
                                 reads=[pk], writes=[ok + (kc,)])
                    P.dma("sp", self.xT_tile(g.tok0 + t * TT), xo[:], reads=[ok + (kc,) for kc in range(8)],
                          writes=[("xT", g.tok0 + t * TT)])
                    n += 1
            P.barrier()

    def load_h(self, st, l, g, colmajor=False):
        P = self.P
        gi = 0 if g.name == "s" else 1
        from contextlib import ExitStack
        hT = self.sb(st, "hT", [128, 8, g.ntok], BF16)
        xst_ = ExitStack()
        xr = self.ring(xst_, "xr", 2, [128, 8, TT])
        for t in range(g.ntile):
            xt, xk = xr[t % 2], ("xr", t % 2)
            P.dma("sp", xt[:], self.xT_tile(g.tok0 + t * TT), reads=[("xT", g.tok0 + t * TT)], writes=[xk])
            for kc in range(8):
                if colmajor:
                    out = hT[:, kc, :].rearrange("p (c r) -> p r c", r=64)[:, t * 8:(t + 1) * 8, :]
                    in_ = xt[:, kc, :].rearrange("p (r c) -> p r c", c=64)
                else:
                    out = hT[:, kc, t * TT:(t + 1) * TT]
                    in_ = xt[:, kc, :]
                P.op("act", lambda e, out=out, in_=in_, kc=kc: e.activation(
                    out=out, in_=in_, func=AF.Identity,
                    scale=self.mods2[:, l, gi, kc:kc + 1], bias=self.modsT[:, l, gi, kc:kc + 1]),
                    reads=[xk], writes=[("hT", kc, t) if not colmajor else ("hT", kc, "all")])
        P.barrier()
        xst_.close()
        return hT

    def w_chunk(self, wt, key, src_cols):
        self.P.dma("pool", wt[:], src_cols.rearrange("(kc p) n -> p kc n", p=128), writes=[key])

    def mm_group(self, pt, pk, wt, wkey, rhs_fn, rkeys, nk=8):
        self.P.op("pe", [lambda e, kc=kc, rhs=rhs_fn(kc): e.matmul(pt, lhsT=wt[:, kc, :], rhs=rhs, start=(kc == 0), stop=(kc == nk - 1))
                         for kc in range(nk)], reads=[wkey] + rkeys, writes=[pk])

    def phase_ab(self, l, g):
        from contextlib import ExitStack
        P, I = self.P, self.I
        j = l // 2
        nl, ll = TT // g.line, g.line
        segs = [(s * g.L, (s + 1) * g.L) for s in range(TT // g.L)] if g.L < TT else None

        def lines(ap):
            return ap.rearrange("p (a b) -> p a b", b=ll)

        with ExitStack() as st:
            hT = self.load_h(st, l, g)
            hkeys = lambda t: [("hT", kc, t) for kc in range(8)]
            scw = self.sb(st, "scw", [128, 3, 4])
            lcw = self.sb(st, "lcw", [128, 4, 4])
            lcb = self.sb(st, "lcb", [128, 4])
            lba = self.sb(st, "lba", [128, 2, 4])
            lbi = self.sb(st, "lbi", [128, 2, 4])
            lam = self.sb(st, "lam", [128, 2, 4])
            h0 = self.sb(st, "h0", [128, 2, 4])
            for tl, nm, pat in ((scw, "sc_conv", "k (cc p) -> p k cc"), (lcw, "lru_conv", "k (cc p) -> p k cc"),
                                (lba, "lru_ba", "k (cc p) -> p k cc"), (lbi, "lru_bi", "k (cc p) -> p k cc"),
                                (lam, "lru_lambda", "k (cc p) -> p k cc")):
                P.dma("sp", tl[:], I[nm][j].rearrange(pat, p=128), writes=[nm], allow_slow_non_contiguous=True)
            P.dma("sp", lcb[:], I["lru_conv_b"][j].rearrange("(cc p) -> p cc", p=128), writes=["lcb"], allow_slow_non_contiguous=True)
            if g.name == "s":
                P.dma("sp", h0[:], I["st_lru"][j].rearrange("k (cc p) -> p k cc", p=128), writes=["h0"], allow_slow_non_contiguous=True)
            clam = self.sb(st, "clam", [128, 2, 4])
            P.op("act", lambda e: e.activation(out=clam[:], in_=lam[:], func=AF.Exp, scale=-1.0), reads=["lru_lambda"], writes=["clam"])
            P.op("act", lambda e: e.activation(out=clam[:], in_=clam[:], func=AF.Ln, bias=1.0), reads=["clam"], writes=["clam"])
            P.op("dve", lambda e: e.tensor_scalar_mul(out=clam[:], in0=clam[:], scalar1=-8.0), reads=["clam"], writes=["clam"])
            wg = self.sb(st, "wg", [128, 2, 2, 4, 128], BF16)
            P.op("dve", lambda e: e.memset(wg[:], 0.0), writes=["wg"])
            for gate, nm in enumerate(("lru_wa", "lru_wi")):
                for d in range(2):
                    for par in range(2):
                        P.dma("pool", wg[par * 64:(par + 1) * 64, d, gate, :, par * 64:(par + 1) * 64],
                              I[nm][j, d].rearrange("(bc par) i o -> par i bc o", par=2)[par],
                              reads=[], writes=["wg"])
            wr = self.ring(st, "wab", 6, [128, 8, 128], BF16)
            self.wn = getattr(self, "wn", 0)

            def getw(col0):
                wt, wk = wr[self.wn % 6], ("wab", self.wn % 6)
                self.wn += 1
                self.w_chunk(wt, wk, I["ab_w_in"][j, :, col0:col0 + 128])
                return wt, wk

            t32 = self.ring(st, "t32", 6, [128, TT])
            self.tn = 0

            def tmp():
                tl, tk = t32[self.tn % 6], ("t32", self.tn % 6)
                self.tn += 1
                return tl, tk
            mst = self.ring(st, "mst", 2, [128, TT], BF16)
            self.mn = 0

            for cc in range(4):
                wb_, wc_, wv_ = getw(cc * 128), getw(512 + cc * 128), getw(1024 + cc * 128)
                for t in range(g.ntile):
                    hs = lambda kc, t=t: hT[:, kc, t * TT:(t + 1) * TT]
                    pb, pbk = self.next_ps(); pc, pck = self.next_ps(); pv, pvk = self.next_ps()
                    self.mm_group(pb[:], pbk, wb_[0], wb_[1], hs, hkeys(t))
                    self.mm_group(pc[:], pck, wc_[0], wc_[1], hs, hkeys(t))
                    self.mm_group(pv[:], pvk, wv_[0], wv_[1], hs, hkeys(t))
                    sc, sck = tmp(); pp, ppk = tmp(); ac, ack = tmp()
                    P.op("act", lambda e, sc=sc, pc=pc: e.copy(out=sc[:], in_=pc[:]), reads=[pck], writes=[sck])
                    P.op("dve", lambda e, pp=pp, pv=pv, sc=sc: e.tensor_tensor(out=pp[:], in0=pv[:], in1=sc[:], op=ALU.mult),
                         reads=[pvk, sck], writes=[ppk])
                    P.op("dve", lambda e, ac=ac, pp=pp, cc=cc: e.tensor_scalar_mul(out=ac[:], in0=pp[:], scalar1=scw[:, 1, cc:cc + 1]),
                         reads=[ppk, "sc_conv"], writes=[ack])
                    P.op("dve", lambda e, ac=ac, pp=pp, cc=cc: e.scalar_tensor_tensor(
                        out=lines(ac[:])[:, :, 1:], in0=lines(pp[:])[:, :, :ll - 1], scalar=scw[:, 0, cc:cc + 1],
                        in1=lines(ac[:])[:, :, 1:], op0=ALU.mult, op1=ALU.add), reads=[ppk, ack], writes=[ack])
                    P.op("dve", lambda e, ac=ac, pp=pp, cc=cc: e.scalar_tensor_tensor(
                        out=lines(ac[:])[:, :, :ll - 1], in0=lines(pp[:])[:, :, 1:], scalar=scw[:, 2, cc:cc + 1],
                        in1=lines(ac[:])[:, :, :ll - 1], op0=ALU.mult, op1=ALU.add), reads=[ppk, ack], writes=[ack])
                    ms, msk = mst[self.mn % 2], ("mst", self.mn % 2); self.mn += 1
                    P.op("dve", lambda e, ms=ms, pb=pb, ac=ac: e.tensor_tensor(out=ms[:], in0=pb[:], in1=ac[:], op=ALU.mult),
                         reads=[pbk, ack], writes=[msk])
                    gt0 = g.tok0 + t * TT
                    P.dma("sp", self.mixT[cc * 128:(cc + 1) * 128, gt0:gt0 + TT], ms[:], reads=[msk], writes=[("mixT", cc, gt0)])

            XC = self.sb(st, "XC", [128, g.ntok])
            XCB = self.sb(st, "XCB", [128, g.ntok], BF16)
            GG = self.sb(st, "GG", [128, g.ntok], BF16)
            HF = self.sb(st, "HF", [128, g.ntok])
            hbr = self.ring(st, "hb", 2, [128, TT])
            for bc in range(4):
                wg_, wx_ = getw(1536 + bc * 128), getw(2048 + bc * 128)

                def gates(d, t, bc=bc):
                    sl_ = slice(t * TT, (t + 1) * TT)
                    pr, prk = self.next_ps(); pi, pik = self.next_ps()
                    P.op("pe", lambda e: e.matmul(pr[:], lhsT=wg[:, d, 0, bc, :], rhs=XCB[:, sl_], start=True, stop=True),
                         reads=["wg", ("XCB", t)], writes=[prk])
                    P.op("pe", lambda e: e.matmul(pi[:], lhsT=wg[:, d, 1, bc, :], rhs=XCB[:, sl_], start=True, stop=True),
                         reads=["wg", ("XCB", t)], writes=[pik])
                    r_, rk = tmp(); a_, ak = tmp(); i_, ik = tmp(); s_, sk = tmp(); u_, uk = tmp()
                    P.op("act", lambda e: e.activation(out=r_[:], in_=pr[:], func=AF.Sigmoid, bias=lba[:, d, bc:bc + 1]),
                         reads=[prk, "lru_ba"], writes=[rk])
                    P.op("act", lambda e: e.activation(out=i_[:], in_=pi[:], func=AF.Sigmoid, bias=lbi[:, d, bc:bc + 1]),
                         reads=[pik, "lru_bi"], writes=[ik])
                    P.op("act", lambda e: e.activation(out=a_[:], in_=r_[:], func=AF.Exp, scale=clam[:, d, bc:bc + 1]),
                         reads=[rk, "clam"], writes=[ak])
                    P.op("act", lambda e: e.activation(out=s_[:], in_=a_[:], func=AF.Square), reads=[ak], writes=[sk])
                    P.op("act", lambda e: e.activation(out=s_[:], in_=s_[:], func=AF.Sqrt, scale=-1.0, bias=1.0), reads=[sk], writes=[sk])
                    P.op("dve", lambda e: e.tensor_tensor(out=u_[:], in0=i_[:], in1=XC[:, sl_], op=ALU.mult),
                         reads=[ik, ("XC", t)], writes=[uk])
                    P.op("dve", lambda e: e.tensor_tensor(out=u_[:], in0=u_[:], in1=s_[:], op=ALU.mult), reads=[uk, sk], writes=[uk])
                    return a_, ak, u_, uk

                for t in range(g.ntile):
                    sl_ = slice(t * TT, (t + 1) * TT)
                    hs = lambda kc, t=t: hT[:, kc, t * TT:(t + 1) * TT]
                    pg, pgk = self.next_ps(); px, pxk = self.next_ps()
                    self.mm_group(pg[:], pgk, wg_[0], wg_[1], hs, hkeys(t))
                    self.mm_group(px[:], pxk, wx_[0], wx_[1], hs, hkeys(t))
                    P.op("act", lambda e, pg=pg, sl_=sl_: e.activation(out=GG[:, sl_], in_=pg[:], func=AF.Gelu), reads=[pgk], writes=[("GG", t)])
                    xl, xlk = tmp()
                    P.op("act", lambda e, px=px, xl=xl: e.copy(out=xl[:], in_=px[:]), reads=[pxk], writes=[xlk])
                    xck = ("XC", t)
                    P.op("dve", lambda e, xl=xl, sl_=sl_, bc=bc: e.tensor_scalar(
                        out=XC[:, sl_], in0=xl[:], scalar1=lcw[:, 2, bc:bc + 1], scalar2=lcb[:, bc:bc + 1], op0=ALU.mult, op1=ALU.add),
                        reads=[xlk, "lru_conv", "lcb"], writes=[xck])
                    for tap, off in ((0, -2), (1, -1), (3, 1)):
                        if off < 0:
                            o_ = lines(XC[:, sl_])[:, :, -off:]; i_ = lines(xl[:])[:, :, :ll + off]
                        else:
                            o_ = lines(XC[:, sl_])[:, :, :ll - off]; i_ = lines(xl[:])[:, :, off:]
                        P.op("dve", lambda e, o_=o_, i_=i_, tap=tap, bc=bc: e.scalar_tensor_tensor(
                            out=o_, in0=i_, scalar=lcw[:, tap, bc:bc + 1], in1=o_, op0=ALU.mult, op1=ALU.add),
                            reads=[xlk, xck], writes=[xck])
                    P.op("act", lambda e, sl_=sl_: e.copy(out=XCB[:, sl_], in_=XC[:, sl_]), reads=[xck], writes=[("XCB", t)])
                    a_, ak, u_, uk = gates(0, t)
                    for (s0, s1) in (segs or [(0, TT)]):
                        if g.name == "s":
                            init = h0[:, 0, bc:bc + 1] if t == 0 else HF[:, t * TT - 1:t * TT]
                            rk_ = ["h0"] if t == 0 else [("HF", t - 1)]
                        else:
                            init, rk_ = 0.0, []
                        P.op("dve", lambda e, s0=s0, s1=s1, init=init, a_=a_, u_=u_, t=t: e.tensor_tensor_scan(
                            out=HF[:, t * TT + s0:t * TT + s1], data0=a_[:, s0:s1], data1=u_[:, s0:s1], initial=init,
                            op0=ALU.mult, op1=ALU.add), reads=[ak, uk] + rk_, writes=[("HF", t)])
                prev_hb = None
                for t in reversed(range(g.ntile)):
                    sl_ = slice(t * TT, (t + 1) * TT)
                    a_, ak, u_, uk = gates(1, t)
                    hb, hbk = hbr[t % 2], ("hb", t % 2)
                    for (s0, s1) in (segs or [(0, TT)]):
                        if g.name == "s":
                            init = h0[:, 1, bc:bc + 1] if t == g.ntile - 1 else prev_hb[0][:, 0:1]
                            rk_ = ["h0"] if t == g.ntile - 1 else [prev_hb[1]]
                        else:
                            init, rk_ = 0.0, []
                        P.op("dve", lambda e, s0=s0, s1=s1, init=init, a_=a_, u_=u_, hb=hb: e.tensor_tensor_scan(
                            out=hb[:, s0:s1][:, ::-1], data0=a_[:, s0:s1][:, ::-1], data1=u_[:, s0:s1][:, ::-1], initial=init,
                            op0=ALU.mult, op1=ALU.add), reads=[ak, uk] + rk_, writes=[hbk])
                    prev_hb = (hb, hbk)
                    if g.name == "p":
                        for si, (s0, s1) in enumerate(segs):
                            b = t * (TT // g.L) + si
                            P.dma("sp", self.O["nlru"][b, j, 0, bc * 128:(bc + 1) * 128].rearrange("(p o) -> p o", o=1),
                                  HF[:, t * TT + s1 - 1:t * TT + s1], reads=[("HF", t)], writes=[("nlru", b, j, 0, bc)])
                            P.dma("sp", self.O["nlru"][b, j, 1, bc * 128:(bc + 1) * 128].rearrange("(p o) -> p o", o=1),
                                  hb[:, s0:s0 + 1], reads=[hbk], writes=[("nlru", b, j, 1, bc)])
                    y_, yk = tmp()
                    P.op("dve", lambda e, y_=y_, hb=hb, sl_=sl_: e.tensor_tensor(out=y_[:], in0=HF[:, sl_], in1=hb[:], op=ALU.add),
                         reads=[("HF", t), hbk], writes=[yk])
                    ms, msk = mst[self.mn % 2], ("mst", self.mn % 2); self.mn += 1
                    P.op("dve", lambda e, ms=ms, y_=y_, sl_=sl_: e.tensor_tensor(out=ms[:], in0=y_[:], in1=GG[:, sl_], op=ALU.mult),
                         reads=[yk, ("GG", t)], writes=[msk])
                    gt0 = g.tok0 + t * TT
                    P.dma("sp", self.mixT[512 + bc * 128:512 + (bc + 1) * 128, gt0:gt0 + TT], ms[:], reads=[msk],
                          writes=[("mixT", 4 + bc, gt0)])
            P.barrier()


    def ln_tile(self, xt, xk, l, which, scr):
        P = self.P
        vb, sq, mean_sb, m2, rstd = scr
        keys = [xk + (kc,) for kc in range(8)]
        P.op("dve", lambda e: e.tensor_copy(out=vb[:], in_=xt[:]), reads=keys, writes=["ln_vb"])
        P.op("act", lambda e: e.activation(out=sq[:], in_=xt[:], func=AF.Square), reads=keys, writes=["ln_sq"])
        pm, pmk = self.next_ps(); pe2, pe2k = self.next_ps()
        P.op("pe", [lambda e, kc=kc: e.matmul(pm[:], lhsT=self.onesb[:], rhs=vb[:, kc, :], start=(kc == 0), stop=(kc == 7)) for kc in range(8)],
             reads=["onesb", "ln_vb"], writes=[pmk])
        P.op("pe", [lambda e, kc=kc: e.matmul(pe2[:], lhsT=self.onesb[:], rhs=sq[:, kc, :], start=(kc == 0), stop=(kc == 7)) for kc in range(8)],
             reads=["onesb", "ln_sq"], writes=[pe2k])
        P.op("act", lambda e: e.copy(out=mean_sb[:], in_=pm[:]), reads=[pmk], writes=["ln_mean"])
        P.op("dve", lambda e: e.tensor_tensor(out=m2[:], in0=mean_sb[:], in1=mean_sb[:], op=ALU.mult), reads=["ln_mean"], writes=["ln_m2"])
        P.op("dve", lambda e: e.tensor_tensor(out=m2[:], in0=pe2[:], in1=m2[:], op=ALU.subtract), reads=[pe2k, "ln_m2"], writes=["ln_m2"])
        P.op("act", lambda e: e.activation(out=rstd[:], in_=m2[:], func=AF.Sqrt, bias=self.epsT[:, 0:1]), reads=["ln_m2", "epsT"], writes=["ln_rstd"])
        P.op("dve", lambda e: e.reciprocal(out=rstd[:], in_=rstd[:]), reads=["ln_rstd"], writes=["ln_rstd"])
        bc_ = lambda ap: ap[:].rearrange("p (o t) -> p o t", o=1).to_broadcast([128, 8, TT])
        P.op("dve", lambda e: e.tensor_tensor(out=xt[:], in0=xt[:], in1=bc_(mean_sb), op=ALU.subtract), reads=keys + ["ln_mean"], writes=keys)
        P.op("dve", lambda e: e.tensor_tensor(out=xt[:], in0=xt[:], in1=bc_(rstd), op=ALU.mult), reads=keys + ["ln_rstd"], writes=keys)
        gi, bi = (0, 1) if which == 1 else (2, 3)
        for kc in range(8):
            P.op("dve", lambda e, kc=kc: e.tensor_scalar(out=xt[:, kc, :], in0=xt[:, kc, :], scalar1=self.lnp[:, l, gi, kc:kc + 1],
                                                        scalar2=self.lnp[:, l, bi, kc:kc + 1], op0=ALU.mult, op1=ALU.add),
                 reads=[keys[kc], "lnp"], writes=[keys[kc]])

    def phase_dense(self, l, last):
        from contextlib import ExitStack
        P, I, O = self.P, self.I, self.O
        with ExitStack() as st:
            xr = self.ring(st, "dx", 2, [128, 8, TT])
            mr = self.ring(st, "dm", 2, [128, 8, TT], BF16)
            h2 = self.sb(st, "h2", [128, 8, TT], BF16)
            hid = self.sb(st, "hid", [128, 32, TT], BF16)
            rt = self.ring(st, "rt", 3, [128, TT], BF16)
            scr = (self.sb(st, "ln_vb", [128, 8, TT], BF16), self.sb(st, "ln_sq", [128, 8, TT], BF16),
                   self.sb(st, "ln_mean", [128, TT]), self.sb(st, "ln_m2", [128, TT]), self.sb(st, "ln_rstd", [128, TT]))
            ws = self.ring(st, "wsm", 6, [128, 8, 128], BF16)
            w2r = self.ring(st, "w2r", 3, [128, 32, 128], BF16)
            ot = self.ring(st, "ot", 2, [128, D]) if last else None
            tiles = [(g, t) for g in self.groups for t in range(g.ntile)]
            cnt = {'wn': 0, 'w2n': 0, 'on': 0}

            def dense_tile(n, g, t):
                gi = 0 if g.name == "s" else 1
                gt0 = g.tok0 + t * TT
                xt, xk = xr[n % 2], ("dx", n % 2)
                mt, mk = mr[n % 2], ("dm", n % 2)
                xkeys = [xk + (kc,) for kc in range(8)]
                P.dma("sp", xt[:], self.xT_tile(gt0), reads=[("xT", gt0)], writes=xkeys)
                P.dma("sp", mt[:], self.mixT_tile(gt0), reads=[("mixT", c, gt0) for c in range(8)], writes=[mk])
                for m in range(8):
                    wt, wk = ws[cnt['wn'] % 6], ("wsm", cnt['wn'] % 6); cnt['wn'] += 1
                    self.w_chunk(wt, wk, I["w_out"][l, :, m * 128:(m + 1) * 128])
                    pt, pk = self.next_ps()
                    self.mm_group(pt[:], pk, wt, wk, lambda kc: mt[:, kc, :], [mk])
                    P.op("dve", lambda e, m=m, pt=pt: e.scalar_tensor_tensor(
                        out=xt[:, m, :], in0=pt[:], scalar=self.mods2[:, l, gi, 8 + m:9 + m], in1=xt[:, m, :],
                        op0=ALU.mult, op1=ALU.add), reads=[pk, xkeys[m], "mods2b"], writes=[xkeys[m]])
                if n == 0 and l == 0:
                    self.dump('mods2', self.mods2[:], ['mods2a', 'mods2b', 'mods2c', 'mods2d'])
                    self.dump('modsT', self.modsT[:], [])
                    self.dump('v1', xt[:], xkeys)
                    self.dump('mt', mt[:], [mk], BF16)
                if n == 0 and l > 0:
                    self.dump('mt_l%d' % l, mt[:], [mk], BF16)
                self.ln_tile(xt, xk, l, 1, scr)
                if n == 0 and l == 0:
                    self.dump('x1', xt[:], xkeys)
                for kc in range(8):
                    P.op("act", lambda e, kc=kc: e.activation(out=h2[:, kc, :], in_=xt[:, kc, :], func=AF.Identity,
                                                              scale=self.mods2[:, l, gi, 16 + kc:17 + kc],
                                                              bias=self.modsT[:, l, gi, 24 + kc:25 + kc]),
                         reads=[xkeys[kc]], writes=[("h2", kc)])
                h2k = [("h2", kc) for kc in range(8)]
                for hc in range(32):
                    wt, wk = ws[cnt['wn'] % 6], ("wsm", cnt['wn'] % 6); cnt['wn'] += 1
                    self.w_chunk(wt, wk, I["mlp_w1"][l, :, hc * 128:(hc + 1) * 128])
                    pt, pk = self.next_ps()
                    self.mm_group(pt[:], pk, wt, wk, lambda kc: h2[:, kc, :], h2k)
                    r_, rk = rt[hc % 3], ("rt", hc % 3)
                    P.op("act", lambda e, r_=r_, pt=pt: e.activation(out=r_[:], in_=pt[:], func=AF.Relu), reads=[pk], writes=[rk])
                    P.op("dve", lambda e, r_=r_, hc=hc, pt=pt: e.tensor_tensor(out=hid[:, hc, :], in0=pt[:], in1=r_[:], op=ALU.mult),
                         reads=[rk, pk], writes=[("hid", hc)])
                hidk = [("hid", hc) for hc in range(32)]
                for m in range(8):
                    wt, wk = w2r[cnt['w2n'] % 3], ("w2r", cnt['w2n'] % 3); cnt['w2n'] += 1
                    P.dma("pool", wt[:], I["mlp_w2"][l, :, m * 128:(m + 1) * 128].rearrange("(kc p) n -> p kc n", p=128), writes=[wk])
                    pt, pk = self.next_ps()
                    self.mm_group(pt[:], pk, wt, wk, lambda kc: hid[:, kc, :], hidk, nk=32)
                    P.op("dve", lambda e, m=m, pt=pt: e.scalar_tensor_tensor(
                        out=xt[:, m, :], in0=pt[:], scalar=self.mods2[:, l, gi, 24 + m:25 + m], in1=xt[:, m, :],
                        op0=ALU.mult, op1=ALU.add), reads=[pk, xkeys[m], "mods2d"], writes=[xkeys[m]])
                if n == 0 and l == 0:
                    self.dump('v2', xt[:], xkeys)
                    self.dump('hid', hid[:], hidk, BF16)
                self.ln_tile(xt, xk, l, 2, scr)
                if n == 0 and l == 0:
                    self.dump('x2', xt[:], xkeys)
                if not last:
                    P.dma("sp", self.xT_tile(gt0), xt[:], reads=xkeys, writes=[("xT", gt0)])
                else:
                    dst = O["ys"] if g.name == "s" else O["yp"]
                    for b in range(4):
                        o_, ok = ot[cnt['on'] % 2], ("ot", cnt['on'] % 2); cnt['on'] += 1
                        for half in range(2):
                            pt, pk = self.next_ps()
                            P.op("pe", [lambda e, q=q, pt=pt, b=b, half=half: e.transpose(
                                pt[:, q * 128:(q + 1) * 128], xt[:, half * 4 + q, b * 128:(b + 1) * 128], self.ident[:]) for q in range(4)],
                                reads=xkeys[half * 4:half * 4 + 4] + ["ident"], writes=[pk])
                            if half == 0:
                                P.op("act", lambda e, o_=o_, pt=pt: e.copy(out=o_[:, 0:512], in_=pt[:]), reads=[pk], writes=[ok + (0,)])
                            else:
                                P.op("dve", lambda e, o_=o_, pt=pt: e.tensor_copy(out=o_[:, 512:1024], in_=pt[:]), reads=[pk], writes=[ok + (1,)])
                        r0 = t * TT + b * 128
                        P.dma("sp", dst[r0:r0 + 128, :], o_[:], reads=[ok + (0,), ok + (1,)], writes=[("out", g.name, r0)])

            for n, (g, t) in enumerate(tiles):
                dense_tile(n, g, t)
            P.barrier()

    def phase_cd(self, l, g):
        from contextlib import ExitStack
        P, I = self.P, self.I
        j = l // 2
        with ExitStack() as st:
            nblk = g.L // 128
            with ExitStack() as st2:
                rw = self.rwkv_setup(st2, l, g)
                hT = self.load_h(st2, l, g, colmajor=(g.name == "s"))
                with ExitStack() as st3:
                    self.cd_hyena_proj(st3, l, g, hT)
                    P.barrier()
                if "norwkv" not in self.flags:
                    self.rwkv_partA(st2, l, g, hT, rw)
                P.barrier()
                if l == 1 and g.name == "p" and self.dbg:
                    for hp_ in range(4):
                        self.dump("rwS%d" % hp_, self.rwS[hp_, :, :, 4096:5120], [])
            if "norwkv" not in self.flags:
                with ExitStack() as st4:
                    rw = self.rwkv_setup(st4, l, g)
                    self.rwkv_partB(st4, l, g, rw)
                    P.barrier()
            if "nohyconv" not in self.flags:
                with ExitStack() as st5:
                    self.cd_hyena_conv(st5, l, g)
                    P.barrier()

    def hkeys(self, g, t):
        if g.name == "s":
            return [("hT", kc, "all") for kc in range(8)]
        return [("hT", kc, t) for kc in range(8)]

    def to_tok(self, src_bf, skey, dst, dkeys_fn, nb):
        P = self.P
        pt, pk = self.next_ps()
        pv = pt[:].bitcast(BF16)
        P.op("pe", [lambda e, b=b: e.transpose(pv[:, b * 128:(b + 1) * 128], src_bf[:, b * 128:(b + 1) * 128], self.identb[:]) for b in range(nb)],
             reads=[skey, "identb"], writes=[pk])
        P.op("act", lambda e: e.copy(out=dst, in_=pv[:, 0:nb * 128].rearrange("p (b c) -> p b c", c=128)), reads=[pk], writes=dkeys_fn)

    def cd_hyena_proj(self, st, l, g, hT):
        P, I = self.P, self.I
        j = l // 2
        nl, ll = TT // g.line, g.line
        lines = lambda ap: ap.rearrange("p (a b) -> p a b", b=ll)
        hyw = self.sb(st, "hyw", [128, 3, 12])
        P.dma("sp", hyw[:], I["hy_conv"][j].rearrange("k (q p) -> p k q", p=128), writes=["hyw"], allow_slow_non_contiguous=True)
        wr = self.ring(st, "whp", 3, [128, 8, 128], BF16)
        xl_r = self.ring(st, "hxl", 2, [128, TT])
        u_r = self.ring(st, "hu", 2, [128, TT])
        ub_r = self.ring(st, "hub", 2, [128, TT], BF16)
        tokr = self.ring(st, "tokst", 2, [128, 4, 128], BF16)
        cnt = {"n": 0}
        nblk = g.L // 128

        def tile(q, t, wt, wk):
            which, cc = q // 4, q % 4
            n = cnt["n"]; cnt["n"] += 1
            pt, pk = self.next_ps()
            self.mm_group(pt[:], pk, wt, wk, lambda kc: hT[:, kc, t * TT:(t + 1) * TT], self.hkeys(g, t))
            xl, xlk = xl_r[n % 2], ("hxl", n % 2)
            u, uk = u_r[n % 2], ("hu", n % 2)
            P.op("act", lambda e: e.copy(out=xl[:], in_=pt[:]), reads=[pk], writes=[xlk])
            P.op("dve", lambda e: e.tensor_scalar_mul(out=u[:], in0=xl[:], scalar1=hyw[:, 1, q:q + 1]), reads=[xlk, "hyw"], writes=[uk])
            P.op("dve", lambda e: e.scalar_tensor_tensor(out=lines(u[:])[:, :, 1:], in0=lines(xl[:])[:, :, :ll - 1], scalar=hyw[:, 0, q:q + 1],
                                                         in1=lines(u[:])[:, :, 1:], op0=ALU.mult, op1=ALU.add), reads=[xlk, uk], writes=[uk])
            P.op("dve", lambda e: e.scalar_tensor_tensor(out=lines(u[:])[:, :, :ll - 1], in0=lines(xl[:])[:, :, 1:], scalar=hyw[:, 2, q:q + 1],
                                                         in1=lines(u[:])[:, :, :ll - 1], op0=ALU.mult, op1=ALU.add), reads=[xlk, uk], writes=[uk])
            gt0 = g.tok0 + t * TT
            P.dma("sp", self.hyS[which, cc * 128:(cc + 1) * 128, gt0:gt0 + TT], u[:], reads=[uk], writes=[("hyS", which, cc, gt0)])
            if which == 0:
                ub, ubk = ub_r[n % 2], ("hub", n % 2)
                P.op("act", lambda e: e.copy(out=ub[:], in_=u[:]), reads=[uk], writes=[ubk])
                tk_, tkk = tokr[n % 2], ("tokst", n % 2)
                self.to_tok(ub, ubk, tk_[:], [tkk], 4)
                gb0 = g.tok0 // 128 + t * 4
                P.dma("sp", self.tokS[:, gb0:gb0 + 4, cc * 128:(cc + 1) * 128], tk_[:], reads=[tkk], writes=[("tokS", gb0, cc)])

        for q in range(12):
            wt, wk = wr[q % 3], ("whp", q % 3)
            self.w_chunk(wt, wk, I["cd_w_in"][j, :, q * 128:(q + 1) * 128])
            for t in range(g.ntile):
                tile(q, t, wt, wk)

    def cd_hyena_conv(self, st, l, g):
        from contextlib import ExitStack
        P, I = self.P, self.I
        j = l // 2
        L, nseq = g.L, g.nseq
        nblk = L // 128
        N2 = 2 * L
        sfx = "4096" if L == 4096 else "256"
        TI = min(512, L)
        ntt = L // TI
        KG = min(8, nblk)
        HW = 512
        kb0 = 0 if g.name == "s" else 32
        hbias = self.sb(st, "hbias", [128, 2, 4])
        P.dma("sp", hbias[:], I["hy_bias"][j].rearrange("o (cc p) -> p o cc", p=128), writes=["hbias"], allow_slow_non_contiguous=True)
        eps6 = self.sb(st, "eps6", [128, 1])
        P.op("dve", lambda e: e.memset(eps6[:], 1e-6), writes=["eps6"])
        ones32 = self.sb(st, "ones32", [128, 128])
        P.op("dve", lambda e: e.memset(ones32[:], 1.0), writes=["ones32"])
        tcn = {"n": 0}

        def mk_table(tb):
            def table(src_ap, shape):
                n = tcn["n"]; tcn["n"] += 1
                tl, tk = tb[n % len(tb)], ("dft", n % len(tb))
                v = tl[:, 0:shape[0] * shape[1]].rearrange("p (a b) -> p a b", b=shape[1])
                P.dma("sp", v, src_ap, writes=[tk])
                return v, tk
            return table

        with ExitStack() as sF:
            GSD = self.sb(sF, "GSD", [128, 2 * nblk * HW], BF16)
            GS = GSD[:, 0:nblk * HW].rearrange("p (b c) -> p b c", c=HW)
            GD = GSD[:, nblk * HW:2 * nblk * HW].rearrange("p (b c) -> p b c", c=HW)
            rnb = self.sb(sF, "rnb", [128, HW])
            def gen_filters(o, half):
                c0 = half * HW
                with ExitStack() as s2:
                    fT = self.sb(s2, "fT", [33, L])
                    P.dma("sp", fT[:], I["featT" + sfx][:, :], writes=["fT"])
                    prm = self.sb(s2, "hyprm", [64, 6])
                    P.dma("sp", prm[:, 0:1], I["hy_freq"][j].rearrange("(p o) -> p o", o=1), writes=["hyprm0"])
                    P.dma("sp", prm[:, 1:2], I["hy_b1"][j].rearrange("(p o) -> p o", o=1), writes=["hyprm1"])
                    P.dma("sp", prm[:, 2:3], I["hy_b2"][j].rearrange("(p o) -> p o", o=1), writes=["hyprm2"])
                    P.op("dve", lambda e: e.tensor_tensor(out=prm[:, 3:5], in0=prm[:, 1:3], in1=prm[:, 0:1].to_broadcast([64, 2]), op=ALU.mult),
                         reads=["hyprm0", "hyprm1", "hyprm2"], writes=["hyprm3"])
                    w1 = self.sb(s2, "hw1", [33, 64]); w2 = self.sb(s2, "hw2", [64, 64]); w3 = self.sb(s2, "hw3", [64, 2048])
                    P.dma("sp", w1[:], I["hy_w1"][j], writes=["hw1"]); P.dma("sp", w2[:], I["hy_w2"][j], writes=["hw2"])
                    P.dma("sp", w3[:], I["hy_w3"][j], writes=["hw3"])
                    ntn = self.sb(s2, "ntn", [128, nblk]); dlt = self.sb(s2, "dlt", [128, 512]); m1 = self.sb(s2, "m1c", [128, 1])
                    P.dma("sp", ntn[:], I["ntn" + sfx][:, :], writes=["ntn"]); P.dma("sp", dlt[:], I["delta_b"][:, :], writes=["dlt"])
                    P.dma("sp", m1[:], I["m1col"][:, :], writes=["m1c"])
                    H1 = self.sb(s2, "H1", [64, L]); H2 = self.sb(s2, "H2", [64, L])
                    tmp = self.ring(s2, "ftmp", 4, [128, 512])
                    hw_ = slice(0, HW)
                    MAGIC = 12582912.0
                    TWO_PI = float(2.0 * np.pi)

                    def sin_layer(dst, wmat, kdim, src, bcol, nm):
                        for c0 in range(0, L, 512):
                            cw = min(512, L - c0)
                            pt, pk = self.next_ps()
                            P.op("pe", lambda e: e.matmul(pt[0:64, 0:cw], lhsT=wmat[:], rhs=src[0:kdim, c0:c0 + cw], start=True, stop=True),
                                 reads=[nm[0], nm[1]], writes=[pk])
                            a, b = tmp[0], tmp[1]
                            P.op("act", lambda e: e.activation(out=a[0:64, 0:cw], in_=pt[0:64, 0:cw], func=AF.Identity,
                                                               scale=prm[:, 0:1], bias=prm[:, bcol:bcol + 1]), reads=[pk, "hyprm3", "hyprm0"], writes=["ft0"])
                            P.op("dve", lambda e: e.tensor_scalar(out=b[0:64, 0:cw], in0=a[0:64, 0:cw], scalar1=1.0 / TWO_PI, scalar2=MAGIC,
                                                                  op0=ALU.mult, op1=ALU.add), reads=["ft0"], writes=["ft1"])
                            P.op("dve", lambda e: e.tensor_scalar_add(out=b[0:64, 0:cw], in0=b[0:64, 0:cw], scalar1=-MAGIC), reads=["ft1"], writes=["ft1"])
                            P.op("dve", lambda e: e.scalar_tensor_tensor(out=a[0:64, 0:cw], in0=b[0:64, 0:cw], scalar=-TWO_PI, in1=a[0:64, 0:cw],
                                                                         op0=ALU.mult, op1=ALU.add), reads=["ft0", "ft1"], writes=["ft0"])
                            P.op("act", lambda e: e.activation(out=dst[:, c0:c0 + cw], in_=a[0:64, 0:cw], func=AF.Sin), reads=["ft0"], writes=[nm[2]])

                    sin_layer(H1, w1, 33, fT, 3, ("hw1", "fT", "H1"))
                    sin_layer(H2, w2, 64, H1, 4, ("hw2", "H1", "H2"))
                    acc, acck = self.ps[7], ("ps", 7)
                    for blk in range(nblk):
                        dec, g0, g1, sq = tmp[0], tmp[1], tmp[2], tmp[3]
                        P.op("act", lambda e, blk=blk: e.activation(out=dec[:, hw_], in_=dlt[:, c0:c0 + HW], func=AF.Exp, scale=ntn[:, blk:blk + 1]),
                             reads=["dlt", "ntn"], writes=["ft0"])
                        for dr, gt, gk in ((0, g0, "ft1"), (1, g1, "ft2")):
                            pt, pk = self.next_ps()
                            ch = o * 2 + dr
                            P.op("pe", lambda e, blk=blk, ch=ch, pt=pt: e.matmul(pt[:, hw_], lhsT=H2[:, blk * 128:(blk + 1) * 128],
                                                                               rhs=w3[:, ch * 512 + c0:ch * 512 + c0 + HW], start=True, stop=True),
                                 reads=["H2", "hw3"], writes=[pk])
                            P.op("dve", lambda e, gt=gt, pt=pt: e.tensor_tensor(out=gt[:, hw_], in0=pt[:, hw_], in1=dec[:, hw_], op=ALU.mult),
                                 reads=[pk, "ft0"], writes=[gk])
                        if blk == 0:
                            P.op("dve", lambda e: e.tensor_scalar_mul(out=g1[:, hw_], in0=g1[:, hw_], scalar1=m1[:, 0:1]), reads=["ft2", "m1c"], writes=["ft2"])
                        P.op("dve", lambda e: e.tensor_tensor(out=sq[:, hw_], in0=g0[:, hw_], in1=g0[:, hw_], op=ALU.mult), reads=["ft1"], writes=["ft3"])
                        P.op("pe", lambda e, blk=blk: e.matmul(acc[:, hw_], lhsT=ones32[:], rhs=sq[:, hw_], start=(blk == 0), stop=False),
                             reads=["ft3", "ones32"], writes=[acck])
                        P.op("dve", lambda e: e.tensor_tensor(out=sq[:, hw_], in0=g1[:, hw_], in1=g1[:, hw_], op=ALU.mult), reads=["ft2"], writes=["ft3"])
                        P.op("pe", lambda e, blk=blk: e.matmul(acc[:, hw_], lhsT=ones32[:], rhs=sq[:, hw_], start=False, stop=(blk == nblk - 1)),
                             reads=["ft3", "ones32"], writes=[acck])
                        P.op("dve", lambda e, blk=blk: e.tensor_tensor(out=GS[:, blk, :], in0=g0[:, hw_], in1=g1[:, hw_], op=ALU.add),
                             reads=["ft1", "ft2"], writes=[("GS", blk)])
                        P.op("dve", lambda e, blk=blk: e.tensor_tensor(out=GD[:, blk, :], in0=g0[:, hw_], in1=g1[:, hw_], op=ALU.subtract),
                             reads=["ft1", "ft2"], writes=[("GD", blk)])
                    P.op("act", lambda e: e.activation(out=rnb[:], in_=acc[:, hw_], func=AF.Sqrt, bias=eps6[:, 0:1]), reads=[acck, "eps6"], writes=["rnb"])
                    P.op("dve", lambda e: e.reciprocal(out=rnb[:], in_=rnb[:]), reads=["rnb"], writes=["rnb"])
                    P.op("dve", lambda e: e.tensor_scalar_mul(out=rnb[:], in0=rnb[:], scalar1=2.0 / N2), reads=["rnb"], writes=["rnb"])
                    P.barrier()


            for o in range(2):
                gen_filters(o, 0)
                with ExitStack() as s2:
                    tb = self.ring(s2, "dftF", 4, [128, KG * 512], BF16)
                    table = mk_table(tb)
                    fo_r = self.ring(s2, "fo", 2, [128, 2, 512])
                    for kb in range(nblk):
                        C, ck = table(I["TC" + sfx][kb], (nblk, 128))
                        S, sk = table(I["TS" + sfx][kb], (nblk, 128))
                        fr, frk = self.next_ps(); fi, fik = self.next_ps()
                        P.op("pe", [lambda e, sb_=sb_: e.matmul(fr[:], lhsT=C[:, sb_, :], rhs=GS[:, sb_, :], start=(sb_ == 0), stop=(sb_ == nblk - 1))
                                    for sb_ in range(nblk)], reads=[ck] + [("GS", b) for b in range(nblk)], writes=[frk])
                        P.op("pe", [lambda e, sb_=sb_: e.matmul(fi[:], lhsT=S[:, sb_, :], rhs=GD[:, sb_, :], start=(sb_ == 0), stop=(sb_ == nblk - 1))
                                    for sb_ in range(nblk)], reads=[sk] + [("GD", b) for b in range(nblk)], writes=[fik])
                        fo, fok = fo_r[kb % 2], ("fo", kb % 2)
                        P.op("dve", lambda e: e.tensor_tensor(out=fo[:, 0, :], in0=fr[:], in1=rnb[:], op=ALU.mult), reads=[frk, "rnb"], writes=[fok + (0,)])
                        P.op("dve", lambda e: e.tensor_tensor(out=fo[:, 1, :], in0=fi[:], in1=rnb[:], op=ALU.mult), reads=[fik, "rnb"], writes=[fok + (1,)])
                        P.dma("sp", self.FS[o, kb0 + kb], fo[:], reads=[fok + (0,), fok + (1,)], writes=[("FS", o, kb0 + kb)])
                    P.barrier()
            P.barrier()

        src_tok = self.sb(st, "srctok", [128, nseq * nblk, 512], BF16)
        P.dma("sp", src_tok[:], self.tokS[:, g.tok0 // 128:g.tok0 // 128 + nseq * nblk, :],
              writes=[("tok", b, cc) for b in range(nseq * nblk) for cc in range(4)])
        msbox = {}

        def conv(o, post):
            with ExitStack() as s2:
                Yr = self.sb(s2, "Yr", [128, nseq * nblk, HW], BF16)
                Yi = self.sb(s2, "Yi", [128, nseq * nblk, HW], BF16)
                tb = self.ring(s2, "dft", 6 if o == 0 else 4, [128, KG * 512], BF16)
                table = mk_table(tb)
                fl_r = self.ring(s2, "fl", 2, [128, 2, 512])
                tmp = self.ring(s2, "ctmp", 4, [128, 512])
                for kb in range(nblk):
                    C, ck = table(I["TC" + sfx][kb], (nblk, 128))
                    S, sk = table(I["TS" + sfx][kb], (nblk, 128))
                    fl, flk = fl_r[kb % 2], ("fl", kb % 2)
                    P.dma("sp", fl[:], self.FS[o, kb0 + kb], reads=[("FS", o, kb0 + kb)], writes=[flk])
                    frs, fis = fl[:, 0, :], fl[:, 1, :]
                    for sq_ in range(nseq):
                        zr, zrk = self.next_ps(); zi, zik = self.next_ps()
                        tkeys = [("tok", sq_ * nblk + b, cc) for b in range(nblk) for cc in range(4)]
                        P.op("pe", [lambda e, sb_=sb_: e.matmul(zr[:], lhsT=C[:, sb_, :], rhs=src_tok[:, sq_ * nblk + sb_, :], start=(sb_ == 0),
                                                                stop=(sb_ == nblk - 1)) for sb_ in range(nblk)], reads=[ck] + tkeys, writes=[zrk])
                        P.op("pe", [lambda e, sb_=sb_: e.matmul(zi[:], lhsT=S[:, sb_, :], rhs=src_tok[:, sq_ * nblk + sb_, :], start=(sb_ == 0),
                                                                stop=(sb_ == nblk - 1)) for sb_ in range(nblk)], reads=[sk] + tkeys, writes=[zik])
                        t1, t2, t3, t4 = tmp[0], tmp[1], tmp[2], tmp[3]
                        yidx = sq_ * nblk + kb
                        P.op("dve", lambda e: e.tensor_tensor(out=t1[:], in0=zr[:], in1=frs, op=ALU.mult), reads=[zrk, flk], writes=["ct2"])
                        P.op("dve", lambda e: e.tensor_tensor(out=t2[:], in0=zi[:], in1=fis, op=ALU.mult), reads=[zik, flk], writes=["ct3"])
                        P.op("dve", lambda e: e.tensor_tensor(out=t3[:], in0=zr[:], in1=fis, op=ALU.mult), reads=[zrk, flk], writes=["ct4"])
                        P.op("dve", lambda e: e.tensor_tensor(out=t4[:], in0=zi[:], in1=frs, op=ALU.mult), reads=[zik, flk], writes=["ct5"])
                        P.op("pool", lambda e, yidx=yidx: e.tensor_tensor(out=Yr[:, yidx, :], in0=t1[:], in1=t2[:], op=ALU.subtract),
                             reads=["ct2", "ct3"], writes=[("Yr", yidx)])
                        P.op("pool", lambda e, yidx=yidx: e.tensor_tensor(out=Yi[:, yidx, :], in0=t3[:], in1=t4[:], op=ALU.add),
                             reads=["ct4", "ct5"], writes=[("Yi", yidx)])
                for sq_ in range(nseq):
                    for tt in range(ntt):
                        banks = [self.next_ps() for _ in range(4)]
                        ngrp = nblk // KG
                        for kg in range(ngrp):
                            Ci, cik = table(I["IC" + sfx][tt, :, kg * KG:(kg + 1) * KG, :], (KG, TI))
                            Si, sik = table(I["IS" + sfx][tt, :, kg * KG:(kg + 1) * KG, :], (KG, TI))
                            for cc in range(4):
                                bk, bkk = banks[cc]
                                fns = []
                                for kq in range(KG):
                                    kb = kg * KG + kq
                                    yidx = sq_ * nblk + kb
                                    fns.append(lambda e, kq=kq, yidx=yidx, bk=bk, cc=cc, kb=kb: e.matmul(
                                        bk[:, 0:TI], lhsT=Yr[:, yidx, cc * 128:(cc + 1) * 128], rhs=Ci[:, kq, :], start=(kb == 0), stop=False))
                                    fns.append(lambda e, kq=kq, yidx=yidx, bk=bk, cc=cc, kb=kb: e.matmul(
                                        bk[:, 0:TI], lhsT=Yi[:, yidx, cc * 128:(cc + 1) * 128], rhs=Si[:, kq, :], start=False, stop=(kb == nblk - 1)))
                                P.op("pe", fns, reads=[cik, sik] + [("Yr", sq_ * nblk + kg * KG + q) for q in range(KG)]
                                     + [("Yi", sq_ * nblk + kg * KG + q) for q in range(KG)], writes=[bkk])
                        for cc in range(4):
                            post(sq_, tt, cc, banks[cc][0], banks[cc][1])
                P.barrier()

        io = {"n": 0}
        iobuf = self.ring(st, "hio", 4, [128, 512])
        zb_r = self.ring(st, "zb", 2, [128, 512], BF16)

        def ld(which, cc, tok_lo, width):
            n = io["n"]; io["n"] += 1
            tl, tk = iobuf[n % 4], ("hio", n % 4)
            src = self.hyS[which, cc * 128:(cc + 1) * 128, tok_lo:tok_lo + width]
            P.dma("sp", tl[:, 0:width], src, reads=[("hyS", which, cc, (tok_lo // TT) * TT)], writes=[tk])
            return tl, tk

        def post1(sq_, tt, cc, bk, bkk):
            tok_lo = g.tok0 + sq_ * L + tt * TI
            hv, hvk = ld(0, cc, tok_lo, TI)
            hx, hxk = ld(1, cc, tok_lo, TI)
            P.op("dve", lambda e: e.scalar_tensor_tensor(out=hv[:, 0:TI], in0=hv[:, 0:TI], scalar=hbias[:, 0, cc:cc + 1], in1=bk[:, 0:TI],
                                                         op0=ALU.mult, op1=ALU.add), reads=[hvk, bkk, "hbias"], writes=[hvk])
            P.op("dve", lambda e: e.tensor_tensor(out=hv[:, 0:TI], in0=hv[:, 0:TI], in1=hx[:, 0:TI], op=ALU.mult), reads=[hvk, hxk], writes=[hvk])
            P.dma("sp", self.hyS[3, cc * 128:(cc + 1) * 128, tok_lo:tok_lo + TI], hv[:, 0:TI], reads=[hvk],
                  writes=[("hyS", 3, cc, (tok_lo // TT) * TT)])
            n = io["n"]
            zb, zbk = zb_r[n % 2], ("zb", n % 2)
            P.op("act", lambda e: e.copy(out=zb[:, 0:TI], in_=hv[:, 0:TI]), reads=[hvk], writes=[zbk])
            nb = TI // 128
            b0 = sq_ * nblk + tt * nb
            self.to_tok(zb, zbk, src_tok[:, b0:b0 + nb, cc * 128:(cc + 1) * 128], [("tok", b0 + b, cc) for b in range(nb)], nb)

        def post2(sq_, tt, cc, bk, bkk):
            tok_lo = g.tok0 + sq_ * L + tt * TI
            z, zk = ld(3, cc, tok_lo, TI)
            hx, hxk = ld(2, cc, tok_lo, TI)
            P.op("dve", lambda e: e.scalar_tensor_tensor(out=z[:, 0:TI], in0=z[:, 0:TI], scalar=hbias[:, 1, cc:cc + 1], in1=bk[:, 0:TI],
                                                         op0=ALU.mult, op1=ALU.add), reads=[zk, bkk, "hbias"], writes=[zk])
            if g.name == "s":
                out = msbox["MS"][:, cc, :].rearrange("p (r c) -> p c r", c=64)[:, tt * 8:(tt + 1) * 8, :]
                a0 = z[:, 0:TI].rearrange("p (c r) -> p c r", r=64)
                a1 = hx[:, 0:TI].rearrange("p (c r) -> p c r", r=64)
            else:
                lo = sq_ * L + tt * TI
                out = msbox["MS"][:, cc, lo:lo + TI]
                a0, a1 = z[:, 0:TI], hx[:, 0:TI]
            P.op("dve", lambda e: e.tensor_tensor(out=out, in0=a0, in1=a1, op=ALU.mult), reads=[zk, hxk], writes=[("MS", cc, sq_, tt)])

        conv(0, post1)
        MS = self.sb(st, "MSh", [128, 4, g.ntok], BF16)
        msbox["MS"] = MS
        conv(1, post2)
        for cc in range(4):
            P.dma("sp", self.mixT[cc * 128:(cc + 1) * 128, g.tok0:g.tok0 + g.ntok], MS[:, cc, :],
                  reads=[("MS", cc, s_, t_) for s_ in range(nseq) for t_ in range(ntt)],
                  writes=[("mixT", cc, g.tok0 + t * TT) for t in range(g.ntile)])

    def rwkv_setup(self, st, l, g):
        from contextlib import ExitStack
        P, I, O = self.P, self.I, self.O
        j = l // 2
        nl, ll = TT // g.line, g.line
        lines = lambda ap: ap.rearrange("p (a b) -> p a b", b=ll)
        L, nseq = g.L, g.nseq
        CH = 128
        rwS = self.rwS
        SL = {"r": 0, "v": 1, "kk": 2, "gs": 3, "kd0": 4, "kd1": 5, "b0": 6, "b1": 7, "lw0": 8, "lw1": 9, "bon": 10}
        cst = self.sb(st, "rwc", [128, 6, 128])
        P.dma("sp", cst[:], I["rw_masks"].rearrange("a p c -> p a c"), writes=["rwc"])
        cstb = self.sb(st, "rwcb", [128, 6, 128], BF16)
        P.op("dve", lambda e: e.tensor_copy(out=cstb[:], in_=cst[:]), reads=["rwc"], writes=["rwcb"])
        prm = self.sb(st, "rwprm", [128, 16, 4])
        names = [("rw_mu", 4), ("rw_w0", 2), ("rw_a0", 2)]
        k0 = 0
        for nm, cntk in names:
            P.dma("sp", prm[:, k0:k0 + cntk, :], I[nm][j].rearrange("k (hp p) -> p k hp", p=128), writes=["rwprm"], allow_slow_non_contiguous=True)
            k0 += cntk
        for nm in ("rw_kk", "rw_ka", "rw_gn_g", "rw_gn_b"):
            P.dma("sp", prm[:, k0, :], I[nm][j].rearrange("(hp p) -> p hp", p=128), writes=["rwprm"], allow_slow_non_contiguous=True)
            k0 += 1
        P.dma("sp", prm[:, k0, :], I["rw_rk"][j].rearrange("(hp h2) n -> (h2 n) hp", h2=2), writes=["rwprm"], allow_slow_non_contiguous=True)
        PMU, PW0, PA0, PKK, PKA, PGG, PGB, PRK = 0, 4, 6, 8, 9, 10, 11, 12
        der = self.sb(st, "rwder", [128, 9, 4])
        P.op("dve", lambda e: e.tensor_scalar_mul(out=der[:, 0:4, :], in0=prm[:, 0:4, :], scalar1=0.5), reads=["rwprm"], writes=["rwder"])
        P.op("dve", lambda e: e.tensor_scalar(out=der[:, 4:8, :], in0=prm[:, 0:4, :], scalar1=-1.0, scalar2=1.0, op0=ALU.mult, op1=ALU.add),
             reads=["rwprm"], writes=["rwder"])
        P.op("dve", lambda e: e.tensor_scalar(out=der[:, 8, :], in0=prm[:, PKA, :], scalar1=-1.0, scalar2=1.0, op0=ALU.mult, op1=ALU.add),
             reads=["rwprm"], writes=["rwder"])
        e12 = self.sb(st, "e12", [128, 1]); gne = self.sb(st, "gne", [128, 1])
        P.op("dve", lambda e: e.memset(e12[:], 1e-12), writes=["e12"])
        P.op("dve", lambda e: e.memset(gne[:], 64e-5), writes=["gne"])
        rw = dict(cst=cst, cstb=cstb, prm=prm, der=der, e12=e12, gne=gne, SL=SL, PMU=PMU, PW0=PW0, PA0=PA0, PKK=PKK, PKA=PKA, PGG=PGG, PGB=PGB, PRK=PRK)
        return rw

    def rwkv_partA(self, st, l, g, hT, rw):
        from contextlib import ExitStack
        P, I, O = self.P, self.I, self.O
        j = l // 2
        nl, ll = TT // g.line, g.line
        lines = lambda ap: ap.rearrange("p (a b) -> p a b", b=ll)
        L, nseq = g.L, g.nseq
        rwS = self.rwS
        cst, cstb, prm, der, e12, gne, SL = rw["cst"], rw["cstb"], rw["prm"], rw["der"], rw["e12"], rw["gne"], rw["SL"]
        PMU, PW0, PA0, PKK, PKA, PGG, PGB, PRK = [rw[k] for k in ("PMU", "PW0", "PA0", "PKK", "PKA", "PGG", "PGB", "PRK")]
        wl = self.sb(st, "wl", [128, 8, 4, 128], BF16)
        w2t = self.sb(st, "w2t", [128, 2, 512], BF16)
        with ExitStack() as s2:
            wraw = self.sb(s2, "wraw", [128, 8, 2, 2, 64])
            mux = self.sb(s2, "mux", [128, 2, 8]); omm = self.sb(s2, "omm", [128, 2, 8])
            for ty, nm in enumerate(("rw_w1", "rw_a1")):
                for d in range(2):
                    P.dma("sp", wraw[:, :, ty, d, :], I[nm][j, d].rearrange("(kc p) n -> p kc n", p=128), writes=[("wraw", ty, d)])
            P.dma("sp", mux[:], I["rw_mu_x"][j].rearrange("k (kc p) -> p k kc", p=128), writes=["mux"], allow_slow_non_contiguous=True)
            P.op("dve", lambda e: e.tensor_scalar(out=omm[:], in0=mux[:], scalar1=-1.0, scalar2=1.0, op0=ALU.mult, op1=ALU.add), reads=["mux"], writes=["omm"])
            for ty in range(2):
                for d in range(2):
                    for var, sc in ((0, omm), (1, mux)):
                        P.op("dve", lambda e, ty=ty, d=d, var=var, sc=sc: e.tensor_tensor(
                            out=wl[:, :, ty * 2 + var, d * 64:(d + 1) * 64], in0=wraw[:, :, ty, d, :],
                            in1=sc[:, ty, :].rearrange("p (k o) -> p k o", o=1).to_broadcast([128, 8, 64]), op=ALU.mult),
                            reads=[("wraw", ty, d), "mux", "omm"], writes=["wl"])
            for ty, nm in enumerate(("rw_w2", "rw_a2")):
                for d in range(2):
                    P.dma("pool", w2t[d * 64:(d + 1) * 64, ty, :], I[nm][j, d], writes=["w2t"])
            P.barrier()
        LWI = self.sb(st, "LWI", [128, g.ntok], BF16)
        LAI = self.sb(st, "LAI", [128, g.ntok], BF16)
        tA = self.ring(st, "rta", 4, [128, TT])
        cn = {"t": 0}

        def tmpA():
            n = cn["t"]; cn["t"] += 1
            return tA[n % 4], ("rta", n % 4)

        def lora_tile(t):
            hs = lambda kc: hT[:, kc, t * TT:(t + 1) * TT]
            for ty, dst in ((0, LWI), (1, LAI)):
                pa, pak = self.next_ps(); pb, pbk = self.next_ps()
                P.op("pe", [lambda e, kc=kc: e.matmul(pa[:], lhsT=wl[:, kc, ty * 2, :], rhs=hs(kc), start=(kc == 0), stop=(kc == 7)) for kc in range(8)],
                     reads=["wl"] + self.hkeys(g, t), writes=[pak])
                P.op("pe", [lambda e, kc=kc: e.matmul(pb[:], lhsT=wl[:, kc, ty * 2 + 1, :], rhs=hs(kc), start=(kc == 0), stop=(kc == 7)) for kc in range(8)],
                     reads=["wl"] + self.hkeys(g, t), writes=[pbk])
                xb, xbk = tmpA(); ac, ack = tmpA()
                P.op("act", lambda e: e.copy(out=xb[:], in_=pb[:]), reads=[pbk], writes=[xbk])
                P.op("act", lambda e: e.copy(out=ac[:], in_=pa[:]), reads=[pak], writes=[ack])
                P.op("dve", lambda e: e.scalar_tensor_tensor(out=lines(ac[:])[:, :, 1:], in0=lines(xb[:])[:, :, :ll - 1], scalar=0.5,
                                                             in1=lines(ac[:])[:, :, 1:], op0=ALU.mult, op1=ALU.add), reads=[xbk, ack], writes=[ack])
                P.op("dve", lambda e: e.scalar_tensor_tensor(out=lines(ac[:])[:, :, :ll - 1], in0=lines(xb[:])[:, :, 1:], scalar=0.5,
                                                             in1=lines(ac[:])[:, :, :ll - 1], op0=ALU.mult, op1=ALU.add), reads=[xbk, ack], writes=[ack])
                fn = AF.Tanh if ty == 0 else AF.Identity
                P.op("act", lambda e: e.activation(out=dst[:, t * TT:(t + 1) * TT], in_=ac[:], func=fn), reads=[ack], writes=[("LI", ty, t)])
        if "norwL" not in self.flags:
            for t in range(g.ntile):
                lora_tile(t)

        wr = self.ring(st, "wrw", 6, [128, 8, 128], BF16)
        wcn = {"n": 0}

        def getw(col0):
            n = wcn["n"]; wcn["n"] += 1
            wt, wk = wr[n % 6], ("wrw", n % 6)
            self.w_chunk(wt, wk, I["cd_w_in"][j, :, col0:col0 + 128])
            return wt, wk

        def store(slot, hp, t, tl, tk):
            gt0 = g.tok0 + t * TT
            P.dma("sp", rwS[hp, SL[slot], :, gt0:gt0 + TT], tl[:], reads=[tk], writes=[("rwS", hp, slot, gt0)])

        roleA = {nm: self.sb(st, "ra_" + nm, [128, TT]) for nm in ("xl0", "xl1", "y0", "y1", "y2", "y3", "kq", "sq", "lw", "a", "kd0", "kd1", "bs")}
        sqb = self.sb(st, "ra_sqb", [128, TT], BF16)

        def role(nm):
            return roleA[nm], ("ra", nm)

        def stageA_tile(hp, t, ws):
            hs = lambda kc: hT[:, kc, t * TT:(t + 1) * TT]
            sl_ = slice(t * TT, (t + 1) * TT)
            outs = []
            for n4 in range(4):
                pt, pk = self.next_ps()
                self.mm_group(pt[:], pk, ws[n4][0], ws[n4][1], hs, self.hkeys(g, t))
                xl, xlk = role("xl%d" % (n4 % 2)); y, yk = role("y%d" % n4)
                P.op("act", lambda e, xl=xl, pt=pt: e.copy(out=xl[:], in_=pt[:]), reads=[pk], writes=[xlk])
                P.op("dve", lambda e, xl=xl, y=y, n4=n4: e.tensor_scalar_mul(out=y[:], in0=xl[:], scalar1=der[:, 4 + n4, hp:hp + 1]),
                     reads=[xlk, "rwder"], writes=[yk])
                P.op("dve", lambda e, xl=xl, y=y, n4=n4: e.scalar_tensor_tensor(out=lines(y[:])[:, :, 1:], in0=lines(xl[:])[:, :, :ll - 1],
                                                                                scalar=der[:, n4, hp:hp + 1], in1=lines(y[:])[:, :, 1:], op0=ALU.mult, op1=ALU.add),
                     reads=[xlk, yk], writes=[yk])
                P.op("dve", lambda e, xl=xl, y=y, n4=n4: e.scalar_tensor_tensor(out=lines(y[:])[:, :, :ll - 1], in0=lines(xl[:])[:, :, 1:],
                                                                                scalar=der[:, n4, hp:hp + 1], in1=lines(y[:])[:, :, :ll - 1], op0=ALU.mult, op1=ALU.add),
                     reads=[xlk, yk], writes=[yk])
                outs.append((y, yk))
            (r_, rk), (k_, kk_k), (v_, vk), (g_, gk) = outs
            P.op("act", lambda e: e.activation(out=g_[:], in_=g_[:], func=AF.Sigmoid), reads=[gk], writes=[gk])
            store("gs", hp, t, g_, gk); store("r", hp, t, r_, rk); store("v", hp, t, v_, vk)
            kq, kqk = role("kq"); sq, sqk = role("sq")
            P.op("dve", lambda e: e.tensor_scalar_mul(out=kq[:], in0=k_[:], scalar1=prm[:, PKK, hp:hp + 1]), reads=[kk_k, "rwprm"], writes=[kqk])
            P.op("act", lambda e: e.activation(out=sqb[:], in_=kq[:], func=AF.Square), reads=[kqk], writes=["sqb"])
            pn, pnk = self.next_ps()
            P.op("pe", lambda e: e.matmul(pn[:], lhsT=cstb[:, 4, :], rhs=sqb[:], start=True, stop=True), reads=["rwcb", "sqb"], writes=[pnk])
            P.op("act", lambda e: e.activation(out=sq[:], in_=pn[:], func=AF.Sqrt, bias=e12[:, 0:1]), reads=[pnk, "e12"], writes=[sqk])
            P.op("dve", lambda e: e.reciprocal(out=sq[:], in_=sq[:]), reads=[sqk], writes=[sqk])
            P.op("dve", lambda e: e.tensor_tensor(out=kq[:], in0=kq[:], in1=sq[:], op=ALU.mult), reads=[kqk, sqk], writes=[kqk])
            store("kk", hp, t, kq, kqk)
            kds = []
            for d in range(2):
                pw, pwk = self.next_ps(); pa, pak = self.next_ps()
                P.op("pe", lambda e, d=d, pw=pw: e.matmul(pw[:], lhsT=w2t[d * 64:(d + 1) * 64, 0, hp * 128:(hp + 1) * 128],
                                                          rhs=LWI[d * 64:(d + 1) * 64, sl_], start=True, stop=True), reads=["w2t", ("LI", 0, t)], writes=[pwk], serial=True)
                P.op("pe", lambda e, d=d, pa=pa: e.matmul(pa[:], lhsT=w2t[d * 64:(d + 1) * 64, 1, hp * 128:(hp + 1) * 128],
                                                          rhs=LAI[d * 64:(d + 1) * 64, sl_], start=True, stop=True), reads=["w2t", ("LI", 1, t)], writes=[pak], serial=True)
                lw, lwk = role("lw"); a_, ak = role("a"); kd, kdk = role("kd%d" % d)
                P.op("act", lambda e, d=d, lw=lw, pw=pw: e.activation(out=lw[:], in_=pw[:], func=AF.Sigmoid, bias=prm[:, PW0 + d, hp:hp + 1]),
                     reads=[pwk, "rwprm"], writes=[lwk])
                P.op("dve", lambda e, lw=lw: e.tensor_scalar_mul(out=lw[:], in0=lw[:], scalar1=-float(np.exp(-0.5))), reads=[lwk], writes=[lwk])
                store("lw%d" % d, hp, t, lw, lwk)
                P.op("act", lambda e, d=d, a_=a_, pa=pa: e.activation(out=a_[:], in_=pa[:], func=AF.Sigmoid, bias=prm[:, PA0 + d, hp:hp + 1]),
                     reads=[pak, "rwprm"], writes=[ak])
                P.op("dve", lambda e, a_=a_, kd=kd: e.tensor_scalar(out=kd[:], in0=a_[:], scalar1=prm[:, PKA, hp:hp + 1], scalar2=der[:, 8, hp:hp + 1],
                                                                    op0=ALU.mult, op1=ALU.add), reads=[ak, "rwprm", "rwder"], writes=[kdk])
                P.op("dve", lambda e, kd=kd: e.tensor_tensor(out=kd[:], in0=kd[:], in1=k_[:], op=ALU.mult), reads=[kdk, kk_k], writes=[kdk])
                store("kd%d" % d, hp, t, kd, kdk)
                P.op("dve", lambda e, a_=a_: e.tensor_tensor(out=a_[:], in0=a_[:], in1=kq[:], op=ALU.mult), reads=[ak, kqk], writes=[ak])
                store("b%d" % d, hp, t, a_, ak)
                kds.append((kd, kdk))
            bs, bsk = role("bs")
            P.op("dve", lambda e: e.tensor_tensor(out=bs[:], in0=kds[0][0][:], in1=kds[1][0][:], op=ALU.add), reads=[kds[0][1], kds[1][1]], writes=[bsk])
            P.op("dve", lambda e: e.scalar_tensor_tensor(out=sqb[:], in0=r_[:], scalar=prm[:, PRK, hp:hp + 1], in1=bs[:], op0=ALU.mult, op1=ALU.mult),
                 reads=[rk, bsk, "rwprm"], writes=["sqb"])
            pbn, pbnk = self.next_ps()
            P.op("pe", lambda e: e.matmul(pbn[:], lhsT=cstb[:, 4, :], rhs=sqb[:], start=True, stop=True), reads=["rwcb", "sqb"], writes=[pbnk])
            P.op("dve", lambda e: e.tensor_tensor(out=bs[:], in0=pbn[:], in1=v_[:], op=ALU.mult), reads=[pbnk, vk], writes=[bsk])
            store("bon", hp, t, bs, bsk)

        for hp in range(4):
            ws = [getw(1536 + n4 * 512 + hp * 128) for n4 in range(4)]
            if "norwA" not in self.flags:
                for t in range(g.ntile):
                    stageA_tile(hp, t, ws)

    def rwkv_partB(self, st, l, g, rw):
        from contextlib import ExitStack
        P, I, O = self.P, self.I, self.O
        j = l // 2
        L, nseq = g.L, g.nseq
        CH = 128
        rwS = self.rwS
        cst, cstb, prm, der, e12, gne, SL = rw["cst"], rw["cstb"], rw["prm"], rw["der"], rw["e12"], rw["gne"], rw["SL"]
        PGG, PGB = rw["PGG"], rw["PGB"]
        tA = self.ring(st, "rtb", 8, [128, TT])
        cn = {"t": 0}

        def tmpA():
            n = cn["t"]; cn["t"] += 1
            return tA[n % 8], ("rtb", n % 8)
        def stageB(hp, sq_, d, bst):
            (ldr, TLar, Bt, Kt, Bh, Kh, Vb, clt, ext, rmask, N2r, X2r, TTr, ATr, TOKr, W1Tr, Zr_, U0r, U2r, ST32, STb, oacc, snat, pcr) = bst
            nch = L // CH
            ntile = max(1, L // TT)
            tw = min(TT, L)
            cpt = tw // CH
            hb_ = 0
            if g.name == "s":
                P.op("dve", lambda e: e.memset(snat[:], 0.0), writes=["snat"])
                yield
                for h2 in range(2):
                    P.dma("sp", snat[h2 * 64:(h2 + 1) * 64, h2 * 64:(h2 + 1) * 64], I["st_rwkv"][j, d, hp * 2 + h2], writes=["snat"])
                    yield
                pt, pk = self.next_ps()
                P.op("pe", lambda e: e.transpose(pt[:, 0:128], snat[:], self.ident[:]), reads=["snat", "ident"], writes=[pk])
                yield
                P.op("dve", lambda e: e.tensor_copy(out=ST32[:], in_=pt[:, 0:128]), reads=[pk], writes=["ST32"])
                yield
            else:
                P.op("dve", lambda e: e.memset(ST32[:], 0.0), writes=["ST32"])
                yield
            P.op("act", lambda e: e.copy(out=STb[0][:], in_=ST32[:]), reads=["ST32"], writes=[("STb", 0)])
            yield
            stn = {"n": 0}
            order = list(range(ntile)) if d == 0 else list(reversed(range(ntile)))
            for ti, t in enumerate(order):
                tok_lo = g.tok0 + sq_ * L + t * tw
                gt0 = (tok_lo // TT) * TT
                ld = {}
                for q, nm in enumerate(("r", "lw%d" % d, "kd%d" % d, "v", "kk", "b%d" % d)):
                    tl, tk = ldr[q], ("ldr", q)
                    P.dma("sp", tl[:, 0:tw], rwS[hp, SL[nm], :, tok_lo:tok_lo + tw], reads=[("rwS", hp, nm, gt0)], writes=[tk])
                    yield
                    ld[q] = (tl, tk)
                (r_, rk), (lw, lwk), (kd, kdk), (v_, vk), (kk, kkk), (b_, bk) = [ld[q] for q in range(6)]
                if "nosb2" in self.flags:
                    continue
                W = slice(0, tw)
                c3 = lambda ap: ap[:, W].rearrange("p (c i) -> p c i", i=CH)
                cl, e1, e2, e3, e4 = clt, ext[0], ext[1], ext[2], ext[3]
                if d == 0:
                    P.op("dve", lambda e: e.tensor_tensor_scan(out=cl[:, W], data0=rmask[:, 0, W], data1=lw[:, W], initial=0.0, op0=ALU.mult, op1=ALU.add),
                         reads=[lwk, "rmask"], writes=["cl"])
                    yield
                    cend = c3(cl)[:, :, CH - 1:CH]
                else:
                    P.op("dve", lambda e: e.tensor_tensor_scan(out=cl[:, W][:, ::-1], data0=rmask[:, 1, W][:, ::-1], data1=lw[:, W][:, ::-1], initial=0.0,
                                                               op0=ALU.mult, op1=ALU.add), reads=[lwk, "rmask"], writes=["cl"])
                    yield
                    cend = c3(cl)[:, :, 0:1]
                P.op("act", lambda e: e.activation(out=e1[:, W], in_=cl[:, W], func=AF.Exp), reads=["cl"], writes=["e1"])
                yield
                P.op("act", lambda e: e.activation(out=e2[:, W], in_=cl[:, W], func=AF.Exp, scale=-1.0), reads=["cl"], writes=["e2"])
                yield
                P.op("dve", lambda e: e.tensor_tensor(out=e3[:, W], in0=cl[:, W], in1=lw[:, W], op=ALU.subtract), reads=["cl", lwk], writes=["e3"])
                yield
                P.op("act", lambda e: e.activation(out=e3[:, W], in_=e3[:, W], func=AF.Exp), reads=["e3"], writes=["e3"])
                yield
                P.op("dve", lambda e: e.tensor_tensor(out=c3(e4), in0=cend.to_broadcast([128, cpt, CH]), in1=c3(cl), op=ALU.subtract),
                     reads=["cl"], writes=["e4"])
                yield
                P.op("act", lambda e: e.activation(out=e4[:, W], in_=e4[:, W], func=AF.Exp), reads=["e4"], writes=["e4"])
                yield
                pc, pck = pcr[ti % 2], ("pcr", ti % 2)
                P.op("act", lambda e: e.activation(out=pc[:, 0:cpt], in_=cend.rearrange("p c o -> p (c o)"), func=AF.Exp), reads=["cl"], writes=[pck])
                yield
                P.op("dve", lambda e: e.scalar_tensor_tensor(out=TLar[:, 0:cpt, 0, :], in0=c3(kk), scalar=-1.0, in1=c3(e3), op0=ALU.mult, op1=ALU.mult),
                     reads=[kkk, "e3"], writes=["TLa"])
                yield
                P.op("dve", lambda e: e.tensor_tensor(out=TLar[:, 0:cpt, 1, :], in0=c3(r_), in1=c3(e1), op=ALU.mult), reads=[rk, "e1"], writes=["TLr"])
                yield
                P.op("dve", lambda e: e.tensor_tensor(out=Bt[:, W], in0=b_[:, W], in1=e2[:, W], op=ALU.mult), reads=[bk, "e2"], writes=["Bt"])
                yield
                P.op("dve", lambda e: e.tensor_tensor(out=Kt[:, W], in0=kd[:, W], in1=e2[:, W], op=ALU.mult), reads=[kdk, "e2"], writes=["Kt"])
                yield
                P.op("pool", lambda e: e.tensor_tensor(out=Bh[:, W], in0=b_[:, W], in1=e4[:, W], op=ALU.mult), reads=[bk, "e4"], writes=["Bh"])
                yield
                P.op("pool", lambda e: e.tensor_tensor(out=Kh[:, W], in0=kd[:, W], in1=e4[:, W], op=ALU.mult), reads=[kdk, "e4"], writes=["Kh"])
                yield
                P.op("act", lambda e: e.copy(out=Vb[:, W], in_=v_[:, W]), reads=[vk], writes=["Vb"])
                yield
                if "nosb3" in self.flags:
                    continue
                corder = list(range(cpt)) if d == 0 else list(reversed(range(cpt)))
                for c in corder:
                    yield from self.rwkv_chunk(hp, sq_, d, t, c, tw, cst, cstb, bst, pc, pck, stn, g)
            if g.name == "p":
                pt, pk = self.next_ps()
                P.op("pe", lambda e: e.transpose(pt[:, 0:128], ST32[:], self.ident[:]), reads=["ST32", "ident"], writes=[pk])
                yield
                P.op("dve", lambda e: e.tensor_copy(out=snat[:], in_=pt[:, 0:128]), reads=[pk], writes=["snat"])
                yield
                for h2 in range(2):
                    P.dma("sp", O["nrwkv"][sq_, j, d, hp * 2 + h2], snat[h2 * 64:(h2 + 1) * 64, h2 * 64:(h2 + 1) * 64], reads=["snat"],
                          writes=[("nrwkv", sq_, j, d, hp, h2)])
                    yield

        with ExitStack() as sB:
            rmask = self.sb(sB, "rmask", [128, 2, TT])
            P.op("dve", lambda e: e.memset(rmask[:], 1.0), writes=["rmask"])
            P.op("dve", lambda e: e.memset(rmask[:, 0, :].rearrange("p (c i) -> p c i", i=CH)[:, :, 0:1], 0.0), writes=["rmask"])
            P.op("dve", lambda e: e.memset(rmask[:, 1, :].rearrange("p (c i) -> p c i", i=CH)[:, :, CH - 1:CH], 0.0), writes=["rmask"])
            bsts = {}
            self.TTb_d = {}
            for d in range(2):
                ldr = self.ring(sB, "ldr", 6, [128, TT])
                TLar = self.sb(sB, "TLar", [128, 4, 2, 128], BF16)
                Bt = self.sb(sB, "Bt", [128, TT], BF16); Kt = self.sb(sB, "Kt", [128, TT], BF16)
                Bh = self.sb(sB, "Bh", [128, TT], BF16); Kh = self.sb(sB, "Kh", [128, TT], BF16); Vb = self.sb(sB, "Vb", [128, TT], BF16)
                clt = self.sb(sB, "clt", [128, TT]); ext = self.ring(sB, "ext", 4, [128, TT])
                N2r = self.ring(sB, "N2", 2, [128, 2, 128]); X2r = self.ring(sB, "X2", 2, [128, 2, 128])
                TTr = self.ring(sB, "TT", 2, [128, 2, 128])
                self.TTb_d[d] = self.sb(sB, "TTb", [128, 2, 128], BF16)
                ATr = self.sb(sB, "ATr", [128, 3, 2, 128], BF16)
                TOKr = self.sb(sB, "TOK", [128, 4, 128], BF16)
                W1Tr = self.sb(sB, "W1T", [128, 128], BF16)
                Zr_ = self.sb(sB, "Zrw", [128, 128], BF16); U0r = self.sb(sB, "U0", [128, 128]); U2r = self.sb(sB, "U2", [128, 128], BF16)
                ST32 = self.sb(sB, "ST32", [128, 128]); STb = self.ring(sB, "STb", 2, [128, 128], BF16)
                oacc = self.sb(sB, "oacc", [128, g.ntok])
                snat = self.sb(sB, "snat", [128, 128]); pcr = self.ring(sB, "pcr", 2, [128, 4])
                bsts[d] = (ldr, TLar, Bt, Kt, Bh, Kh, Vb, clt, ext, rmask, N2r, X2r, TTr, ATr, TOKr, W1Tr, Zr_, U0r, U2r, ST32, STb, oacc, snat, pcr)
            msr = self.ring(sB, "msd", 2, [128, g.ntok], BF16)
            self.post_b = (self.sb(sB, "post_ob", [128, TT], BF16), self.sb(sB, "post_sq", [128, TT], BF16))
            self.cstb_ = cstb

            def thread(hp, d):
                for sq_ in range(nseq):
                    yield from stageB(hp, sq_, d, bsts[d])

            def run_threads(gens):
                gens = list(gens)
                while gens:
                    for item in list(gens):
                        P.ns = item[0]
                        try:
                            next(item[1])
                        except StopIteration:
                            gens.remove(item)
                P.ns = None

            for hp in range(4):
                if "norwB" not in self.flags:
                    run_threads([(d, thread(hp, d)) for d in range(2)])
                if "norwP" not in self.flags:
                    self.rwkv_post(hp, g, j, cst, prm, gne, (bsts[0][21], bsts[1][21]), msr[hp % 2], ("msd", hp % 2), tmpA, SL, PGG, PGB)

    def rwkv_chunk(self, hp, sq_, d, t, c, tw, cst, cstb, bst, pc, pck, stn, g):
        P = self.P
        (ldr, TLar, Bt, Kt, Bh, Kh, Vb, clt, ext, rmask, N2r, X2r, TTr, ATr, TOKr, W1Tr, Zr_, U0r, U2r, ST32, STb, oacc, snat, pcr) = bst
        CH = 128
        cs = slice(c * CH, (c + 1) * CH)
        mN, mX, mI = (0, 2, 3) if d == 0 else (2, 0, 1)
        h2v = lambda ap, w: ap[:, 0:2 * w].rearrange("p (h i) -> p h i", h=2)
        bc2 = lambda m: cst[:, m, :].rearrange("p (o i) -> p o i", o=1).to_broadcast([128, 2, 128])
        tl_keys = ["TLa", "TLr"]
        pA, pAk = self.next_ps()
        pB, pBk = self.next_ps()
        pC, pCk = self.next_ps()
        N_, X_, T_ = N2r[0], X2r[0], TTr[0]
        for h in range(2):
            tl2 = TLar[h * 64:(h + 1) * 64, c, :, :].rearrange("p a i -> p (a i)")
            P.op("pe", lambda e, h=h: e.matmul(pA[:, h * 128:(h + 1) * 128], lhsT=TLar[h * 64:(h + 1) * 64, c, 0, :], rhs=Bt[h * 64:(h + 1) * 64, cs],
                                               start=True, stop=True), reads=["TLa", "Bt"], writes=[pAk], serial=True)
            P.op("pe", lambda e, h=h, tl2=tl2: e.matmul(pB[:, h * 256:(h + 1) * 256], lhsT=Bt[h * 64:(h + 1) * 64, cs], rhs=tl2, start=True, stop=True),
                 reads=["TLa", "TLr", "Bt"], writes=[pBk])
            P.op("pe", lambda e, h=h, tl2=tl2: e.matmul(pC[:, h * 256:(h + 1) * 256], lhsT=Kt[h * 64:(h + 1) * 64, cs], rhs=tl2, start=True, stop=True),
                 reads=["TLa", "TLr", "Kt"], writes=[pCk])
        P.op("dve", lambda e: e.tensor_tensor(out=N_[:], in0=h2v(pA, 128), in1=bc2(mN), op=ALU.mult), reads=[pAk, "rwc"], writes=[("N2", 0)])
        yield
        pB4 = pB[:, 0:512].rearrange("p (h a i) -> p h a i", h=2, a=2)
        P.op("dve", lambda e: e.tensor_tensor(out=X_[:], in0=pB4[:, :, 0, :], in1=bc2(mX), op=ALU.mult), reads=[pBk, "rwc"], writes=[("X2", 0)])
        yield
        P.op("dve", lambda e: e.tensor_tensor(out=ATr[:, 0, :, :], in0=pB4[:, :, 1, :], in1=bc2(mI), op=ALU.mult), reads=[pBk, "rwc"], writes=["ArbT"])
        yield
        pC4 = pC[:, 0:512].rearrange("p (h a i) -> p h a i", h=2, a=2)
        P.op("dve", lambda e: e.tensor_tensor(out=ATr[:, 1, :, :], in0=pC4[:, :, 0, :], in1=bc2(mX), op=ALU.mult), reads=[pCk, "rwc"], writes=["AakT"])
        yield
        P.op("dve", lambda e: e.tensor_tensor(out=ATr[:, 2, :, :], in0=pC4[:, :, 1, :], in1=bc2(mI), op=ALU.mult), reads=[pCk, "rwc"], writes=["ArkT"])
        yield
        self._pe_serial_next = True
        idb = self.ident[:].rearrange("p (o i) -> p o i", o=1).to_broadcast([128, 2, 128])
        P.op("pool", lambda e: e.tensor_tensor(out=T_[:], in0=X_[:], in1=idb, op=ALU.add), reads=[("X2", 0), "ident"], writes=[("TT", 0)])
        yield
        cur = 0
        for r in range(1, 7):
            nx = 1 - cur
            Nc, Xc, Tc = N2r[cur], X2r[cur], TTr[cur]
            Nn, Xn, Tn = N2r[nx], X2r[nx], TTr[nx]
            pn, pnk = self.next_ps()
            P.op("pe", [lambda e, h=h, Nc=Nc, Xc=Xc, pn=pn: e.matmul(pn[:, h * 128:(h + 1) * 128], lhsT=Xc[:, h, :], rhs=Nc[:, h, :], start=True, stop=True)
                        for h in range(2)], reads=[("X2", cur), ("N2", cur)], writes=[pnk])
            yield
            P.op("act", lambda e, Nn=Nn, pn=pn: e.copy(out=Nn[:], in_=h2v(pn, 128)), reads=[pnk], writes=[("N2", nx)])
            yield
            if r < 6:
                px, pxk = self.next_ps()
                P.op("pe", [lambda e, h=h, Nc=Nc, Xc=Xc, px=px: e.matmul(px[:, h * 128:(h + 1) * 128], lhsT=Nc[:, h, :], rhs=Xc[:, h, :], start=True, stop=True)
                            for h in range(2)], reads=[("X2", cur), ("N2", cur)], writes=[pxk])
                yield
                P.op("act", lambda e, Xn=Xn, px=px: e.copy(out=Xn[:], in_=h2v(px, 128)), reads=[pxk], writes=[("X2", nx)])
                yield
            pt_, ptk = self.next_ps()
            P.op("pe", [lambda e, h=h, Nn=Nn, Tc=Tc, pt_=pt_: e.matmul(pt_[:, h * 128:(h + 1) * 128], lhsT=Nn[:, h, :], rhs=Tc[:, h, :], start=True, stop=True)
                        for h in range(2)], reads=[("N2", nx), ("TT", cur)], writes=[ptk])
            yield
            P.op("dve", lambda e, Tn=Tn, Tc=Tc, pt_=pt_: e.tensor_tensor(out=Tn[:], in0=h2v(pt_, 128), in1=Tc[:], op=ALU.add),
                 reads=[ptk, ("TT", cur)], writes=[("TT", nx)])
            yield
            cur = nx
        Tf, Tfk = self.TTb_d[d], "TTb"
        P.op("pool", lambda e: e.tensor_copy(out=Tf[:], in_=TTr[cur][:]), reads=[("TT", cur)], writes=["TTb"])
        yield
        if "norc2" in self.flags:
            return
        ptk_, ptkk = self.next_ps()
        pv = ptk_[:].bitcast(BF16)
        srcs = [(TLar[:, c, 0, :], "TLa"), (Bh[:, cs], "Bh"), (Kh[:, cs], "Kh"), (Vb[:, cs], "Vb")]
        P.op("pe", [lambda e, q=q, s=s: e.transpose(pv[:, q * 128:(q + 1) * 128], s[0], self.identb[:]) for q, s in enumerate(srcs)],
             reads=["TLa", "Bh", "Kh", "Vb", "identb"], writes=[ptkk])
        yield
        P.op("act", lambda e: e.copy(out=TOKr[:], in_=pv[:, 0:512].rearrange("p (q i) -> p q i", q=4)), reads=[ptkk], writes=["TOK"])
        yield
        pw, pwk = self.next_ps()
        P.op("pe", [lambda e, h=h: e.matmul(pw[:, h * 128:(h + 1) * 128], lhsT=TOKr[:, 0, :], rhs=Tf[:, h, :], start=True, stop=True) for h in range(2)],
             reads=["TOK", Tfk], writes=[pwk])
        yield
        P.op("act", lambda e: e.copy(out=W1Tr[0:64, :], in_=pw[0:64, 0:128]), reads=[pwk], writes=["W1Ta"])
        yield
        P.op("dve", lambda e: e.tensor_copy(out=W1Tr[64:128, :], in_=pw[64:128, 128:256]), reads=[pwk], writes=["W1Tb"])
        yield
        pz, pzk = self.next_ps()
        P.op("pe", [lambda e, h=h: e.matmul(pz[:, h * 64:(h + 1) * 64], lhsT=ATr[:, 1, h, :], rhs=TOKr[:, 3, h * 64:(h + 1) * 64], start=True, stop=True)
                    for h in range(2)], reads=["AakT", "TOK"], writes=[pzk])
        yield
        P.op("act", lambda e: e.copy(out=Zr_[:], in_=pz[:, 0:128]), reads=[pzk], writes=["Zrw"])
        yield
        pu0, pu0k = self.next_ps()
        P.op("pe", [lambda e, h=h: e.matmul(pu0[:, h * 64:(h + 1) * 64], lhsT=Tf[:, h, :], rhs=Zr_[:, h * 64:(h + 1) * 64], start=True, stop=True)
                    for h in range(2)], reads=[Tfk, "Zrw"], writes=[pu0k])
        yield
        P.op("dve", lambda e: e.tensor_copy(out=U0r[:], in_=pu0[:, 0:128]), reads=[pu0k], writes=["U0"])
        yield
        if "norc3" in self.flags:
            return
        n = stn["n"]; stn["n"] += 1
        Sc, Sck = STb[n % 2], ("STb", n % 2)
        Sn, Snk = STb[(n + 1) % 2], ("STb", (n + 1) % 2)
        pkb, pkbk = self.next_ps()
        P.op("pe", lambda e: e.matmul(pkb[:, 0:128], lhsT=TOKr[:, 2, :], rhs=TOKr[:, 3, :], start=True, stop=False), reads=["TOK"], writes=[pkbk])
        yield
        pu, puk = self.next_ps()
        P.op("pe", lambda e: e.matmul(pu[:, 0:128], lhsT=W1Tr[:], rhs=Sc[:], start=True, stop=True), reads=["W1Ta", "W1Tb", Sck], writes=[puk])
        yield
        P.op("dve", lambda e: e.tensor_tensor(out=U2r[:], in0=pu[:, 0:128], in1=U0r[:], op=ALU.add), reads=[puk, "U0"], writes=["U2"])
        yield
        P.op("pe", lambda e: e.matmul(pkb[:, 0:128], lhsT=TOKr[:, 1, :], rhs=U2r[:], start=False, stop=True), reads=["TOK", "U2"], writes=[pkbk])
        yield
        po, pok = self.next_ps()
        fns = [lambda e: e.matmul(po[:, 0:128], lhsT=TLar[:, c, 1, :], rhs=Sc[:], start=True, stop=False)]
        for h in range(2):
            fns.append(lambda e, h=h: e.matmul(po[:, h * 64:(h + 1) * 64], lhsT=ATr[:, 0, h, :], rhs=U2r[:, h * 64:(h + 1) * 64], start=False, stop=False))
            fns.append(lambda e, h=h: e.matmul(po[:, h * 64:(h + 1) * 64], lhsT=ATr[:, 2, h, :], rhs=TOKr[:, 3, h * 64:(h + 1) * 64], start=False, stop=(h == 1)))
        P.op("pe", fns, reads=["TLr", Sck, "ArbT", "ArkT", "U2", "TOK"], writes=[pok])
        yield
        if "norc4" in self.flags:
            return
        tS = ext[0]
        P.op("dve", lambda e: e.tensor_tensor(out=tS[:, 0:128], in0=pkb[:, 0:128], in1=cst[:, 4, :], op=ALU.mult), reads=[pkbk, "rwc"], writes=["tS"])
        yield
        P.op("dve", lambda e: e.scalar_tensor_tensor(out=ST32[:], in0=ST32[:], scalar=pc[:, c:c + 1], in1=tS[:, 0:128], op0=ALU.mult, op1=ALU.add),
             reads=["ST32", "tS", pck], writes=["ST32"])
        yield
        P.op("act", lambda e: e.copy(out=Sn[:], in_=ST32[:]), reads=["ST32"], writes=[Snk])
        yield
        osb = ext[1]
        P.op("act", lambda e: e.copy(out=osb[:, 0:128], in_=po[:, 0:128]), reads=[pok], writes=["osb"])
        yield
        pot, potk = self.next_ps()
        P.op("pe", lambda e: e.transpose(pot[:, 0:128], osb[:, 0:128], self.ident[:]), reads=["osb", "ident"], writes=[potk])
        yield
        lo = sq_ * g.L + t * tw + c * CH
        P.op("dve", lambda e: e.tensor_copy(out=oacc[:, lo:lo + CH], in_=pot[:, 0:128]), reads=[potk], writes=[("oacc", lo)])
        yield

    def rwkv_post(self, hp, g, j, cst, prm, gne, oacc, ms, msk, tmpA, SL, PGG, PGB):
        P = self.P
        rwS = self.rwS
        oacc0, oacc1 = oacc
        oacc = oacc0
        for t in range(g.ntile):
            sl_ = slice(t * TT, (t + 1) * TT)
            ok0 = [("ns", 0, ("oacc", t * TT + q * 128)) for q in range(4)]
            ok1 = [("ns", 1, ("oacc", t * TT + q * 128)) for q in range(4)]
            P.op("dve", lambda e, sl_=sl_: e.tensor_tensor(out=oacc0[:, sl_], in0=oacc0[:, sl_], in1=oacc1[:, sl_], op=ALU.add), reads=ok0 + ok1, writes=ok0)
            okeys = ok0
            gt0 = g.tok0 + t * TT
            bon, bonk = tmpA(); gs, gsk = tmpA(); sq, sqk = tmpA(); mean, meank = tmpA(); on, onk = tmpA()
            P.dma("sp", bon[:], rwS[hp, SL["bon"], :, gt0:gt0 + TT], reads=[("rwS", hp, "bon", gt0)], writes=[bonk])
            P.dma("sp", gs[:], rwS[hp, SL["gs"], :, gt0:gt0 + TT], reads=[("rwS", hp, "gs", gt0)], writes=[gsk])
            ob, sb2 = self.post_b
            P.op("act", lambda e, sl_=sl_: e.activation(out=sb2[:], in_=oacc[:, sl_], func=AF.Square), reads=okeys, writes=["post_sq"])
            P.op("act", lambda e, sl_=sl_: e.copy(out=ob[:], in_=oacc[:, sl_]), reads=okeys, writes=["post_ob"])
            pm, pmk = self.next_ps(); p2, p2k = self.next_ps()
            P.op("pe", lambda e, pm=pm: e.matmul(pm[:], lhsT=self.cstb_[:, 4, :], rhs=ob[:], start=True, stop=True), reads=["post_ob", "rwcb"], writes=[pmk])
            P.op("pe", lambda e, p2=p2: e.matmul(p2[:], lhsT=self.cstb_[:, 4, :], rhs=sb2[:], start=True, stop=True), reads=["post_sq", "rwcb"], writes=[p2k])
            P.op("act", lambda e, mean=mean, pm=pm: e.activation(out=mean[:], in_=pm[:], func=AF.Identity, scale=1.0 / 64.0), reads=[pmk], writes=[meank])
            P.op("dve", lambda e, sq=sq, mean=mean: e.tensor_tensor(out=sq[:], in0=mean[:], in1=mean[:], op=ALU.mult), reads=[meank], writes=[sqk])
            P.op("dve", lambda e, sq=sq, p2=p2: e.scalar_tensor_tensor(out=sq[:], in0=p2[:], scalar=1.0 / 64.0, in1=sq[:], op0=ALU.mult, op1=ALU.subtract),
                 reads=[p2k, sqk], writes=[sqk])
            P.op("act", lambda e, sq=sq: e.activation(out=sq[:], in_=sq[:], func=AF.Sqrt, bias=gne[:, 0:1]), reads=[sqk, "gne"], writes=[sqk])
            P.op("dve", lambda e, sq=sq: e.reciprocal(out=sq[:], in_=sq[:]), reads=[sqk], writes=[sqk])
            P.op("dve", lambda e, on=on, mean=mean, sl_=sl_: e.tensor_tensor(out=on[:], in0=oacc[:, sl_], in1=mean[:], op=ALU.subtract),
                 reads=okeys + [meank], writes=[onk])
            P.op("dve", lambda e, on=on, sq=sq: e.tensor_tensor(out=on[:], in0=on[:], in1=sq[:], op=ALU.mult), reads=[onk, sqk], writes=[onk])
            P.op("dve", lambda e, on=on: e.tensor_scalar(out=on[:], in0=on[:], scalar1=prm[:, PGG, hp:hp + 1], scalar2=prm[:, PGB, hp:hp + 1],
                                                         op0=ALU.mult, op1=ALU.add), reads=[onk, "rwprm"], writes=[onk])
            P.op("dve", lambda e, on=on, bon=bon: e.tensor_tensor(out=on[:], in0=on[:], in1=bon[:], op=ALU.add), reads=[onk, bonk], writes=[onk])
            if g.name == "s":
                out = ms[:, :].rearrange("p (r c) -> p c r", c=64)[:, t * 8:(t + 1) * 8, :]
                a0 = on[:].rearrange("p (c r) -> p c r", r=64); a1 = gs[:].rearrange("p (c r) -> p c r", r=64)
            else:
                out, a0, a1 = ms[:, sl_], on[:], gs[:]
            P.op("dve", lambda e, out=out, a0=a0, a1=a1: e.tensor_tensor(out=out, in0=a0, in1=a1, op=ALU.mult), reads=[onk, gsk], writes=[msk + (t,)])
        P.dma("sp", self.mixT[512 + hp * 128:512 + (hp + 1) * 128, g.tok0:g.tok0 + g.ntok], ms[:],
              reads=[msk + (t,) for t in range(g.ntile)], writes=[("mixT", 4 + hp, g.tok0 + t * TT) for t in range(g.ntile)])


WEIGHT_SHAPES = {
    "w_mod": [4, 1024, 6144], "b_mod": [4, 6144], "ln1_g": [4, 1024], "ln1_b": [4, 1024], "ln2_g": [4, 1024], "ln2_b": [4, 1024],
    "mlp_w1": [4, 1024, 4096], "mlp_w2": [4, 4096, 1024], "w_out": [4, 1024, 1024], "ab_w_in": [2, 1024, 2560],
    "sc_conv": [2, 3, 512], "lru_conv": [2, 4, 512], "lru_conv_b": [2, 512], "lru_wa": [2, 2, 8, 64, 64], "lru_ba": [2, 2, 512],
    "lru_wi": [2, 2, 8, 64, 64], "lru_bi": [2, 2, 512], "lru_lambda": [2, 2, 512], "cd_w_in": [2, 1024, 3584],
    "hy_conv": [2, 3, 1536], "hy_w1": [2, 33, 64], "hy_b1": [2, 64], "hy_w2": [2, 64, 64], "hy_b2": [2, 64], "hy_w3": [2, 64, 2048],
    "hy_freq": [2, 64], "hy_bias": [2, 2, 512], "rw_mu": [2, 4, 512], "rw_mu_x": [2, 2, 1024], "rw_w0": [2, 2, 512],
    "rw_w1": [2, 2, 1024, 64], "rw_w2": [2, 2, 64, 512], "rw_a0": [2, 2, 512], "rw_a1": [2, 2, 1024, 64], "rw_a2": [2, 2, 64, 512],
    "rw_kk": [2, 512], "rw_ka": [2, 512], "rw_rk": [2, 8, 64], "rw_gn_g": [2, 512], "rw_gn_b": [2, 512],
}

CONST_SPECS = [
    ("rw_masks", [6, 128, 128], F32),
    ("TC4096", [32, 128, 32, 128], BF16), ("TS4096", [32, 128, 32, 128], BF16),
    ("IC4096", [8, 128, 32, 512], BF16), ("IS4096", [8, 128, 32, 512], BF16),
    ("TC256", [2, 128, 2, 128], BF16), ("TS256", [2, 128, 2, 128], BF16),
    ("IC256", [1, 128, 2, 256], BF16), ("IS256", [1, 128, 2, 256], BF16),
    ("featT4096", [33, 4096], F32), ("featT256", [33, 256], F32),
    ("ntn4096", [128, 32], F32), ("ntn256", [128, 2], F32),
    ("delta_b", [128, 512], F32), ("m1col", [128, 1], F32),
]
_CONSTS = None


def make_consts():
    global _CONSTS
    if _CONSTS is not None:
        return _CONSTS
    c = {}
    i = np.arange(128)[:, None]; jj = np.arange(128)[None, :]
    bd = (i // 64 == jj // 64)
    c["rw_masks"] = np.stack([jj < i, jj <= i, jj > i, jj >= i, bd, bd / 64.0]).astype(np.float32)
    bf = ml_dtypes.bfloat16
    for L, sfx, TI in ((4096, "4096", 512), (256, "256", 256)):
        nb = L // 128
        k = np.arange(L, dtype=np.float64) + 0.5
        s_ = np.arange(L, dtype=np.float64)
        th = np.mod(np.outer(s_, k), 2.0 * L) * (np.pi / L)
        C = np.cos(th); S = -np.sin(th)
        for nm, M in (("TC", C), ("TS", S)):
            c[nm + sfx] = np.ascontiguousarray(M.reshape(nb, 128, nb, 128).transpose(2, 1, 0, 3)).astype(bf)
        for nm, M in (("IC", C), ("IS", S)):
            c[nm + sfx] = np.ascontiguousarray(M.reshape(L // TI, TI, nb, 128).transpose(0, 3, 2, 1)).astype(bf)
        del C, S, th
        tn = np.linspace(0.0, 1.0, L, dtype=np.float32)
        tr = np.arange(L, dtype=np.float32)
        bands = np.linspace(1e-4, 15.0, 16, dtype=np.float32)
        ang = (np.float32(2.0 * np.pi / L) * tr[:, None] * bands[None, :]).astype(np.float32)
        feats = np.concatenate([tn[:, None], np.cos(ang), -np.sin(ang)], -1).astype(np.float32)
        c["featT" + sfx] = np.ascontiguousarray(feats.T)
        c["ntn" + sfx] = np.ascontiguousarray((-tn).reshape(nb, 128).T)
    deltas = np.abs(np.linspace(np.log(1e-2) / 1.5, np.log(1e-2) / 0.3, 512, dtype=np.float32))
    c["delta_b"] = np.ascontiguousarray(np.broadcast_to(deltas[None, :], (128, 512))).astype(np.float32)
    m1 = np.ones((128, 1), np.float32); m1[0, 0] = 0.0
    c["m1col"] = m1
    _CONSTS = c
    return c


_NC_CACHE = {}


def _get_nc(depth=4, groups=("s", "p")):
    key = (depth, tuple(groups))
    if key not in _NC_CACHE:
        _NC_CACHE[key] = Builder(depth=depth, groups=groups).build()
    return _NC_CACHE[key]


def make_in_maps(inputs):
    f = lambda a: np.ascontiguousarray(np.asarray(a, dtype=np.float32))
    w = {k: f(inputs[k]) for k in WEIGHT_SHAPES}
    ident = np.eye(128, dtype=np.float32)
    w.update(make_consts())
    maps = []
    for i in range(NCORES):
        m = dict(w)
        m["xs"] = f(inputs["x_sample"][i])
        m["xp"] = f(inputs["x_prompt"][4 * i:4 * i + 4]).reshape(1024, D)
        m["st_lru"] = f(inputs["state_lru"][i])
        m["st_rwkv"] = f(inputs["state_rwkv"][i])
        m["cvecs"] = np.ascontiguousarray(np.stack([inputs["c"][i], inputs["c_ctx"]]).astype(np.float32))
        m["ident"] = ident
        maps.append(m)
    return maps


def kernel(**inputs):
    nc = _get_nc()
    maps = make_in_maps(inputs)
    res = run_bass_kernel_spmd(nc, maps, core_ids=list(range(NCORES)))
    r = res.results
    y_prompt = np.concatenate([np.asarray(r[i]["yp"]).reshape(4, 256, D) for i in range(NCORES)], 0)
    y_sample = np.stack([np.asarray(r[i]["ys"]) for i in range(NCORES)], 0)
    nlru = np.concatenate([np.asarray(r[i]["nlru"]) for i in range(NCORES)], 0)
    nrwkv = np.concatenate([np.asarray(r[i]["nrwkv"]) for i in range(NCORES)], 0)
    return (y_prompt.astype(np.float32), y_sample.astype(np.float32), nlru.astype(np.float32), nrwkv.astype(np.float32))
```

```python
import numpy as np
import ml_dtypes
import concourse.bass as bass
import concourse.mybir as mybir
from concourse.bass_utils import run_bass_kernel_spmd

F32 = mybir.dt.float32
BF16 = mybir.dt.bfloat16
AF = mybir.ActivationFunctionType
ALU = mybir.AluOpType

D = 1024
KC = 8
NCORES = 8
TT = 512
ALPHA = 8.0 ** 0.25
LN_EPS = 1e-5


class _Rec:
    def __init__(self):
        self.calls = []

    def __getattr__(self, name):
        def f(*args, **kwargs):
            self.calls.append((name, args, kwargs))
            return self
        return f


class Prog:
    ENGS = ("pe", "act", "dve", "pool", "sp")

    def __init__(self, nc):
        self.nc = nc
        self.stream = {e: [] for e in self.ENGS}
        self.sem = {}
        self.cnt = {}
        self.known = {e: {} for e in self.ENGS}
        self.last_w = {}
        self.readers = {}
        self.dma_ring = {}
        self.ndma_sems = 8
        self.ninstr = 0

    def setup(self, stack):
        for e in ("pe", "act", "dve", "pool"):
            self.sem["c_" + e] = stack.enter_context(self.nc.semaphore("c_" + e))
            self.cnt["c_" + e] = 0
        for q in ("sp", "pool", "act"):
            names = []
            for i in range(self.ndma_sems):
                n = f"d_{q}{i}"
                self.sem[n] = stack.enter_context(self.nc.semaphore(n))
                self.cnt[n] = 0
                names.append(n)
            self.dma_ring[q] = [names, 0]

    def _need(self, eng, tickets):
        best = {}
        for t in tickets:
            if t is None:
                continue
            s, v = t
            if eng == "pe" and s == "c_pe":
                continue
            if self.known[eng].get(s, 0) >= v:
                continue
            if best.get(s, 0) < v:
                best[s] = v
        for s, v in best.items():
            self.known[eng][s] = v
            self.stream[eng].append(("wait", (s, v)))

    GLOBAL_KEYS = {"rwc", "rwcb", "ident", "identb", "rwprm", "rmask", "onesb", "rwder", "w2t", "e12", "wl"}
    GLOBAL_PREF = {"ps", "rwS", "mixT", "nrwkv", "xT", "hyS", "tokS", "FS", "LI", "wrw", "hT"}

    def _ns(self, keys):
        ns = getattr(self, "ns", None)
        if ns is None:
            return keys
        out = []
        for k in keys:
            if (isinstance(k, str) and k in self.GLOBAL_KEYS) or (isinstance(k, tuple) and k[0] in self.GLOBAL_PREF):
                out.append(k)
            else:
                out.append(("ns", ns, k))
        return out

    def _deps(self, reads, writes):
        ts = []
        for k in reads:
            ts.append(self.last_w.get(k))
        for k in writes:
            ts.append(self.last_w.get(k))
            ts.extend(self.readers.get(k, ()))
        return ts

    def _commit(self, ticket, reads, writes):
        for k in reads:
            self.readers.setdefault(k, []).append(ticket)
        for k in writes:
            self.last_w[k] = ticket
            self.readers[k] = []

    def op(self, eng, fns, reads=(), writes=(), serial=False):
        if callable(fns):
            fns = [fns]
        calls = []
        for f in fns:
            rec = _Rec()
            f(rec)
            assert len(rec.calls) == 1
            calls.append(rec.calls[0])
        fns = calls
        reads, writes = self._ns(reads), self._ns(writes)
        self._need(eng, self._deps(reads, writes))
        if serial and self.cnt["c_" + eng] > 0:
            sv = ("c_" + eng, self.cnt["c_" + eng])
            if self.known[eng].get(sv[0], 0) < sv[1]:
                self.known[eng][sv[0]] = sv[1]
                self.stream[eng].append(("wait", sv))
        s = "c_" + eng
        self.cnt[s] += 1
        ticket = (s, self.cnt[s])
        self.stream[eng].append(("ops", (fns, ticket)))
        self.ninstr += len(fns)
        self._commit(ticket, reads, writes)
        return ticket

    def dma(self, q, out, in_, reads=(), writes=(), **kw):
        names, idx = self.dma_ring[q]
        s = names[idx % len(names)]
        self.dma_ring[q][1] = idx + 1
        prev = (s, self.cnt[s]) if self.cnt[s] > 0 else None
        reads, writes = self._ns(reads), self._ns(writes)
        self._need(q, self._deps(reads, writes) + [prev])
        self.cnt[s] += 16
        ticket = (s, self.cnt[s])
        self.stream[q].append(("dma", (out, in_, kw, ticket)))
        self.ninstr += 1
        self._commit(ticket, reads, writes)
        return ticket

    def barrier(self):
        for e in self.ENGS:
            self._need(e, [(s, v) for s, v in self.cnt.items() if v > 0])
        self.last_w.clear()
        self.readers.clear()

    def finish(self):
        self._need("sp", [(s, v) for s, v in self.cnt.items() if v > 0])

    def emit(self, block):
        nc = self.nc
        sem = self.sem

        def run(eng_name):
            def body(e):
                for kind, pl in self.stream[eng_name]:
                    if kind == "wait":
                        e.wait_ge(sem[pl[0]], pl[1])
                    elif kind == "ops":
                        fns, (s, v) = pl
                        ins = None
                        for (name, args, kwargs) in fns:
                            ins = getattr(e, name)(*args, **kwargs)
                        ins.then_inc(sem[s], 1)
                    else:
                        out, in_, kw, (s, v) = pl
                        e.dma_start(out=out, in_=in_, **kw).then_inc(sem[s], 16)
            return body

        block.tensor(run("pe"))
        block.scalar(run("act"))
        block.vector(run("dve"))
        block.gpsimd(run("pool"))
        block.sync(run("sp"))


class Group:
    def __init__(self, name, tok0, ntok, nseq, L, line):
        self.name, self.tok0, self.ntok, self.nseq, self.L, self.line = name, tok0, ntok, nseq, L, line
        self.ntile = ntok // TT


class Builder:
    def __init__(self, depth=4, groups=("s", "p"), dbg=False, flags=()):
        self.flags = set(flags)
        self.depth = depth
        self.dbg = dbg
        self.nc = bass.Bass("TRN2", target_bir_lowering=False)
        self.groups = []
        if "s" in groups:
            self.groups.append(Group("s", 0, 4096, 1, 4096, 64))
        if "p" in groups:
            self.groups.append(Group("p", 4096, 1024, 4, 256, 256))
        self.NT = 5120
        self.uid = 0

    def din(self, name, shape, dt=F32):
        return self.nc.dram_tensor(name, list(shape), dt, kind="ExternalInput").ap()

    def dout(self, name, shape, dt=F32):
        return self.nc.dram_tensor(name, list(shape), dt, kind="ExternalOutput").ap()

    def dscr(self, name, shape, dt=F32):
        return self.nc.dram_tensor(name, list(shape), dt).ap()

    def sb(self, stack, name, shape, dt=F32):
        self.uid += 1
        return stack.enter_context(self.nc.sbuf_tensor(f"{name}_{self.uid}", list(shape), dt))

    def ring(self, stack, name, n, shape, dt=F32):
        return [self.sb(stack, f"{name}{i}", shape, dt) for i in range(n)]

    def dump(self, name, ap, reads, dt=F32):
        if not self.dbg:
            return
        shape = list(ap.shape)
        o = self.dout("dbg_" + name, shape, dt)
        self.P.dma("sp", o, ap, reads=reads, writes=[("dbg", name)])

    def next_ps(self):
        i = self.ps_i % (len(self.ps) - 1)
        self.ps_i += 1
        return self.ps[i], ("ps", i)

    def build(self):
        from contextlib import ExitStack
        nc = self.nc
        P = self.P = Prog(nc)
        I = self.I = {}
        O = self.O = {}
        I["xs"] = self.din("xs", [4096, D])
        I["xp"] = self.din("xp", [1024, D])
        I["st_lru"] = self.din("st_lru", [2, 2, 512])
        I["st_rwkv"] = self.din("st_rwkv", [2, 2, 8, 64, 64])
        I["cvecs"] = self.din("cvecs", [2, D])
        I["ident"] = self.din("ident", [128, 128])
        for nm, shp in WEIGHT_SHAPES.items():
            I[nm] = self.din(nm, shp)
        O["ys"] = self.dout("ys", [4096, D])
        O["yp"] = self.dout("yp", [1024, D])
        O["nlru"] = self.dout("nlru", [4, 2, 2, 512])
        O["nrwkv"] = self.dout("nrwkv", [4, 2, 2, 8, 64, 64])
        self.xT = self.dscr("xT_scr", [D, self.NT])
        self.mixT = self.dscr("mixT_scr", [D, self.NT], BF16)
        self.hyS = self.dscr("hyS_scr", [4, 512, self.NT])
        self.rwS = self.dscr("rwS_scr", [4, 11, 128, self.NT])
        self.tokS = self.dscr("tokS_scr", [128, self.NT // 128, 512], BF16)
        self.FS = self.dscr("FS_scr", [2, 34, 128, 2, 512])
        for nm, shp, dt in CONST_SPECS:
            I[nm] = self.din(nm, shp, dt)

        with ExitStack() as top:
            P.setup(top)
            self.ps = [top.enter_context(nc.psum_tensor(f"ps{i}", [128, 512], F32)) for i in range(8)]
            self.ps_i = 0
            self.ident = self.sb(top, "ident", [128, 128])
            P.dma("sp", self.ident[:], I["ident"][:, :], writes=["ident"])
            self.identb = self.sb(top, "identb", [128, 128], BF16)
            P.op("dve", lambda e: e.tensor_copy(out=self.identb[:], in_=self.ident[:]), reads=["ident"], writes=["identb"])
            self.onesb = self.sb(top, "onesb", [128, 128], BF16)
            P.op("dve", lambda e: e.memset(self.onesb[:], 1.0 / 1024.0), writes=["onesb"])
            self.modsT = self.sb(top, "modsT", [128, 4, 2, 48])
            self.lnp = self.sb(top, "lnp", [128, 4, 4, 8])
            for i, nm in enumerate(("ln1_g", "ln1_b", "ln2_g", "ln2_b")):
                for l in range(4):
                    P.dma("sp", self.lnp[:, l, i, :], I[nm][l].rearrange("(kc p) -> p kc", p=128),
                          writes=["lnp"], allow_slow_non_contiguous=True)

            self.epsT = self.sb(top, "epsT", [128, 1])
            P.op("dve", lambda e: e.memset(self.epsT[:], LN_EPS / (ALPHA * ALPHA)), writes=["epsT"])
            self.mods2 = self.sb(top, "mods2", [128, 4, 2, 32])
            self.phase_mods()
            self.phase_prepass()
            for l in range(self.depth):
                for g in self.groups:
                    if l % 2 == 0:
                        self.phase_ab(l, g)
                    else:
                        self.phase_cd(l, g)
                self.phase_dense(l, last=(l == self.depth - 1))
            P.finish()
            with nc.Block() as block:
                P.emit(block)
        return nc

    def phase_mods(self):
        from contextlib import ExitStack
        P, I = self.P, self.I
        with ExitStack() as st:
            cv = self.sb(st, "cv", [128, 2, 8])
            sl = self.sb(st, "sl", [128, 8, 2])
            bm = self.sb(st, "bm", [128, 4, 48])
            P.dma("sp", cv[:], I["cvecs"].rearrange("g (kc p) -> p g kc", p=128), writes=["cv"],
                  allow_slow_non_contiguous=True)
            P.dma("sp", bm[:], I["b_mod"].rearrange("l (m p) -> p l m", p=128), writes=["bm"],
                  allow_slow_non_contiguous=True)
            P.op("act", lambda e: e.activation(out=sl[:].rearrange("p kc g -> p g kc"), in_=cv[:], func=AF.Silu),
                 reads=["cv"], writes=["sl"])
            wbuf = self.ring(st, "wmod", 2, [128, 8, 1536])
            n = 0
            for l in range(self.depth):
                for s4 in range(4):
                    wb = wbuf[n % 2]
                    key = ("wmod", n % 2)
                    P.dma("sp", wb[:], I["w_mod"][l, :, s4 * 1536:(s4 + 1) * 1536].rearrange("(kc p) n -> p kc n", p=128),
                          writes=[key])
                    pt, pk = self.next_ps()
                    fns = []
                    for m in range(12):
                        for kc in range(8):
                            fns.append(lambda e, m=m, kc=kc, wb=wb, pt=pt: e.matmul(
                                pt[:, m * 2:m * 2 + 2], lhsT=wb[:, kc, m * 128:(m + 1) * 128], rhs=sl[:, kc, :],
                                start=(kc == 0), stop=(kc == 7)))
                    P.op("pe", fns, reads=[key, "sl"], writes=[pk])
                    for g in range(2):
                        P.op("dve", lambda e, g=g, l=l, s4=s4, pt=pt: e.tensor_tensor(
                            out=self.modsT[:, l, g, s4 * 12:(s4 + 1) * 12],
                            in0=pt[:, 0:24].rearrange("p (m g) -> p g m", g=2)[:, g, :],
                            in1=bm[:, l, s4 * 12:(s4 + 1) * 12], op=ALU.add),
                            reads=[pk, "bm"], writes=[("modsT", l, g, s4)])
                    n += 1
            allm = [("modsT", l, g, s4) for l in range(self.depth) for g in range(2) for s4 in range(4)]
            m2 = self.mods2
            dl = slice(0, self.depth)
            P.op("dve", lambda e: e.tensor_scalar_add(out=m2[:, dl, :, 0:8], in0=self.modsT[:, dl, :, 8:16], scalar1=1.0),
                 reads=allm, writes=["mods2a"])
            P.op("dve", lambda e: e.tensor_scalar_mul(out=m2[:, dl, :, 8:16], in0=self.modsT[:, dl, :, 16:24], scalar1=1.0 / ALPHA),
                 reads=allm, writes=["mods2b"])
            P.op("dve", lambda e: e.tensor_scalar_add(out=m2[:, dl, :, 16:24], in0=self.modsT[:, dl, :, 32:40], scalar1=1.0),
                 reads=allm, writes=["mods2c"])
            P.op("dve", lambda e: e.tensor_scalar_mul(out=m2[:, dl, :, 24:32], in0=self.modsT[:, dl, :, 40:48], scalar1=1.0 / ALPHA),
                 reads=allm, writes=["mods2d"])
            P.barrier()

    def xT_tile(self, gt0):
        return self.xT.rearrange("(kc p) t -> p kc t", p=128)[:, :, gt0:gt0 + TT]

    def mixT_tile(self, gt0):
        return self.mixT.rearrange("(kc p) t -> p kc t", p=128)[:, :, gt0:gt0 + TT]

    def phase_prepass(self):
        from contextlib import ExitStack
        P, I = self.P, self.I
        with ExitStack() as st:
            xin = self.ring(st, "xin", 2, [128, 4, D])
            xst = self.ring(st, "xst", 2, [128, 8, TT])
            n = 0
            for g in self.groups:
                src = I["xs"] if g.name == "s" else I["xp"]
                for t in range(g.ntile):
                    xi, xk = xin[n % 2], ("xin", n % 2)
                    xo, ok = xst[n % 2], ("xst", n % 2)
                    P.dma("sp", xi[:], src[t * TT:(t + 1) * TT, :].rearrange("(b p) d -> p b d", p=128), writes=[xk])
                    for kc in range(8):
                        pt, pk = self.next_ps()
                        P.op("pe", [lambda e, b=b, kc=kc, xi=xi, pt=pt: e.transpose(
                            pt[:, b * 128:(b + 1) * 128], xi[:, b, kc * 128:(kc + 1) * 128], self.ident[:]) for b in range(4)],
                            reads=[xk, "ident"], writes=[pk])
                        if kc % 2 == 0:
                            P.op("act", lambda e, kc=kc, xo=xo, pt=pt: e.copy(out=xo[:, kc, :], in_=pt[:]),
                                 reads=[pk], writes=[ok + (kc,)])
                        else:
                            P.op("dve", lambda e, kc=kc, xo=xo, pt=pt: e.tensor_copy(out=xo[:, kc, :], in_=pt[:]),
                                 reads=[pk], writes=[ok + (kc,)])
                    P.dma("sp", self.xT_tile(g.tok0 + t * TT), xo[:], reads=[ok + (kc,) for kc in range(8)],
                          writes=[("xT", g.tok0 + t * TT)])
                    n += 1
            P.barrier()

    def load_h(self, st, l, g, colmajor=False):
        P = self.P
        gi = 0 if g.name == "s" else 1
        from contextlib import ExitStack
        hT = self.sb(st, "hT", [128, 8, g.ntok], BF16)
        xst_ = ExitStack()
        xr = self.ring(xst_, "xr", 2, [128, 8, TT])
        for t in range(g.ntile):
            xt, xk = xr[t % 2], ("xr", t % 2)
            P.dma("sp", xt[:], self.xT_tile(g.tok0 + t * TT), reads=[("xT", g.tok0 + t * TT)], writes=[xk])
            for kc in range(8):
                if colmajor:
                    out = hT[:, kc, :].rearrange("p (c r) -> p r c", r=64)[:, t * 8:(t + 1) * 8, :]
                    in_ = xt[:, kc, :].rearrange("p (r c) -> p r c", c=64)
                else:
                    out = hT[:, kc, t * TT:(t + 1) * TT]
                    in_ = xt[:, kc, :]
                P.op("act", lambda e, out=out, in_=in_, kc=kc: e.activation(
                    out=out, in_=in_, func=AF.Identity,
                    scale=self.mods2[:, l, gi, kc:kc + 1], bias=self.modsT[:, l, gi, kc:kc + 1]),
                    reads=[xk], writes=[("hT", kc, t) if not colmajor else ("hT", kc, "all")])
        P.barrier()
        xst_.close()
        return hT

    def w_chunk(self, wt, key, src_cols):
        self.P.dma("pool", wt[:], src_cols.rearrange("(kc p) n -> p kc n", p=128), writes=[key])

    def mm_group(self, pt, pk, wt, wkey, rhs_fn, rkeys, nk=8):
        self.P.op("pe", [lambda e, kc=kc, rhs=rhs_fn(kc): e.matmul(pt, lhsT=wt[:, kc, :], rhs=rhs, start=(kc == 0), stop=(kc == nk - 1))
                         for kc in range(nk)], reads=[wkey] + rkeys, writes=[pk])

    def phase_ab(self, l, g):
        from contextlib import ExitStack
        P, I = self.P, self.I
        j = l // 2
        nl, ll = TT // g.line, g.line
        segs = [(s * g.L, (s + 1) * g.L) for s in range(TT // g.L)] if g.L < TT else None

        def lines(ap):
            return ap.rearrange("p (a b) -> p a b", b=ll)

        with ExitStack() as st:
            hT = self.load_h(st, l, g)
            hkeys = lambda t: [("hT", kc, t) for kc in range(8)]
            scw = self.sb(st, "scw", [128, 3, 4])
            lcw = self.sb(st, "lcw", [128, 4, 4])
            lcb = self.sb(st, "lcb", [128, 4])
            lba = self.sb(st, "lba", [128, 2, 4])
            lbi = self.sb(st, "lbi", [128, 2, 4])
            lam = self.sb(st, "lam", [128, 2, 4])
            h0 = self.sb(st, "h0", [128, 2, 4])
            for tl, nm, pat in ((scw, "sc_conv", "k (cc p) -> p k cc"), (lcw, "lru_conv", "k (cc p) -> p k cc"),
                                (lba, "lru_ba", "k (cc p) -> p k cc"), (lbi, "lru_bi", "k (cc p) -> p k cc"),
                                (lam, "lru_lambda", "k (cc p) -> p k cc")):
                P.dma("sp", tl[:], I[nm][j].rearrange(pat, p=128), writes=[nm], allow_slow_non_contiguous=True)
            P.dma("sp", lcb[:], I["lru_conv_b"][j].rearrange("(cc p) -> p cc", p=128), writes=["lcb"], allow_slow_non_contiguous=True)
            if g.name == "s":
                P.dma("sp", h0[:], I["st_lru"][j].rearrange("k (cc p) -> p k cc", p=128), writes=["h0"], allow_slow_non_contiguous=True)
            clam = self.sb(st, "clam", [128, 2, 4])
            P.op("act", lambda e: e.activation(out=clam[:], in_=lam[:], func=AF.Exp, scale=-1.0), reads=["lru_lambda"], writes=["clam"])
            P.op("act", lambda e: e.activation(out=clam[:], in_=clam[:], func=AF.Ln, bias=1.0), reads=["clam"], writes=["clam"])
            P.op("dve", lambda e: e.tensor_scalar_mul(out=clam[:], in0=clam[:], scalar1=-8.0), reads=["clam"], writes=["clam"])
            wg = self.sb(st, "wg", [128, 2, 2, 4, 128], BF16)
            P.op("dve", lambda e: e.memset(wg[:], 0.0), writes=["wg"])
            for gate, nm in enumerate(("lru_wa", "lru_wi")):
                for d in range(2):
                    for par in range(2):
                        P.dma("pool", wg[par * 64:(par + 1) * 64, d, gate, :, par * 64:(par + 1) * 64],
                              I[nm][j, d].rearrange("(bc par) i o -> par i bc o", par=2)[par],
                              reads=[], writes=["wg"])
            wr = self.ring(st, "wab", 6, [128, 8, 128], BF16)
            self.wn = getattr(self, "wn", 0)

            def getw(col0):
                wt, wk = wr[self.wn % 6], ("wab", self.wn % 6)
                self.wn += 1
                self.w_chunk(wt, wk, I["ab_w_in"][j, :, col0:col0 + 128])
                return wt, wk

            t32 = self.ring(st, "t32", 6, [128, TT])
            self.tn = 0

            def tmp():
                tl, tk = t32[self.tn % 6], ("t32", self.tn % 6)
                self.tn += 1
                return tl, tk
            mst = self.ring(st, "mst", 2, [128, TT], BF16)
            self.mn = 0

            for cc in range(4):
                wb_, wc_, wv_ = getw(cc * 128), getw(512 + cc * 128), getw(1024 + cc * 128)
                for t in range(g.ntile):
                    hs = lambda kc, t=t: hT[:, kc, t * TT:(t + 1) * TT]
                    pb, pbk = self.next_ps(); pc, pck = self.next_ps(); pv, pvk = self.next_ps()
                    self.mm_group(pb[:], pbk, wb_[0], wb_[1], hs, hkeys(t))
                    self.mm_group(pc[:], pck, wc_[0], wc_[1], hs, hkeys(t))
                    self.mm_group(pv[:], pvk, wv_[0], wv_[1], hs, hkeys(t))
                    sc, sck = tmp(); pp, ppk = tmp(); ac, ack = tmp()
                    P.op("act", lambda e, sc=sc, pc=pc: e.copy(out=sc[:], in_=pc[:]), reads=[pck], writes=[sck])
                    P.op("dve", lambda e, pp=pp, pv=pv, sc=sc: e.tensor_tensor(out=pp[:], in0=pv[:], in1=sc[:], op=ALU.mult),
                         reads=[pvk, sck], writes=[ppk])
                    P.op("dve", lambda e, ac=ac, pp=pp, cc=cc: e.tensor_scalar_mul(out=ac[:], in0=pp[:], scalar1=scw[:, 1, cc:cc + 1]),
                         reads=[ppk, "sc_conv"], writes=[ack])
                    P.op("dve", lambda e, ac=ac, pp=pp, cc=cc: e.scalar_tensor_tensor(
                        out=lines(ac[:])[:, :, 1:], in0=lines(pp[:])[:, :, :ll - 1], scalar=scw[:, 0, cc:cc + 1],
                        in1=lines(ac[:])[:, :, 1:], op0=ALU.mult, op1=ALU.add), reads=[ppk, ack], writes=[ack])
                    P.op("dve", lambda e, ac=ac, pp=pp, cc=cc: e.scalar_tensor_tensor(
                        out=lines(ac[:])[:, :, :ll - 1], in0=lines(pp[:])[:, :, 1:], scalar=scw[:, 2, cc:cc + 1],
                        in1=lines(ac[:])[:, :, :ll - 1], op0=ALU.mult, op1=ALU.add), reads=[ppk, ack], writes=[ack])
                    ms, msk = mst[self.mn % 2], ("mst", self.mn % 2); self.mn += 1
                    P.op("dve", lambda e, ms=ms, pb=pb, ac=ac: e.tensor_tensor(out=ms[:], in0=pb[:], in1=ac[:], op=ALU.mult),
                         reads=[pbk, ack], writes=[msk])
                    gt0 = g.tok0 + t * TT
                    P.dma("sp", self.mixT[cc * 128:(cc + 1) * 128, gt0:gt0 + TT], ms[:], reads=[msk], writes=[("mixT", cc, gt0)])

            XC = self.sb(st, "XC", [128, g.ntok])
            XCB = self.sb(st, "XCB", [128, g.ntok], BF16)
            GG = self.sb(st, "GG", [128, g.ntok], BF16)
            HF = self.sb(st, "HF", [128, g.ntok])
            hbr = self.ring(st, "hb", 2, [128, TT])
            for bc in range(4):
                wg_, wx_ = getw(1536 + bc * 128), getw(2048 + bc * 128)

                def gates(d, t, bc=bc):
                    sl_ = slice(t * TT, (t + 1) * TT)
                    pr, prk = self.next_ps(); pi, pik = self.next_ps()
                    P.op("pe", lambda e: e.matmul(pr[:], lhsT=wg[:, d, 0, bc, :], rhs=XCB[:, sl_], start=True, stop=True),
                         reads=["wg", ("XCB", t)], writes=[prk])
                    P.op("pe", lambda e: e.matmul(pi[:], lhsT=wg[:, d, 1, bc, :], rhs=XCB[:, sl_], start=True, stop=True),
                         reads=["wg", ("XCB", t)], writes=[pik])
                    r_, rk = tmp(); a_, ak = tmp(); i_, ik = tmp(); s_, sk = tmp(); u_, uk = tmp()
                    P.op("act", lambda e: e.activation(out=r_[:], in_=pr[:], func=AF.Sigmoid, bias=lba[:, d, bc:bc + 1]),
                         reads=[prk, "lru_ba"], writes=[rk])
                    P.op("act", lambda e: e.activation(out=i_[:], in_=pi[:], func=AF.Sigmoid, bias=lbi[:, d, bc:bc + 1]),
                         reads=[pik, "lru_bi"], writes=[ik])
                    P.op("act", lambda e: e.activation(out=a_[:], in_=r_[:], func=AF.Exp, scale=clam[:, d, bc:bc + 1]),
                         reads=[rk, "clam"], writes=[ak])
                    P.op("act", lambda e: e.activation(out=s_[:], in_=a_[:], func=AF.Square), reads=[ak], writes=[sk])
                    P.op("act", lambda e: e.activation(out=s_[:], in_=s_[:], func=AF.Sqrt, scale=-1.0, bias=1.0), reads=[sk], writes=[sk])
                    P.op("dve", lambda e: e.tensor_tensor(out=u_[:], in0=i_[:], in1=XC[:, sl_], op=ALU.mult),
                         reads=[ik, ("XC", t)], writes=[uk])
                    P.op("dve", lambda e: e.tensor_tensor(out=u_[:], in0=u_[:], in1=s_[:], op=ALU.mult), reads=[uk, sk], writes=[uk])
                    return a_, ak, u_, uk

                for t in range(g.ntile):
                    sl_ = slice(t * TT, (t + 1) * TT)
                    hs = lambda kc, t=t: hT[:, kc, t * TT:(t + 1) * TT]
                    pg, pgk = self.next_ps(); px, pxk = self.next_ps()
                    self.mm_group(pg[:], pgk, wg_[0], wg_[1], hs, hkeys(t))
                    self.mm_group(px[:], pxk, wx_[0], wx_[1], hs, hkeys(t))
                    P.op("act", lambda e, pg=pg, sl_=sl_: e.activation(out=GG[:, sl_], in_=pg[:], func=AF.Gelu), reads=[pgk], writes=[("GG", t)])
                    xl, xlk = tmp()
                    P.op("act", lambda e, px=px, xl=xl: e.copy(out=xl[:], in_=px[:]), reads=[pxk], writes=[xlk])
                    xck = ("XC", t)
                    P.op("dve", lambda e, xl=xl, sl_=sl_, bc=bc: e.tensor_scalar(
                        out=XC[:, sl_], in0=xl[:], scalar1=lcw[:, 2, bc:bc + 1], scalar2=lcb[:, bc:bc + 1], op0=ALU.mult, op1=ALU.add),
                        reads=[xlk, "lru_conv", "lcb"], writes=[xck])
                    for tap, off in ((0, -2), (1, -1), (3, 1)):
                        if off < 0:
                            o_ = lines(XC[:, sl_])[:, :, -off:]; i_ = lines(xl[:])[:, :, :ll + off]
                        else:
                            o_ = lines(XC[:, sl_])[:, :, :ll - off]; i_ = lines(xl[:])[:, :, off:]
                        P.op("dve", lambda e, o_=o_, i_=i_, tap=tap, bc=bc: e.scalar_tensor_tensor(
                            out=o_, in0=i_, scalar=lcw[:, tap, bc:bc + 1], in1=o_, op0=ALU.mult, op1=ALU.add),
                            reads=[xlk, xck], writes=[xck])
                    P.op("act", lambda e, sl_=sl_: e.copy(out=XCB[:, sl_], in_=XC[:, sl_]), reads=[xck], writes=[("XCB", t)])
                    a_, ak, u_, uk = gates(0, t)
                    for (s0, s1) in (segs or [(0, TT)]):
                        if g.name == "s":
                            init = h0[:, 0, bc:bc + 1] if t == 0 else HF[:, t * TT - 1:t * TT]
                            rk_ = ["h0"] if t == 0 else [("HF", t - 1)]
                        else:
                            init, rk_ = 0.0, []
                        P.op("dve", lambda e, s0=s0, s1=s1, init=init, a_=a_, u_=u_, t=t: e.tensor_tensor_scan(
                            out=HF[:, t * TT + s0:t * TT + s1], data0=a_[:, s0:s1], data1=u_[:, s0:s1], initial=init,
                            op0=ALU.mult, op1=ALU.add), reads=[ak, uk] + rk_, writes=[("HF", t)])
                prev_hb = None
                for t in reversed(range(g.ntile)):
                    sl_ = slice(t * TT, (t + 1) * TT)
                    a_, ak, u_, uk = gates(1, t)
                    hb, hbk = hbr[t % 2], ("hb", t % 2)
                    for (s0, s1) in (segs or [(0, TT)]):
                        if g.name == "s":
                            init = h0[:, 1, bc:bc + 1] if t == g.ntile - 1 else prev_hb[0][:, 0:1]
                            rk_ = ["h0"] if t == g.ntile - 1 else [prev_hb[1]]
                        else:
                            init, rk_ = 0.0, []
                        P.op("dve", lambda e, s0=s0, s1=s1, init=init, a_=a_, u_=u_, hb=hb: e.tensor_tensor_scan(
                            out=hb[:, s0:s1][:, ::-1], data0=a_[:, s0:s1][:, ::-1], data1=u_[:, s0:s1][:, ::-1], initial=init,
                            op0=ALU.mult, op1=ALU.add), reads=[ak, uk] + rk_, writes=[hbk])
                    prev_hb = (hb, hbk)
                    if g.name == "p":
                        for si, (s0, s1) in enumerate(segs):
                            b = t * (TT // g.L) + si
                            P.dma("sp", self.O["nlru"][b, j, 0, bc * 128:(bc + 1) * 128].rearrange("(p o) -> p o", o=1),
                                  HF[:, t * TT + s1 - 1:t * TT + s1], reads=[("HF", t)], writes=[("nlru", b, j, 0, bc)])
                            P.dma("sp", self.O["nlru"][b, j, 1, bc * 128:(bc + 1) * 128].rearrange("(p o) -> p o", o=1),
                                  hb[:, s0:s0 + 1], reads=[hbk], writes=[("nlru", b, j, 1, bc)])
                    y_, yk = tmp()
                    P.op("dve", lambda e, y_=y_, hb=hb, sl_=sl_: e.tensor_tensor(out=y_[:], in0=HF[:, sl_], in1=hb[:], op=ALU.add),
                         reads=[("HF", t), hbk], writes=[yk])
                    ms, msk = mst[self.mn % 2], ("mst", self.mn % 2); self.mn += 1
                    P.op("dve", lambda e, ms=ms, y_=y_, sl_=sl_: e.tensor_tensor(out=ms[:], in0=y_[:], in1=GG[:, sl_], op=ALU.mult),
                         reads=[yk, ("GG", t)], writes=[msk])
                    gt0 = g.tok0 + t * TT
                    P.dma("sp", self.mixT[512 + bc * 128:512 + (bc + 1) * 128, gt0:gt0 + TT], ms[:], reads=[msk],
                          writes=[("mixT", 4 + bc, gt0)])
            P.barrier()


    def ln_tile(self, xt, xk, l, which, scr):
        P = self.P
        vb, sq, mean_sb, m2, rstd = scr
        keys = [xk + (kc,) for kc in range(8)]
        P.op("dve", lambda e: e.tensor_copy(out=vb[:], in_=xt[:]), reads=keys, writes=["ln_vb"])
        P.op("act", lambda e: e.activation(out=sq[:], in_=xt[:], func=AF.Square), reads=keys, writes=["ln_sq"])
        pm, pmk = self.next_ps(); pe2, pe2k = self.next_ps()
        P.op("pe", [lambda e, kc=kc: e.matmul(pm[:], lhsT=self.onesb[:], rhs=vb[:, kc, :], start=(kc == 0), stop=(kc == 7)) for kc in range(8)],
             reads=["onesb", "ln_vb"], writes=[pmk])
        P.op("pe", [lambda e, kc=kc: e.matmul(pe2[:], lhsT=self.onesb[:], rhs=sq[:, kc, :], start=(kc == 0), stop=(kc == 7)) for kc in range(8)],
             reads=["onesb", "ln_sq"], writes=[pe2k])
        P.op("act", lambda e: e.copy(out=mean_sb[:], in_=pm[:]), reads=[pmk], writes=["ln_mean"])
        P.op("dve", lambda e: e.tensor_tensor(out=m2[:], in0=mean_sb[:], in1=mean_sb[:], op=ALU.mult), reads=["ln_mean"], writes=["ln_m2"])
        P.op("dve", lambda e: e.tensor_tensor(out=m2[:], in0=pe2[:], in1=m2[:], op=ALU.subtract), reads=[pe2k, "ln_m2"], writes=["ln_m2"])
        P.op("act", lambda e: e.activation(out=rstd[:], in_=m2[:], func=AF.Sqrt, bias=self.epsT[:, 0:1]), reads=["ln_m2", "epsT"], writes=["ln_rstd"])
        P.op("dve", lambda e: e.reciprocal(out=rstd[:], in_=rstd[:]), reads=["ln_rstd"], writes=["ln_rstd"])
        bc_ = lambda ap: ap[:].rearrange("p (o t) -> p o t", o=1).to_broadcast([128, 8, TT])
        P.op("dve", lambda e: e.tensor_tensor(out=xt[:], in0=xt[:], in1=bc_(mean_sb), op=ALU.subtract), reads=keys + ["ln_mean"], writes=keys)
        P.op("dve", lambda e: e.tensor_tensor(out=xt[:], in0=xt[:], in1=bc_(rstd), op=ALU.mult), reads=keys + ["ln_rstd"], writes=keys)
        gi, bi = (0, 1) if which == 1 else (2, 3)
        for kc in range(8):
            P.op("dve", lambda e, kc=kc: e.tensor_scalar(out=xt[:, kc, :], in0=xt[:, kc, :], scalar1=self.lnp[:, l, gi, kc:kc + 1],
                                                        scalar2=self.lnp[:, l, bi, kc:kc + 1], op0=ALU.mult, op1=ALU.add),
                 reads=[keys[kc], "lnp"], writes=[keys[kc]])

    def phase_dense(self, l, last):
        from contextlib import ExitStack
        P, I, O = self.P, self.I, self.O
        with ExitStack() as st:
            xr = self.ring(st, "dx", 2, [128, 8, TT])
            mr = self.ring(st, "dm", 2, [128, 8, TT], BF16)
            h2 = self.sb(st, "h2", [128, 8, TT], BF16)
            hid = self.sb(st, "hid", [128, 32, TT], BF16)
            rt = self.ring(st, "rt", 3, [128, TT], BF16)
            scr = (self.sb(st, "ln_vb", [128, 8, TT], BF16), self.sb(st, "ln_sq", [128, 8, TT], BF16),
                   self.sb(st, "ln_mean", [128, TT]), self.sb(st, "ln_m2", [128, TT]), self.sb(st, "ln_rstd", [128, TT]))
            ws = self.ring(st, "wsm", 6, [128, 8, 128], BF16)
            w2r = self.ring(st, "w2r", 3, [128, 32, 128], BF16)
            ot = self.ring(st, "ot", 2, [128, D]) if last else None
            tiles = [(g, t) for g in self.groups for t in range(g.ntile)]
            cnt = {'wn': 0, 'w2n': 0, 'on': 0}

            def dense_tile(n, g, t):
                gi = 0 if g.name == "s" else 1
                gt0 = g.tok0 + t * TT
                xt, xk = xr[n % 2], ("dx", n % 2)
                mt, mk = mr[n % 2], ("dm", n % 2)
                xkeys = [xk + (kc,) for kc in range(8)]
                P.dma("sp", xt[:], self.xT_tile(gt0), reads=[("xT", gt0)], writes=xkeys)
                P.dma("sp", mt[:], self.mixT_tile(gt0), reads=[("mixT", c, gt0) for c in range(8)], writes=[mk])
                for m in range(8):
                    wt, wk = ws[cnt['wn'] % 6], ("wsm", cnt['wn'] % 6); cnt['wn'] += 1
                    self.w_chunk(wt, wk, I["w_out"][l, :, m * 128:(m + 1) * 128])
                    pt, pk = self.next_ps()
                    self.mm_group(pt[:], pk, wt, wk, lambda kc: mt[:, kc, :], [mk])
                    P.op("dve", lambda e, m=m, pt=pt: e.scalar_tensor_tensor(
                        out=xt[:, m, :], in0=pt[:], scalar=self.mods2[:, l, gi, 8 + m:9 + m], in1=xt[:, m, :],
                        op0=ALU.mult, op1=ALU.add), reads=[pk, xkeys[m], "mods2b"], writes=[xkeys[m]])
                if n == 0 and l == 0:
                    self.dump('mods2', self.mods2[:], ['mods2a', 'mods2b', 'mods2c', 'mods2d'])
                    self.dump('modsT', self.modsT[:], [])
                    self.dump('v1', xt[:], xkeys)
                    self.dump('mt', mt[:], [mk], BF16)
                if n == 0 and l > 0:
                    self.dump('mt_l%d' % l, mt[:], [mk], BF16)
                self.ln_tile(xt, xk, l, 1, scr)
                if n == 0 and l == 0:
                    self.dump('x1', xt[:], xkeys)
                for kc in range(8):
                    P.op("act", lambda e, kc=kc: e.activation(out=h2[:, kc, :], in_=xt[:, kc, :], func=AF.Identity,
                                                              scale=self.mods2[:, l, gi, 16 + kc:17 + kc],
                                                              bias=self.modsT[:, l, gi, 24 + kc:25 + kc]),
                         reads=[xkeys[kc]], writes=[("h2", kc)])
                h2k = [("h2", kc) for kc in range(8)]
                for hc in range(32):
                    wt, wk = ws[cnt['wn'] % 6], ("wsm", cnt['wn'] % 6); cnt['wn'] += 1
                    self.w_chunk(wt, wk, I["mlp_w1"][l, :, hc * 128:(hc + 1) * 128])
                    pt, pk = self.next_ps()
                    self.mm_group(pt[:], pk, wt, wk, lambda kc: h2[:, kc, :], h2k)
                    r_, rk = rt[hc % 3], ("rt", hc % 3)
                    P.op("act", lambda e, r_=r_, pt=pt: e.activation(out=r_[:], in_=pt[:], func=AF.Relu), reads=[pk], writes=[rk])
                    P.op("dve", lambda e, r_=r_, hc=hc, pt=pt: e.tensor_tensor(out=hid[:, hc, :], in0=pt[:], in1=r_[:], op=ALU.mult),
                         reads=[rk, pk], writes=[("hid", hc)])
                hidk = [("hid", hc) for hc in range(32)]
                for m in range(8):
                    wt, wk = w2r[cnt['w2n'] % 3], ("w2r", cnt['w2n'] % 3); cnt['w2n'] += 1
                    P.dma("pool", wt[:], I["mlp_w2"][l, :, m * 128:(m + 1) * 128].rearrange("(kc p) n -> p kc n", p=128), writes=[wk])
                    pt, pk = self.next_ps()
                    self.mm_group(pt[:], pk, wt, wk, lambda kc: hid[:, kc, :], hidk, nk=32)
                    P.op("dve", lambda e, m=m, pt=pt: e.scalar_tensor_tensor(
                        out=xt[:, m, :], in0=pt[:], scalar=self.mods2[:, l, gi, 24 + m:25 + m], in1=xt[:, m, :],
                        op0=ALU.mult, op1=ALU.add), reads=[pk, xkeys[m], "mods2d"], writes=[xkeys[m]])
                if n == 0 and l == 0:
                    self.dump('v2', xt[:], xkeys)
                    self.dump('hid', hid[:], hidk, BF16)
                self.ln_tile(xt, xk, l, 2, scr)
                if n == 0 and l == 0:
                    self.dump('x2', xt[:], xkeys)
                if not last:
                    P.dma("sp", self.xT_tile(gt0), xt[:], reads=xkeys, writes=[("xT", gt0)])
                else:
                    dst = O["ys"] if g.name == "s" else O["yp"]
                    for b in range(4):
                        o_, ok = ot[cnt['on'] % 2], ("ot", cnt['on'] % 2); cnt['on'] += 1
                        for half in range(2):
                            pt, pk = self.next_ps()
                            P.op("pe", [lambda e, q=q, pt=pt, b=b, half=half: e.transpose(
                                pt[:, q * 128:(q + 1) * 128], xt[:, half * 4 + q, b * 128:(b + 1) * 128], self.ident[:]) for q in range(4)],
                                reads=xkeys[half * 4:half * 4 + 4] + ["ident"], writes=[pk])
                            if half == 0:
                                P.op("act", lambda e, o_=o_, pt=pt: e.copy(out=o_[:, 0:512], in_=pt[:]), reads=[pk], writes=[ok + (0,)])
                            else:
                                P.op("dve", lambda e, o_=o_, pt=pt: e.tensor_copy(out=o_[:, 512:1024], in_=pt[:]), reads=[pk], writes=[ok + (1,)])
                        r0 = t * TT + b * 128
                        P.dma("sp", dst[r0:r0 + 128, :], o_[:], reads=[ok + (0,), ok + (1,)], writes=[("out", g.name, r0)])

            for n, (g, t) in enumerate(tiles):
                dense_tile(n, g, t)
            P.barrier()

    def phase_cd(self, l, g):
        from contextlib import ExitStack
        P, I = self.P, self.I
        j = l // 2
        with ExitStack() as st:
            nblk = g.L // 128
            with ExitStack() as st2:
                rw = self.rwkv_setup(st2, l, g)
                hT = self.load_h(st2, l, g, colmajor=(g.name == "s"))
                with ExitStack() as st3:
                    self.cd_hyena_proj(st3, l, g, hT)
                    P.barrier()
                if "norwkv" not in self.flags:
                    self.rwkv_partA(st2, l, g, hT, rw)
                P.barrier()
                if l == 1 and g.name == "p" and self.dbg:
                    for hp_ in range(4):
                        self.dump("rwS%d" % hp_, self.rwS[hp_, :, :, 4096:5120], [])
            if "norwkv" not in self.flags:
                with ExitStack() as st4:
                    rw = self.rwkv_setup(st4, l, g)
                    self.rwkv_partB(st4, l, g, rw)
                    P.barrier()
            if "nohyconv" not in self.flags:
                with ExitStack() as st5:
                    self.cd_hyena_conv(st5, l, g)
                    P.barrier()

    def hkeys(self, g, t):
        if g.name == "s":
            return [("hT", kc, "all") for kc in range(8)]
        return [("hT", kc, t) for kc in range(8)]

    def to_tok(self, src_bf, skey, dst, dkeys_fn, nb):
        P = self.P
        pt, pk = self.next_ps()
        pv = pt[:].bitcast(BF16)
        P.op("pe", [lambda e, b=b: e.transpose(pv[:, b * 128:(b + 1) * 128], src_bf[:, b * 128:(b + 1) * 128], self.identb[:]) for b in range(nb)],
             reads=[skey, "identb"], writes=[pk])
        P.op("act", lambda e: e.copy(out=dst, in_=pv[:, 0:nb * 128].rearrange("p (b c) -> p b c", c=128)), reads=[pk], writes=dkeys_fn)

    def cd_hyena_proj(self, st, l, g, hT):
        P, I = self.P, self.I
        j = l // 2
        nl, ll = TT // g.line, g.line
        lines = lambda ap: ap.rearrange("p (a b) -> p a b", b=ll)
        hyw = self.sb(st, "hyw", [128, 3, 12])
        P.dma("sp", hyw[:], I["hy_conv"][j].rearrange("k (q p) -> p k q", p=128), writes=["hyw"], allow_slow_non_contiguous=True)
        wr = self.ring(st, "whp", 3, [128, 8, 128], BF16)
        xl_r = self.ring(st, "hxl", 2, [128, TT])
        u_r = self.ring(st, "hu", 2, [128, TT])
        ub_r = self.ring(st, "hub", 2, [128, TT], BF16)
        tokr = self.ring(st, "tokst", 2, [128, 4, 128], BF16)
        cnt = {"n": 0}
        nblk = g.L // 128

        def tile(q, t, wt, wk):
            which, cc = q // 4, q % 4
            n = cnt["n"]; cnt["n"] += 1
            pt, pk = self.next_ps()
            self.mm_group(pt[:], pk, wt, wk, lambda kc: hT[:, kc, t * TT:(t + 1) * TT], self.hkeys(g, t))
            xl, xlk = xl_r[n % 2], ("hxl", n % 2)
            u, uk = u_r[n % 2], ("hu", n % 2)
            P.op("act", lambda e: e.copy(out=xl[:], in_=pt[:]), reads=[pk], writes=[xlk])
            P.op("dve", lambda e: e.tensor_scalar_mul(out=u[:], in0=xl[:], scalar1=hyw[:, 1, q:q + 1]), reads=[xlk, "hyw"], writes=[uk])
            P.op("dve", lambda e: e.scalar_tensor_tensor(out=lines(u[:])[:, :, 1:], in0=lines(xl[:])[:, :, :ll - 1], scalar=hyw[:, 0, q:q + 1],
                                                         in1=lines(u[:])[:, :, 1:], op0=ALU.mult, op1=ALU.add), reads=[xlk, uk], writes=[uk])
            P.op("dve", lambda e: e.scalar_tensor_tensor(out=lines(u[:])[:, :, :ll - 1], in0=lines(xl[:])[:, :, 1:], scalar=hyw[:, 2, q:q + 1],
                                                         in1=lines(u[:])[:, :, :ll - 1], op0=ALU.mult, op1=ALU.add), reads=[xlk, uk], writes=[uk])
            gt0 = g.tok0 + t * TT
            P.dma("sp", self.hyS[which, cc * 128:(cc + 1) * 128, gt0:gt0 + TT], u[:], reads=[uk], writes=[("hyS", which, cc, gt0)])
            if which == 0:
                ub, ubk = ub_r[n % 2], ("hub", n % 2)
                P.op("act", lambda e: e.copy(out=ub[:], in_=u[:]), reads=[uk], writes=[ubk])
                tk_, tkk = tokr[n % 2], ("tokst", n % 2)
                self.to_tok(ub, ubk, tk_[:], [tkk], 4)
                gb0 = g.tok0 // 128 + t * 4
                P.dma("sp", self.tokS[:, gb0:gb0 + 4, cc * 128:(cc + 1) * 128], tk_[:], reads=[tkk], writes=[("tokS", gb0, cc)])

        for q in range(12):
            wt, wk = wr[q % 3], ("whp", q % 3)
            self.w_chunk(wt, wk, I["cd_w_in"][j, :, q * 128:(q + 1) * 128])
            for t in range(g.ntile):
                tile(q, t, wt, wk)

    def cd_hyena_conv(self, st, l, g):
        from contextlib import ExitStack
        P, I = self.P, self.I
        j = l // 2
        L, nseq = g.L, g.nseq
        nblk = L // 128
        N2 = 2 * L
        sfx = "4096" if L == 4096 else "256"
        TI = min(512, L)
        ntt = L // TI
        KG = min(8, nblk)
        HW = 512
        kb0 = 0 if g.name == "s" else 32
        hbias = self.sb(st, "hbias", [128, 2, 4])
        P.dma("sp", hbias[:], I["hy_bias"][j].rearrange("o (cc p) -> p o cc", p=128), writes=["hbias"], allow_slow_non_contiguous=True)
        eps6 = self.sb(st, "eps6", [128, 1])
        P.op("dve", lambda e: e.memset(eps6[:], 1e-6), writes=["eps6"])
        ones32 = self.sb(st, "ones32", [128, 128])
        P.op("dve", lambda e: e.memset(ones32[:], 1.0), writes=["ones32"])
        tcn = {"n": 0}

        def mk_table(tb):
            def table(src_ap, shape):
                n = tcn["n"]; tcn["n"] += 1
                tl, tk = tb[n % len(tb)], ("dft", n % len(tb))
                v = tl[:, 0:shape[0] * shape[1]].rearrange("p (a b) -> p a b", b=shape[1])
                P.dma("sp", v, src_ap, writes=[tk])
                return v, tk
            return table

        with ExitStack() as sF:
            GSD = self.sb(sF, "GSD", [128, 2 * nblk * HW], BF16)
            GS = GSD[:, 0:nblk * HW].rearrange("p (b c) -> p b c", c=HW)
            GD = GSD[:, nblk * HW:2 * nblk * HW].rearrange("p (b c) -> p b c", c=HW)
            rnb = self.sb(sF, "rnb", [128, HW])
            def gen_filters(o, half):
                c0 = half * HW
                with ExitStack() as s2:
                    fT = self.sb(s2, "fT", [33, L])
                    P.dma("sp", fT[:], I["featT" + sfx][:, :], writes=["fT"])
                    prm = self.sb(s2, "hyprm", [64, 6])
                    P.dma("sp", prm[:, 0:1], I["hy_freq"][j].rearrange("(p o) -> p o", o=1), writes=["hyprm0"])
                    P.dma("sp", prm[:, 1:2], I["hy_b1"][j].rearrange("(p o) -> p o", o=1), writes=["hyprm1"])
                    P.dma("sp", prm[:, 2:3], I["hy_b2"][j].rearrange("(p o) -> p o", o=1), writes=["hyprm2"])
                    P.op("dve", lambda e: e.tensor_tensor(out=prm[:, 3:5], in0=prm[:, 1:3], in1=prm[:, 0:1].to_broadcast([64, 2]), op=ALU.mult),
                         reads=["hyprm0", "hyprm1", "hyprm2"], writes=["hyprm3"])
                    w1 = self.sb(s2, "hw1", [33, 64]); w2 = self.sb(s2, "hw2", [64, 64]); w3 = self.sb(s2, "hw3", [64, 2048])
                    P.dma("sp", w1[:], I["hy_w1"][j], writes=["hw1"]); P.dma("sp", w2[:], I["hy_w2"][j], writes=["hw2"])
                    P.dma("sp", w3[:], I["hy_w3"][j], writes=["hw3"])
                    ntn = self.sb(s2, "ntn", [128, nblk]); dlt = self.sb(s2, "dlt", [128, 512]); m1 = self.sb(s2, "m1c", [128, 1])
                    P.dma("sp", ntn[:], I["ntn" + sfx][:, :], writes=["ntn"]); P.dma("sp", dlt[:], I["delta_b"][:, :], writes=["dlt"])
                    P.dma("sp", m1[:], I["m1col"][:, :], writes=["m1c"])
                    H1 = self.sb(s2, "H1", [64, L]); H2 = self.sb(s2, "H2", [64, L])
                    tmp = self.ring(s2, "ftmp", 4, [128, 512])
                    hw_ = slice(0, HW)
                    MAGIC = 12582912.0
                    TWO_PI = float(2.0 * np.pi)

                    def sin_layer(dst, wmat, kdim, src, bcol, nm):
                        for c0 in range(0, L, 512):
                            cw = min(512, L - c0)
                            pt, pk = self.next_ps()
                            P.op("pe", lambda e: e.matmul(pt[0:64, 0:cw], lhsT=wmat[:], rhs=src[0:kdim, c0:c0 + cw], start=True, stop=True),
                                 reads=[nm[0], nm[1]], writes=[pk])
                            a, b = tmp[0], tmp[1]
                            P.op("act", lambda e: e.activation(out=a[0:64, 0:cw], in_=pt[0:64, 0:cw], func=AF.Identity,
                                                               scale=prm[:, 0:1], bias=prm[:, bcol:bcol + 1]), reads=[pk, "hyprm3", "hyprm0"], writes=["ft0"])
                            P.op("dve", lambda e: e.tensor_scalar(out=b[0:64, 0:cw], in0=a[0:64, 0:cw], scalar1=1.0 / TWO_PI, scalar2=MAGIC,
                                                                  op0=ALU.mult, op1=ALU.add), reads=["ft0"], writes=["ft1"])
                            P.op("dve", lambda e: e.tensor_scalar_add(out=b[0:64, 0:cw], in0=b[0:64, 0:cw], scalar1=-MAGIC), reads=["ft1"], writes=["ft1"])
                            P.op("dve", lambda e: e.scalar_tensor_tensor(out=a[0:64, 0:cw], in0=b[0:64, 0:cw], scalar=-TWO_PI, in1=a[0:64, 0:cw],
                                                                         op0=ALU.mult, op1=ALU.add), reads=["ft0", "ft1"], writes=["ft0"])
                            P.op("act", lambda e: e.activation(out=dst[:, c0:c0 + cw], in_=a[0:64, 0:cw], func=AF.Sin), reads=["ft0"], writes=[nm[2]])

                    sin_layer(H1, w1, 33, fT, 3, ("hw1", "fT", "H1"))
                    sin_layer(H2, w2, 64, H1, 4, ("hw2", "H1", "H2"))
                    acc, acck = self.ps[7], ("ps", 7)
                    for blk in range(nblk):
                        dec, g0, g1, sq = tmp[0], tmp[1], tmp[2], tmp[3]
                        P.op("act", lambda e, blk=blk: e.activation(out=dec[:, hw_], in_=dlt[:, c0:c0 + HW], func=AF.Exp, scale=ntn[:, blk:blk + 1]),
                             reads=["dlt", "ntn"], writes=["ft0"])
                        for dr, gt, gk in ((0, g0, "ft1"), (1, g1, "ft2")):
                            pt, pk = self.next_ps()
                            ch = o * 2 + dr
                            P.op("pe", lambda e, blk=blk, ch=ch, pt=pt: e.matmul(pt[:, hw_], lhsT=H2[:, blk * 128:(blk + 1) * 128],
                                                                               rhs=w3[:, ch * 512 + c0:ch * 512 + c0 + HW], start=True, stop=True),
                                 reads=["H2", "hw3"], writes=[pk])
                            P.op("dve", lambda e, gt=gt, pt=pt: e.tensor_tensor(out=gt[:, hw_], in0=pt[:, hw_], in1=dec[:, hw_], op=ALU.mult),
                                 reads=[pk, "ft0"], writes=[gk])
                        if blk == 0:
                            P.op("dve", lambda e: e.tensor_scalar_mul(out=g1[:, hw_], in0=g1[:, hw_], scalar1=m1[:, 0:1]), reads=["ft2", "m1c"], writes=["ft2"])
                        P.op("dve", lambda e: e.tensor_tensor(out=sq[:, hw_], in0=g0[:, hw_], in1=g0[:, hw_], op=ALU.mult), reads=["ft1"], writes=["ft3"])
                        P.op("pe", lambda e, blk=blk: e.matmul(acc[:, hw_], lhsT=ones32[:], rhs=sq[:, hw_], start=(blk == 0), stop=False),
                             reads=["ft3", "ones32"], writes=[acck])
                        P.op("dve", lambda e: e.tensor_tensor(out=sq[:, hw_], in0=g1[:, hw_], in1=g1[:, hw_], op=ALU.mult), reads=["ft2"], writes=["ft3"])
                        P.op("pe", lambda e, blk=blk: e.matmul(acc[:, hw_], lhsT=ones32[:], rhs=sq[:, hw_], start=False, stop=(blk == nblk - 1)),
                             reads=["ft3", "ones32"], writes=[acck])
                        P.op("dve", lambda e, blk=blk: e.tensor_tensor(out=GS[:, blk, :], in0=g0[:, hw_], in1=g1[:, hw_], op=ALU.add),
                             reads=["ft1", "ft2"], writes=[("GS", blk)])
                        P.op("dve", lambda e, blk=blk: e.tensor_tensor(out=GD[:, blk, :], in0=g0[:, hw_], in1=g1[:, hw_], op=ALU.subtract),
                             reads=["ft1", "ft2"], writes=[("GD", blk)])
                    P.op("act", lambda e: e.activation(out=rnb[:], in_=acc[:, hw_], func=AF.Sqrt, bias=eps6[:, 0:1]), reads=[acck, "eps6"], writes=["rnb"])
                    P.op("dve", lambda e: e.reciprocal(out=rnb[:], in_=rnb[:]), reads=["rnb"], writes=["rnb"])
                    P.op("dve", lambda e: e.tensor_scalar_mul(out=rnb[:], in0=rnb[:], scalar1=2.0 / N2), reads=["rnb"], writes=["rnb"])
                    P.barrier()


            for o in range(2):
                gen_filters(o, 0)
                with ExitStack() as s2:
                    tb = self.ring(s2, "dftF", 4, [128, KG * 512], BF16)
                    table = mk_table(tb)
                    fo_r = self.ring(s2, "fo", 2, [128, 2, 512])
                    for kb in range(nblk):
                        C, ck = table(I["TC" + sfx][kb], (nblk, 128))
                        S, sk = table(I["TS" + sfx][kb], (nblk, 128))
                        fr, frk = self.next_ps(); fi, fik = self.next_ps()
                        P.op("pe", [lambda e, sb_=sb_: e.matmul(fr[:], lhsT=C[:, sb_, :], rhs=GS[:, sb_, :], start=(sb_ == 0), stop=(sb_ == nblk - 1))
                                    for sb_ in range(nblk)], reads=[ck] + [("GS", b) for b in range(nblk)], writes=[frk])
                        P.op("pe", [lambda e, sb_=sb_: e.matmul(fi[:], lhsT=S[:, sb_, :], rhs=GD[:, sb_, :], start=(sb_ == 0), stop=(sb_ == nblk - 1))
                                    for sb_ in range(nblk)], reads=[sk] + [("GD", b) for b in range(nblk)], writes=[fik])
                        fo, fok = fo_r[kb % 2], ("fo", kb % 2)
                        P.op("dve", lambda e: e.tensor_tensor(out=fo[:, 0, :], in0=fr[:], in1=rnb[:], op=ALU.mult), reads=[frk, "rnb"], writes=[fok + (0,)])
                        P.op("dve", lambda e: e.tensor_tensor(out=fo[:, 1, :], in0=fi[:], in1=rnb[:], op=ALU.mult), reads=[fik, "rnb"], writes=[fok + (1,)])
                        P.dma("sp", self.FS[o, kb0 + kb], fo[:], reads=[fok + (0,), fok + (1,)], writes=[("FS", o, kb0 + kb)])
                    P.barrier()
            P.barrier()

        src_tok = self.sb(st, "srctok", [128, nseq * nblk, 512], BF16)
        P.dma("sp", src_tok[:], self.tokS[:, g.tok0 // 128:g.tok0 // 128 + nseq * nblk, :],
              writes=[("tok", b, cc) for b in range(nseq * nblk) for cc in range(4)])
        msbox = {}

        def conv(o, post):
            with ExitStack() as s2:
                Yr = self.sb(s2, "Yr", [128, nseq * nblk, HW], BF16)
                Yi = self.sb(s2, "Yi", [128, nseq * nblk, HW], BF16)
                tb = self.ring(s2, "dft", 6 if o == 0 else 4, [128, KG * 512], BF16)
                table = mk_table(tb)
                fl_r = self.ring(s2, "fl", 2, [128, 2, 512])
                tmp = self.ring(s2, "ctmp", 4, [128, 512])
                for kb in range(nblk):
                    C, ck = table(I["TC" + sfx][kb], (nblk, 128))
                    S, sk = table(I["TS" + sfx][kb], (nblk, 128))
                    fl, flk = fl_r[kb % 2], ("fl", kb % 2)
                    P.dma("sp", fl[:], self.FS[o, kb0 + kb], reads=[("FS", o, kb0 + kb)], writes=[flk])
                    frs, fis = fl[:, 0, :], fl[:, 1, :]
                    for sq_ in range(nseq):
                        zr, zrk = self.next_ps(); zi, zik = self.next_ps()
                        tkeys = [("tok", sq_ * nblk + b, cc) for b in range(nblk) for cc in range(4)]
                        P.op("pe", [lambda e, sb_=sb_: e.matmul(zr[:], lhsT=C[:, sb_, :], rhs=src_tok[:, sq_ * nblk + sb_, :], start=(sb_ == 0),
                                                                stop=(sb_ == nblk - 1)) for sb_ in range(nblk)], reads=[ck] + tkeys, writes=[zrk])
                        P.op("pe", [lambda e, sb_=sb_: e.matmul(zi[:], lhsT=S[:, sb_, :], rhs=src_tok[:, sq_ * nblk + sb_, :], start=(sb_ == 0),
                                                                stop=(sb_ == nblk - 1)) for sb_ in range(nblk)], reads=[sk] + tkeys, writes=[zik])
                        t1, t2, t3, t4 = tmp[0], tmp[1], tmp[2], tmp[3]
                        yidx = sq_ * nblk + kb
                        P.op("dve", lambda e: e.tensor_tensor(out=t1[:], in0=zr[:], in1=frs, op=ALU.mult), reads=[zrk, flk], writes=["ct2"])
                        P.op("dve", lambda e: e.tensor_tensor(out=t2[:], in0=zi[:], in1=fis, op=ALU.mult), reads=[zik, flk], writes=["ct3"])
                        P.op("dve", lambda e: e.tensor_tensor(out=t3[:], in0=zr[:], in1=fis, op=ALU.mult), reads=[zrk, flk], writes=["ct4"])
                        P.op("dve", lambda e: e.tensor_tensor(out=t4[:], in0=zi[:], in1=frs, op=ALU.mult), reads=[zik, flk], writes=["ct5"])
                        P.op("pool", lambda e, yidx=yidx: e.tensor_tensor(out=Yr[:, yidx, :], in0=t1[:], in1=t2[:], op=ALU.subtract),
                             reads=["ct2", "ct3"], writes=[("Yr", yidx)])
                        P.op("pool", lambda e, yidx=yidx: e.tensor_tensor(out=Yi[:, yidx, :], in0=t3[:], in1=t4[:], op=ALU.add),
                             reads=["ct4", "ct5"], writes=[("Yi", yidx)])
                for sq_ in range(nseq):
                    for tt in range(ntt):
                        banks = [self.next_ps() for _ in range(4)]
                        ngrp = nblk // KG
                        for kg in range(ngrp):
                            Ci, cik = table(I["IC" + sfx][tt, :, kg * KG:(kg + 1) * KG, :], (KG, TI))
                            Si, sik = table(I["IS" + sfx][tt, :, kg * KG:(kg + 1) * KG, :], (KG, TI))
                            for cc in range(4):
                                bk, bkk = banks[cc]
                                fns = []
                                for kq in range(KG):
                                    kb = kg * KG + kq
                                    yidx = sq_ * nblk + kb
                                    fns.append(lambda e, kq=kq, yidx=yidx, bk=bk, cc=cc, kb=kb: e.matmul(
                                        bk[:, 0:TI], lhsT=Yr[:, yidx, cc * 128:(cc + 1) * 128], rhs=Ci[:, kq, :], start=(kb == 0), stop=False))
                                    fns.append(lambda e, kq=kq, yidx=yidx, bk=bk, cc=cc, kb=kb: e.matmul(
                                        bk[:, 0:TI], lhsT=Yi[:, yidx, cc * 128:(cc + 1) * 128], rhs=Si[:, kq, :], start=False, stop=(kb == nblk - 1)))
                                P.op("pe", fns, reads=[cik, sik] + [("Yr", sq_ * nblk + kg * KG + q) for q in range(KG)]
                                     + [("Yi", sq_ * nblk + kg * KG + q) for q in range(KG)], writes=[bkk])
                        for cc in range(4):
                            post(sq_, tt, cc, banks[cc][0], banks[cc][1])
                P.barrier()

        io = {"n": 0}
        iobuf = self.ring(st, "hio", 4, [128, 512])
        zb_r = self.ring(st, "zb", 2, [128, 512], BF16)

        def ld(which, cc, tok_lo, width):
            n = io["n"]; io["n"] += 1
            tl, tk = iobuf[n % 4], ("hio", n % 4)
            src = self.hyS[which, cc * 128:(cc + 1) * 128, tok_lo:tok_lo + width]
            P.dma("sp", tl[:, 0:width], src, reads=[("hyS", which, cc, (tok_lo // TT) * TT)], writes=[tk])
            return tl, tk

        def post1(sq_, tt, cc, bk, bkk):
            tok_lo = g.tok0 + sq_ * L + tt * TI
            hv, hvk = ld(0, cc, tok_lo, TI)
            hx, hxk = ld(1, cc, tok_lo, TI)
            P.op("dve", lambda e: e.scalar_tensor_tensor(out=hv[:, 0:TI], in0=hv[:, 0:TI], scalar=hbias[:, 0, cc:cc + 1], in1=bk[:, 0:TI],
                                                         op0=ALU.mult, op1=ALU.add), reads=[hvk, bkk, "hbias"], writes=[hvk])
            P.op("dve", lambda e: e.tensor_tensor(out=hv[:, 0:TI], in0=hv[:, 0:TI], in1=hx[:, 0:TI], op=ALU.mult), reads=[hvk, hxk], writes=[hvk])
            P.dma("sp", self.hyS[3, cc * 128:(cc + 1) * 128, tok_lo:tok_lo + TI], hv[:, 0:TI], reads=[hvk],
                  writes=[("hyS", 3, cc, (tok_lo // TT) * TT)])
            n = io["n"]
            zb, zbk = zb_r[n % 2], ("zb", n % 2)
            P.op("act", lambda e: e.copy(out=zb[:, 0:TI], in_=hv[:, 0:TI]), reads=[hvk], writes=[zbk])
            nb = TI // 128
            b0 = sq_ * nblk + tt * nb
            self.to_tok(zb, zbk, src_tok[:, b0:b0 + nb, cc * 128:(cc + 1) * 128], [("tok", b0 + b, cc) for b in range(nb)], nb)

        def post2(sq_, tt, cc, bk, bkk):
            tok_lo = g.tok0 + sq_ * L + tt * TI
            z, zk = ld(3, cc, tok_lo, TI)
            hx, hxk = ld(2, cc, tok_lo, TI)
            P.op("dve", lambda e: e.scalar_tensor_tensor(out=z[:, 0:TI], in0=z[:, 0:TI], scalar=hbias[:, 1, cc:cc + 1], in1=bk[:, 0:TI],
                                                         op0=ALU.mult, op1=ALU.add), reads=[zk, bkk, "hbias"], writes=[zk])
            if g.name == "s":
                out = msbox["MS"][:, cc, :].rearrange("p (r c) -> p c r", c=64)[:, tt * 8:(tt + 1) * 8, :]
                a0 = z[:, 0:TI].rearrange("p (c r) -> p c r", r=64)
                a1 = hx[:, 0:TI].rearrange("p (c r) -> p c r", r=64)
            else:
                lo = sq_ * L + tt * TI
                out = msbox["MS"][:, cc, lo:lo + TI]
                a0, a1 = z[:, 0:TI], hx[:, 0:TI]
            P.op("dve", lambda e: e.tensor_tensor(out=out, in0=a0, in1=a1, op=ALU.mult), reads=[zk, hxk], writes=[("MS", cc, sq_, tt)])

        conv(0, post1)
        MS = self.sb(st, "MSh", [128, 4, g.ntok], BF16)
        msbox["MS"] = MS
        conv(1, post2)
        for cc in range(4):
            P.dma("sp", self.mixT[cc * 128:(cc + 1) * 128, g.tok0:g.tok0 + g.ntok], MS[:, cc, :],
                  reads=[("MS", cc, s_, t_) for s_ in range(nseq) for t_ in range(ntt)],
                  writes=[("mixT", cc, g.tok0 + t * TT) for t in range(g.ntile)])

    def rwkv_setup(self, st, l, g):
        from contextlib import ExitStack
        P, I, O = self.P, self.I, self.O
        j = l // 2
        nl, ll = TT // g.line, g.line
        lines = lambda ap: ap.rearrange("p (a b) -> p a b", b=ll)
        L, nseq = g.L, g.nseq
        CH = 128
        rwS = self.rwS
        SL = {"r": 0, "v": 1, "kk": 2, "gs": 3, "kd0": 4, "kd1": 5, "b0": 6, "b1": 7, "lw0": 8, "lw1": 9, "bon": 10}
        cst = self.sb(st, "rwc", [128, 6, 128])
        P.dma("sp", cst[:], I["rw_masks"].rearrange("a p c -> p a c"), writes=["rwc"])
        cstb = self.sb(st, "rwcb", [128, 6, 128], BF16)
        P.op("dve", lambda e: e.tensor_copy(out=cstb[:], in_=cst[:]), reads=["rwc"], writes=["rwcb"])
        prm = self.sb(st, "rwprm", [128, 16, 4])
        names = [("rw_mu", 4), ("rw_w0", 2), ("rw_a0", 2)]
        k0 = 0
        for nm, cntk in names:
            P.dma("sp", prm[:, k0:k0 + cntk, :], I[nm][j].rearrange("k (hp p) -> p k hp", p=128), writes=["rwprm"], allow_slow_non_contiguous=True)
            k0 += cntk
        for nm in ("rw_kk", "rw_ka", "rw_gn_g", "rw_gn_b"):
            P.dma("sp", prm[:, k0, :], I[nm][j].rearrange("(hp p) -> p hp", p=128), writes=["rwprm"], allow_slow_non_contiguous=True)
            k0 += 1
        P.dma("sp", prm[:, k0, :], I["rw_rk"][j].rearrange("(hp h2) n -> (h2 n) hp", h2=2), writes=["rwprm"], allow_slow_non_contiguous=True)
        PMU, PW0, PA0, PKK, PKA, PGG, PGB, PRK = 0, 4, 6, 8, 9, 10, 11, 12
        der = self.sb(st, "rwder", [128, 9, 4])
        P.op("dve", lambda e: e.tensor_scalar_mul(out=der[:, 0:4, :], in0=prm[:, 0:4, :], scalar1=0.5), reads=["rwprm"], writes=["rwder"])
        P.op("dve", lambda e: e.tensor_scalar(out=der[:, 4:8, :], in0=prm[:, 0:4, :], scalar1=-1.0, scalar2=1.0, op0=ALU.mult, op1=ALU.add),
             reads=["rwprm"], writes=["rwder"])
        P.op("dve", lambda e: e.tensor_scalar(out=der[:, 8, :], in0=prm[:, PKA, :], scalar1=-1.0, scalar2=1.0, op0=ALU.mult, op1=ALU.add),
             reads=["rwprm"], writes=["rwder"])
        e12 = self.sb(st, "e12", [128, 1]); gne = self.sb(st, "gne", [128, 1])
        P.op("dve", lambda e: e.memset(e12[:], 1e-12), writes=["e12"])
        P.op("dve", lambda e: e.memset(gne[:], 64e-5), writes=["gne"])
        rw = dict(cst=cst, cstb=cstb, prm=prm, der=der, e12=e12, gne=gne, SL=SL, PMU=PMU, PW0=PW0, PA0=PA0, PKK=PKK, PKA=PKA, PGG=PGG, PGB=PGB, PRK=PRK)
        return rw

    def rwkv_partA(self, st, l, g, hT, rw):
        from contextlib import ExitStack
        P, I, O = self.P, self.I, self.O
        j = l // 2
        nl, ll = TT // g.line, g.line
        lines = lambda ap: ap.rearrange("p (a b) -> p a b", b=ll)
        L, nseq = g.L, g.nseq
        rwS = self.rwS
        cst, cstb, prm, der, e12, gne, SL = rw["cst"], rw["cstb"], rw["prm"], rw["der"], rw["e12"], rw["gne"], rw["SL"]
        PMU, PW0, PA0, PKK, PKA, PGG, PGB, PRK = [rw[k] for k in ("PMU", "PW0", "PA0", "PKK", "PKA", "PGG", "PGB", "PRK")]
        wl = self.sb(st, "wl", [128, 8, 4, 128], BF16)
        w2t = self.sb(st, "w2t", [128, 2, 512], BF16)
        with ExitStack() as s2:
            wraw = self.sb(s2, "wraw", [128, 8, 2, 2, 64])
            mux = self.sb(s2, "mux", [128, 2, 8]); omm = self.sb(s2, "omm", [128, 2, 8])
            for ty, nm in enumerate(("rw_w1", "rw_a1")):
                for d in range(2):
                    P.dma("sp", wraw[:, :, ty, d, :], I[nm][j, d].rearrange("(kc p) n -> p kc n", p=128), writes=[("wraw", ty, d)])
            P.dma("sp", mux[:], I["rw_mu_x"][j].rearrange("k (kc p) -> p k kc", p=128), writes=["mux"], allow_slow_non_contiguous=True)
            P.op("dve", lambda e: e.tensor_scalar(out=omm[:], in0=mux[:], scalar1=-1.0, scalar2=1.0, op0=ALU.mult, op1=ALU.add), reads=["mux"], writes=["omm"])
            for ty in range(2):
                for d in range(2):
                    for var, sc in ((0, omm), (1, mux)):
                        P.op("dve", lambda e, ty=ty, d=d, var=var, sc=sc: e.tensor_tensor(
                            out=wl[:, :, ty * 2 + var, d * 64:(d + 1) * 64], in0=wraw[:, :, ty, d, :],
                            in1=sc[:, ty, :].rearrange("p (k o) -> p k o", o=1).to_broadcast([128, 8, 64]), op=ALU.mult),
                            reads=[("wraw", ty, d), "mux", "omm"], writes=["wl"])
            for ty, nm in enumerate(("rw_w2", "rw_a2")):
                for d in range(2):
                    P.dma("pool", w2t[d * 64:(d + 1) * 64, ty, :], I[nm][j, d], writes=["w2t"])
            P.barrier()
        LWI = self.sb(st, "LWI", [128, g.ntok], BF16)
        LAI = self.sb(st, "LAI", [128, g.ntok], BF16)
        tA = self.ring(st, "rta", 4, [128, TT])
        cn = {"t": 0}

        def tmpA():
            n = cn["t"]; cn["t"] += 1
            return tA[n % 4], ("rta", n % 4)

        def lora_tile(t):
            hs = lambda kc: hT[:, kc, t * TT:(t + 1) * TT]
            for ty, dst in ((0, LWI), (1, LAI)):
                pa, pak = self.next_ps(); pb, pbk = self.next_ps()
                P.op("pe", [lambda e, kc=kc: e.matmul(pa[:], lhsT=wl[:, kc, ty * 2, :], rhs=hs(kc), start=(kc == 0), stop=(kc == 7)) for kc in range(8)],
                     reads=["wl"] + self.hkeys(g, t), writes=[pak])
                P.op("pe", [lambda e, kc=kc: e.matmul(pb[:], lhsT=wl[:, kc, ty * 2 + 1, :], rhs=hs(kc), start=(kc == 0), stop=(kc == 7)) for kc in range(8)],
                     reads=["wl"] + self.hkeys(g, t), writes=[pbk])
                xb, xbk = tmpA(); ac, ack = tmpA()
                P.op("act", lambda e: e.copy(out=xb[:], in_=pb[:]), reads=[pbk], writes=[xbk])
                P.op("act", lambda e: e.copy(out=ac[:], in_=pa[:]), reads=[pak], writes=[ack])
                P.op("dve", lambda e: e.scalar_tensor_tensor(out=lines(ac[:])[:, :, 1:], in0=lines(xb[:])[:, :, :ll - 1], scalar=0.5,
                                                             in1=lines(ac[:])[:, :, 1:], op0=ALU.mult, op1=ALU.add), reads=[xbk, ack], writes=[ack])
                P.op("dve", lambda e: e.scalar_tensor_tensor(out=lines(ac[:])[:, :, :ll - 1], in0=lines(xb[:])[:, :, 1:], scalar=0.5,
                                                             in1=lines(ac[:])[:, :, :ll - 1], op0=ALU.mult, op1=ALU.add), reads=[xbk, ack], writes=[ack])
                fn = AF.Tanh if ty == 0 else AF.Identity
                P.op("act", lambda e: e.activation(out=dst[:, t * TT:(t + 1) * TT], in_=ac[:], func=fn), reads=[ack], writes=[("LI", ty, t)])
        if "norwL" not in self.flags:
            for t in range(g.ntile):
                lora_tile(t)

        wr = self.ring(st, "wrw", 6, [128, 8, 128], BF16)
        wcn = {"n": 0}

        def getw(col0):
            n = wcn["n"]; wcn["n"] += 1
            wt, wk = wr[n % 6], ("wrw", n % 6)
            self.w_chunk(wt, wk, I["cd_w_in"][j, :, col0:col0 + 128])
            return wt, wk

        def store(slot, hp, t, tl, tk):
            gt0 = g.tok0 + t * TT
            P.dma("sp", rwS[hp, SL[slot], :, gt0:gt0 + TT], tl[:], reads=[tk], writes=[("rwS", hp, slot, gt0)])

        roleA_sets = [{nm: self.sb(st, "ra_" + nm, [128, TT]) for nm in ("xl0", "xl1", "y0", "y1", "y2", "y3", "kq", "sq", "lw", "a", "kd0", "kd1", "bs")}
                      for _ in range(2)]
        sqb_sets = [self.sb(st, "ra_sqb", [128, TT], BF16) for _ in range(2)]
        thr = {"i": 0}

        def role(nm):
            return roleA_sets[thr["i"]][nm], ("ra", nm)

        def stageA_tile(hp, t, ws):
            sqb = sqb_sets[thr["i"]]
            hs = lambda kc: hT[:, kc, t * TT:(t + 1) * TT]
            sl_ = slice(t * TT, (t + 1) * TT)
            outs = []
            for n4 in range(4):
                pt, pk = self.next_ps()
                self.mm_group(pt[:], pk, ws[n4][0], ws[n4][1], hs, self.hkeys(g, t))
                yield
                xl, xlk = role("xl%d" % (n4 % 2)); y, yk = role("y%d" % n4)
                P.op("act", lambda e, xl=xl, pt=pt: e.copy(out=xl[:], in_=pt[:]), reads=[pk], writes=[xlk])
                yield
                P.op("dve", lambda e, xl=xl, y=y, n4=n4: e.tensor_scalar_mul(out=y[:], in0=xl[:], scalar1=der[:, 4 + n4, hp:hp + 1]),
                     reads=[xlk, "rwder"], writes=[yk])
                yield
                P.op("dve", lambda e, xl=xl, y=y, n4=n4: e.scalar_tensor_tensor(out=lines(y[:])[:, :, 1:], in0=lines(xl[:])[:, :, :ll - 1],
                                                                                scalar=der[:, n4, hp:hp + 1], in1=lines(y[:])[:, :, 1:], op0=ALU.mult, op1=ALU.add),
                     reads=[xlk, yk], writes=[yk])
                yield
                P.op("dve", lambda e, xl=xl, y=y, n4=n4: e.scalar_tensor_tensor(out=lines(y[:])[:, :, :ll - 1], in0=lines(xl[:])[:, :, 1:],
                                                                                scalar=der[:, n4, hp:hp + 1], in1=lines(y[:])[:, :, :ll - 1], op0=ALU.mult, op1=ALU.add),
                     reads=[xlk, yk], writes=[yk])
                yield
                outs.append((y, yk))
            (r_, rk), (k_, kk_k), (v_, vk), (g_, gk) = outs
            P.op("act", lambda e: e.activation(out=g_[:], in_=g_[:], func=AF.Sigmoid), reads=[gk], writes=[gk])
            yield
            store("gs", hp, t, g_, gk); store("r", hp, t, r_, rk); store("v", hp, t, v_, vk)
            yield
            kq, kqk = role("kq"); sq, sqk = role("sq")
            P.op("dve", lambda e: e.tensor_scalar_mul(out=kq[:], in0=k_[:], scalar1=prm[:, PKK, hp:hp + 1]), reads=[kk_k, "rwprm"], writes=[kqk])
            yield
            P.op("act", lambda e: e.activation(out=sqb[:], in_=kq[:], func=AF.Square), reads=[kqk], writes=["sqb"])
            yield
            pn, pnk = self.next_ps()
            P.op("pe", lambda e: e.matmul(pn[:], lhsT=cstb[:, 4, :], rhs=sqb[:], start=True, stop=True), reads=["rwcb", "sqb"], writes=[pnk])
            yield
            P.op("act", lambda e: e.activation(out=sq[:], in_=pn[:], func=AF.Sqrt, bias=e12[:, 0:1]), reads=[pnk, "e12"], writes=[sqk])
            yield
            P.op("dve", lambda e: e.reciprocal(out=sq[:], in_=sq[:]), reads=[sqk], writes=[sqk])
            yield
            P.op("dve", lambda e: e.tensor_tensor(out=kq[:], in0=kq[:], in1=sq[:], op=ALU.mult), reads=[kqk, sqk], writes=[kqk])
            yield
            store("kk", hp, t, kq, kqk)
            yield
            kds = []
            for d in range(2):
                pw, pwk = self.next_ps(); pa, pak = self.next_ps()
                P.op("pe", lambda e, d=d, pw=pw: e.matmul(pw[:], lhsT=w2t[d * 64:(d + 1) * 64, 0, hp * 128:(hp + 1) * 128],
                                                          rhs=LWI[d * 64:(d + 1) * 64, sl_], start=True, stop=True), reads=["w2t", ("LI", 0, t)], writes=[pwk], serial=True)
                yield
                P.op("pe", lambda e, d=d, pa=pa: e.matmul(pa[:], lhsT=w2t[d * 64:(d + 1) * 64, 1, hp * 128:(hp + 1) * 128],
                                                          rhs=LAI[d * 64:(d + 1) * 64, sl_], start=True, stop=True), reads=["w2t", ("LI", 1, t)], writes=[pak], serial=True)
                yield
                lw, lwk = role("lw"); a_, ak = role("a"); kd, kdk = role("kd%d" % d)
                P.op("act", lambda e, d=d, lw=lw, pw=pw: e.activation(out=lw[:], in_=pw[:], func=AF.Sigmoid, bias=prm[:, PW0 + d, hp:hp + 1]),
                     reads=[pwk, "rwprm"], writes=[lwk])
                yield
                P.op("dve", lambda e, lw=lw: e.tensor_scalar_mul(out=lw[:], in0=lw[:], scalar1=-float(np.exp(-0.5))), reads=[lwk], writes=[lwk])
                yield
                store("lw%d" % d, hp, t, lw, lwk)
                yield
                P.op("act", lambda e, d=d, a_=a_, pa=pa: e.activation(out=a_[:], in_=pa[:], func=AF.Sigmoid, bias=prm[:, PA0 + d, hp:hp + 1]),
                     reads=[pak, "rwprm"], writes=[ak])
                yield
                P.op("dve", lambda e, a_=a_, kd=kd: e.tensor_scalar(out=kd[:], in0=a_[:], scalar1=prm[:, PKA, hp:hp + 1], scalar2=der[:, 8, hp:hp + 1],
                                                                    op0=ALU.mult, op1=ALU.add), reads=[ak, "rwprm", "rwder"], writes=[kdk])
                yield
                P.op("dve", lambda e, kd=kd: e.tensor_tensor(out=kd[:], in0=kd[:], in1=k_[:], op=ALU.mult), reads=[kdk, kk_k], writes=[kdk])
                yield
                store("kd%d" % d, hp, t, kd, kdk)
                yield
                P.op("dve", lambda e, a_=a_: e.tensor_tensor(out=a_[:], in0=a_[:], in1=kq[:], op=ALU.mult), reads=[ak, kqk], writes=[ak])
                yield
                store("b%d" % d, hp, t, a_, ak)
                yield
                kds.append((kd, kdk))
            bs, bsk = role("bs")
            P.op("dve", lambda e: e.tensor_tensor(out=bs[:], in0=kds[0][0][:], in1=kds[1][0][:], op=ALU.add), reads=[kds[0][1], kds[1][1]], writes=[bsk])
            yield
            P.op("dve", lambda e: e.scalar_tensor_tensor(out=sqb[:], in0=r_[:], scalar=prm[:, PRK, hp:hp + 1], in1=bs[:], op0=ALU.mult, op1=ALU.mult),
                 reads=[rk, bsk, "rwprm"], writes=["sqb"])
            yield
            pbn, pbnk = self.next_ps()
            P.op("pe", lambda e: e.matmul(pbn[:], lhsT=cstb[:, 4, :], rhs=sqb[:], start=True, stop=True), reads=["rwcb", "sqb"], writes=[pbnk])
            yield
            P.op("dve", lambda e: e.tensor_tensor(out=bs[:], in0=pbn[:], in1=v_[:], op=ALU.mult), reads=[pbnk, vk], writes=[bsk])
            yield
            store("bon", hp, t, bs, bsk)
            yield

        def a_thread(hp, ws, tiles):
            for t in tiles:
                yield from stageA_tile(hp, t, ws)

        for hp in range(4):
            ws = [getw(1536 + n4 * 512 + hp * 128) for n4 in range(4)]
            if "norwA" not in self.flags:
                gens = [(i, a_thread(hp, ws, list(range(i, g.ntile, 2)))) for i in range(2)]
                while gens:
                    for item in list(gens):
                        thr["i"] = item[0]
                        P.ns = ("A", item[0])
                        try:
                            next(item[1])
                        except StopIteration:
                            gens.remove(item)
                P.ns = None

    def rwkv_partB(self, st, l, g, rw):
        from contextlib import ExitStack
        P, I, O = self.P, self.I, self.O
        j = l // 2
        L, nseq = g.L, g.nseq
        CH = 128
        rwS = self.rwS
        cst, cstb, prm, der, e12, gne, SL = rw["cst"], rw["cstb"], rw["prm"], rw["der"], rw["e12"], rw["gne"], rw["SL"]
        PGG, PGB = rw["PGG"], rw["PGB"]
        tA = self.ring(st, "rtb", 8, [128, TT])
        cn = {"t": 0}

        def tmpA():
            n = cn["t"]; cn["t"] += 1
            return tA[n % 8], ("rtb", n % 8)
        def stageB(hp, sq_, d, bst):
            (ldr, TLar, Bt, Kt, Bh, Kh, Vb, clt, ext, rmask, N2r, X2r, TTr, ATr, TOKr, W1Tr, Zr_, U0r, U2r, ST32, STb, oacc, snat, pcr) = bst
            nch = L // CH
            ntile = max(1, L // TT)
            tw = min(TT, L)
            cpt = tw // CH
            hb_ = 0
            if g.name == "s":
                P.op("dve", lambda e: e.memset(snat[:], 0.0), writes=["snat"])
                yield
                for h2 in range(2):
                    P.dma("sp", snat[h2 * 64:(h2 + 1) * 64, h2 * 64:(h2 + 1) * 64], I["st_rwkv"][j, d, hp * 2 + h2], writes=["snat"])
                    yield
                pt, pk = self.next_ps()
                P.op("pe", lambda e: e.transpose(pt[:, 0:128], snat[:], self.ident[:]), reads=["snat", "ident"], writes=[pk])
                yield
                P.op("dve", lambda e: e.tensor_copy(out=ST32[:], in_=pt[:, 0:128]), reads=[pk], writes=["ST32"])
                yield
            else:
                P.op("dve", lambda e: e.memset(ST32[:], 0.0), writes=["ST32"])
                yield
            P.op("act", lambda e: e.copy(out=STb[0][:], in_=ST32[:]), reads=["ST32"], writes=[("STb", 0)])
            yield
            stn = {"n": 0}
            order = list(range(ntile)) if d == 0 else list(reversed(range(ntile)))
            for ti, t in enumerate(order):
                tok_lo = g.tok0 + sq_ * L + t * tw
                gt0 = (tok_lo // TT) * TT
                ld = {}
                for q, nm in enumerate(("r", "lw%d" % d, "kd%d" % d, "v", "kk", "b%d" % d)):
                    tl, tk = ldr[q], ("ldr", q)
                    P.dma("sp", tl[:, 0:tw], rwS[hp, SL[nm], :, tok_lo:tok_lo + tw], reads=[("rwS", hp, nm, gt0)], writes=[tk])
                    yield
                    ld[q] = (tl, tk)
                (r_, rk), (lw, lwk), (kd, kdk), (v_, vk), (kk, kkk), (b_, bk) = [ld[q] for q in range(6)]
                if "nosb2" in self.flags:
                    continue
                W = slice(0, tw)
                c3 = lambda ap: ap[:, W].rearrange("p (c i) -> p c i", i=CH)
                cl, e1, e2, e3, e4 = clt, ext[0], ext[1], ext[2], ext[3]
                if d == 0:
                    P.op("dve", lambda e: e.tensor_tensor_scan(out=cl[:, W], data0=rmask[:, 0, W], data1=lw[:, W], initial=0.0, op0=ALU.mult, op1=ALU.add),
                         reads=[lwk, "rmask"], writes=["cl"])
                    yield
                    cend = c3(cl)[:, :, CH - 1:CH]
                else:
                    P.op("dve", lambda e: e.tensor_tensor_scan(out=cl[:, W][:, ::-1], data0=rmask[:, 1, W][:, ::-1], data1=lw[:, W][:, ::-1], initial=0.0,
                                                               op0=ALU.mult, op1=ALU.add), reads=[lwk, "rmask"], writes=["cl"])
                    yield
                    cend = c3(cl)[:, :, 0:1]
                P.op("act", lambda e: e.activation(out=e1[:, W], in_=cl[:, W], func=AF.Exp), reads=["cl"], writes=["e1"])
                yield
                P.op("act", lambda e: e.activation(out=e2[:, W], in_=cl[:, W], func=AF.Exp, scale=-1.0), reads=["cl"], writes=["e2"])
                yield
                P.op("dve", lambda e: e.tensor_tensor(out=e3[:, W], in0=cl[:, W], in1=lw[:, W], op=ALU.subtract), reads=["cl", lwk], writes=["e3"])
                yield
                P.op("act", lambda e: e.activation(out=e3[:, W], in_=e3[:, W], func=AF.Exp), reads=["e3"], writes=["e3"])
                yield
                P.op("dve", lambda e: e.tensor_tensor(out=c3(e4), in0=cend.to_broadcast([128, cpt, CH]), in1=c3(cl), op=ALU.subtract),
                     reads=["cl"], writes=["e4"])
                yield
                P.op("act", lambda e: e.activation(out=e4[:, W], in_=e4[:, W], func=AF.Exp), reads=["e4"], writes=["e4"])
                yield
                pc, pck = pcr[ti % 2], ("pcr", ti % 2)
                P.op("act", lambda e: e.activation(out=pc[:, 0:cpt], in_=cend.rearrange("p c o -> p (c o)"), func=AF.Exp), reads=["cl"], writes=[pck])
                yield
                P.op("dve", lambda e: e.scalar_tensor_tensor(out=TLar[:, 0:cpt, 0, :], in0=c3(kk), scalar=-1.0, in1=c3(e3), op0=ALU.mult, op1=ALU.mult),
                     reads=[kkk, "e3"], writes=["TLa"])
                yield
                P.op("dve", lambda e: e.tensor_tensor(out=TLar[:, 0:cpt, 1, :], in0=c3(r_), in1=c3(e1), op=ALU.mult), reads=[rk, "e1"], writes=["TLr"])
                yield
                P.op("dve", lambda e: e.tensor_tensor(out=Bt[:, W], in0=b_[:, W], in1=e2[:, W], op=ALU.mult), reads=[bk, "e2"], writes=["Bt"])
                yield
                P.op("dve", lambda e: e.tensor_tensor(out=Kt[:, W], in0=kd[:, W], in1=e2[:, W], op=ALU.mult), reads=[kdk, "e2"], writes=["Kt"])
                yield
                P.op("pool", lambda e: e.tensor_tensor(out=Bh[:, W], in0=b_[:, W], in1=e4[:, W], op=ALU.mult), reads=[bk, "e4"], writes=["Bh"])
                yield
                P.op("pool", lambda e: e.tensor_tensor(out=Kh[:, W], in0=kd[:, W], in1=e4[:, W], op=ALU.mult), reads=[kdk, "e4"], writes=["Kh"])
                yield
                P.op("act", lambda e: e.copy(out=Vb[:, W], in_=v_[:, W]), reads=[vk], writes=["Vb"])
                yield
                if "nosb3" in self.flags:
                    continue
                corder = list(range(cpt)) if d == 0 else list(reversed(range(cpt)))
                for c in corder:
                    yield from self.rwkv_chunk(hp, sq_, d, t, c, tw, cst, cstb, bst, pc, pck, stn, g)
            if g.name == "p":
                pt, pk = self.next_ps()
                P.op("pe", lambda e: e.transpose(pt[:, 0:128], ST32[:], self.ident[:]), reads=["ST32", "ident"], writes=[pk])
                yield
                P.op("dve", lambda e: e.tensor_copy(out=snat[:], in_=pt[:, 0:128]), reads=[pk], writes=["snat"])
                yield
                for h2 in range(2):
                    P.dma("sp", O["nrwkv"][sq_, j, d, hp * 2 + h2], snat[h2 * 64:(h2 + 1) * 64, h2 * 64:(h2 + 1) * 64], reads=["snat"],
                          writes=[("nrwkv", sq_, j, d, hp, h2)])
                    yield

        with ExitStack() as sB:
            rmask = self.sb(sB, "rmask", [128, 2, TT])
            P.op("dve", lambda e: e.memset(rmask[:], 1.0), writes=["rmask"])
            P.op("dve", lambda e: e.memset(rmask[:, 0, :].rearrange("p (c i) -> p c i", i=CH)[:, :, 0:1], 0.0), writes=["rmask"])
            P.op("dve", lambda e: e.memset(rmask[:, 1, :].rearrange("p (c i) -> p c i", i=CH)[:, :, CH - 1:CH], 0.0), writes=["rmask"])
            bsts = {}
            self.TTb_d = {}
            for d in range(2):
                ldr = self.ring(sB, "ldr", 6, [128, TT])
                TLar = self.sb(sB, "TLar", [128, 4, 2, 128], BF16)
                Bt = self.sb(sB, "Bt", [128, TT], BF16); Kt = self.sb(sB, "Kt", [128, TT], BF16)
                Bh = self.sb(sB, "Bh", [128, TT], BF16); Kh = self.sb(sB, "Kh", [128, TT], BF16); Vb = self.sb(sB, "Vb", [128, TT], BF16)
                clt = self.sb(sB, "clt", [128, TT]); ext = self.ring(sB, "ext", 4, [128, TT])
                N2r = self.ring(sB, "N2", 2, [128, 2, 128]); X2r = self.ring(sB, "X2", 2, [128, 2, 128])
                TTr = self.ring(sB, "TT", 2, [128, 2, 128])
                self.TTb_d[d] = self.sb(sB, "TTb", [128, 2, 128], BF16)
                ATr = self.sb(sB, "ATr", [128, 3, 2, 128], BF16)
                TOKr = self.sb(sB, "TOK", [128, 4, 128], BF16)
                W1Tr = self.sb(sB, "W1T", [128, 128], BF16)
                Zr_ = self.sb(sB, "Zrw", [128, 128], BF16); U0r = self.sb(sB, "U0", [128, 128]); U2r = self.sb(sB, "U2", [128, 128], BF16)
                ST32 = self.sb(sB, "ST32", [128, 128]); STb = self.ring(sB, "STb", 2, [128, 128], BF16)
                oacc = self.sb(sB, "oacc", [128, g.ntok])
                snat = self.sb(sB, "snat", [128, 128]); pcr = self.ring(sB, "pcr", 2, [128, 4])
                bsts[d] = (ldr, TLar, Bt, Kt, Bh, Kh, Vb, clt, ext, rmask, N2r, X2r, TTr, ATr, TOKr, W1Tr, Zr_, U0r, U2r, ST32, STb, oacc, snat, pcr)
            msr = self.ring(sB, "msd", 2, [128, g.ntok], BF16)
            self.post_b = (self.sb(sB, "post_ob", [128, TT], BF16), self.sb(sB, "post_sq", [128, TT], BF16))
            self.cstb_ = cstb

            def thread(hp, d):
                for sq_ in range(nseq):
                    yield from stageB(hp, sq_, d, bsts[d])

            def run_threads(gens):
                gens = list(gens)
                while gens:
                    for item in list(gens):
                        P.ns = item[0]
                        try:
                            next(item[1])
                        except StopIteration:
                            gens.remove(item)
                P.ns = None

            for hp in range(4):
                if "norwB" not in self.flags:
                    run_threads([(d, thread(hp, d)) for d in range(2)])
                if "norwP" not in self.flags:
                    self.rwkv_post(hp, g, j, cst, prm, gne, (bsts[0][21], bsts[1][21]), msr[hp % 2], ("msd", hp % 2), tmpA, SL, PGG, PGB)

    def rwkv_chunk(self, hp, sq_, d, t, c, tw, cst, cstb, bst, pc, pck, stn, g):
        P = self.P
        (ldr, TLar, Bt, Kt, Bh, Kh, Vb, clt, ext, rmask, N2r, X2r, TTr, ATr, TOKr, W1Tr, Zr_, U0r, U2r, ST32, STb, oacc, snat, pcr) = bst
        CH = 128
        cs = slice(c * CH, (c + 1) * CH)
        mN, mX, mI = (0, 2, 3) if d == 0 else (2, 0, 1)
        h2v = lambda ap, w: ap[:, 0:2 * w].rearrange("p (h i) -> p h i", h=2)
        bc2 = lambda m: cst[:, m, :].rearrange("p (o i) -> p o i", o=1).to_broadcast([128, 2, 128])
        tl_keys = ["TLa", "TLr"]
        pA, pAk = self.next_ps()
        pB, pBk = self.next_ps()
        pC, pCk = self.next_ps()
        N_, X_, T_ = N2r[0], X2r[0], TTr[0]
        for h in range(2):
            tl2 = TLar[h * 64:(h + 1) * 64, c, :, :].rearrange("p a i -> p (a i)")
            P.op("pe", lambda e, h=h: e.matmul(pA[:, h * 128:(h + 1) * 128], lhsT=TLar[h * 64:(h + 1) * 64, c, 0, :], rhs=Bt[h * 64:(h + 1) * 64, cs],
                                               start=True, stop=True), reads=["TLa", "Bt"], writes=[pAk], serial=True)
            P.op("pe", lambda e, h=h, tl2=tl2: e.matmul(pB[:, h * 256:(h + 1) * 256], lhsT=Bt[h * 64:(h + 1) * 64, cs], rhs=tl2, start=True, stop=True),
                 reads=["TLa", "TLr", "Bt"], writes=[pBk])
            P.op("pe", lambda e, h=h, tl2=tl2: e.matmul(pC[:, h * 256:(h + 1) * 256], lhsT=Kt[h * 64:(h + 1) * 64, cs], rhs=tl2, start=True, stop=True),
                 reads=["TLa", "TLr", "Kt"], writes=[pCk])
        P.op("dve", lambda e: e.tensor_tensor(out=N_[:], in0=h2v(pA, 128), in1=bc2(mN), op=ALU.mult), reads=[pAk, "rwc"], writes=[("N2", 0)])
        yield
        pB4 = pB[:, 0:512].rearrange("p (h a i) -> p h a i", h=2, a=2)
        P.op("dve", lambda e: e.tensor_tensor(out=X_[:], in0=pB4[:, :, 0, :], in1=bc2(mX), op=ALU.mult), reads=[pBk, "rwc"], writes=[("X2", 0)])
        yield
        P.op("dve", lambda e: e.tensor_tensor(out=ATr[:, 0, :, :], in0=pB4[:, :, 1, :], in1=bc2(mI), op=ALU.mult), reads=[pBk, "rwc"], writes=["ArbT"])
        yield
        pC4 = pC[:, 0:512].rearrange("p (h a i) -> p h a i", h=2, a=2)
        P.op("dve", lambda e: e.tensor_tensor(out=ATr[:, 1, :, :], in0=pC4[:, :, 0, :], in1=bc2(mX), op=ALU.mult), reads=[pCk, "rwc"], writes=["AakT"])
        yield
        P.op("dve", lambda e: e.tensor_tensor(out=ATr[:, 2, :, :], in0=pC4[:, :, 1, :], in1=bc2(mI), op=ALU.mult), reads=[pCk, "rwc"], writes=["ArkT"])
        yield
        self._pe_serial_next = True
        idb = self.ident[:].rearrange("p (o i) -> p o i", o=1).to_broadcast([128, 2, 128])
        P.op("pool", lambda e: e.tensor_tensor(out=T_[:], in0=X_[:], in1=idb, op=ALU.add), reads=[("X2", 0), "ident"], writes=[("TT", 0)])
        yield
        cur = 0
        for r in range(1, 7):
            nx = 1 - cur
            Nc, Xc, Tc = N2r[cur], X2r[cur], TTr[cur]
            Nn, Xn, Tn = N2r[nx], X2r[nx], TTr[nx]
            pn, pnk = self.next_ps()
            P.op("pe", [lambda e, h=h, Nc=Nc, Xc=Xc, pn=pn: e.matmul(pn[:, h * 128:(h + 1) * 128], lhsT=Xc[:, h, :], rhs=Nc[:, h, :], start=True, stop=True)
                        for h in range(2)], reads=[("X2", cur), ("N2", cur)], writes=[pnk])
            yield
            P.op("act", lambda e, Nn=Nn, pn=pn: e.copy(out=Nn[:], in_=h2v(pn, 128)), reads=[pnk], writes=[("N2", nx)])
            yield
            if r < 6:
                px, pxk = self.next_ps()
                P.op("pe", [lambda e, h=h, Nc=Nc, Xc=Xc, px=px: e.matmul(px[:, h * 128:(h + 1) * 128], lhsT=Nc[:, h, :], rhs=Xc[:, h, :], start=True, stop=True)
                            for h in range(2)], reads=[("X2", cur), ("N2", cur)], writes=[pxk])
                yield
                P.op("act", lambda e, Xn=Xn, px=px: e.copy(out=Xn[:], in_=h2v(px, 128)), reads=[pxk], writes=[("X2", nx)])
                yield
            pt_, ptk = self.next_ps()
            P.op("pe", [lambda e, h=h, Nn=Nn, Tc=Tc, pt_=pt_: e.matmul(pt_[:, h * 128:(h + 1) * 128], lhsT=Nn[:, h, :], rhs=Tc[:, h, :], start=True, stop=True)
                        for h in range(2)], reads=[("N2", nx), ("TT", cur)], writes=[ptk])
            yield
            P.op("dve", lambda e, Tn=Tn, Tc=Tc, pt_=pt_: e.tensor_tensor(out=Tn[:], in0=h2v(pt_, 128), in1=Tc[:], op=ALU.add),
                 reads=[ptk, ("TT", cur)], writes=[("TT", nx)])
            yield
            cur = nx
        Tf, Tfk = self.TTb_d[d], "TTb"
        P.op("pool", lambda e: e.tensor_copy(out=Tf[:], in_=TTr[cur][:]), reads=[("TT", cur)], writes=["TTb"])
        yield
        if "norc2" in self.flags:
            return
        ptk_, ptkk = self.next_ps()
        pv = ptk_[:].bitcast(BF16)
        srcs = [(TLar[:, c, 0, :], "TLa"), (Bh[:, cs], "Bh"), (Kh[:, cs], "Kh"), (Vb[:, cs], "Vb")]
        P.op("pe", [lambda e, q=q, s=s: e.transpose(pv[:, q * 128:(q + 1) * 128], s[0], self.identb[:]) for q, s in enumerate(srcs)],
             reads=["TLa", "Bh", "Kh", "Vb", "identb"], writes=[ptkk])
        yield
        P.op("act", lambda e: e.copy(out=TOKr[:], in_=pv[:, 0:512].rearrange("p (q i) -> p q i", q=4)), reads=[ptkk], writes=["TOK"])
        yield
        pw, pwk = self.next_ps()
        P.op("pe", [lambda e, h=h: e.matmul(pw[:, h * 128:(h + 1) * 128], lhsT=TOKr[:, 0, :], rhs=Tf[:, h, :], start=True, stop=True) for h in range(2)],
             reads=["TOK", Tfk], writes=[pwk])
        yield
        P.op("act", lambda e: e.copy(out=W1Tr[0:64, :], in_=pw[0:64, 0:128]), reads=[pwk], writes=["W1Ta"])
        yield
        P.op("dve", lambda e: e.tensor_copy(out=W1Tr[64:128, :], in_=pw[64:128, 128:256]), reads=[pwk], writes=["W1Tb"])
        yield
        pz, pzk = self.next_ps()
        P.op("pe", [lambda e, h=h: e.matmul(pz[:, h * 64:(h + 1) * 64], lhsT=ATr[:, 1, h, :], rhs=TOKr[:, 3, h * 64:(h + 1) * 64], start=True, stop=True)
                    for h in range(2)], reads=["AakT", "TOK"], writes=[pzk])
        yield
        P.op("act", lambda e: e.copy(out=Zr_[:], in_=pz[:, 0:128]), reads=[pzk], writes=["Zrw"])
        yield
        pu0, pu0k = self.next_ps()
        P.op("pe", [lambda e, h=h: e.matmul(pu0[:, h * 64:(h + 1) * 64], lhsT=Tf[:, h, :], rhs=Zr_[:, h * 64:(h + 1) * 64], start=True, stop=True)
                    for h in range(2)], reads=[Tfk, "Zrw"], writes=[pu0k])
        yield
        P.op("dve", lambda e: e.tensor_copy(out=U0r[:], in_=pu0[:, 0:128]), reads=[pu0k], writes=["U0"])
        yield
        if "norc3" in self.flags:
            return
        n = stn["n"]; stn["n"] += 1
        Sc, Sck = STb[n % 2], ("STb", n % 2)
        Sn, Snk = STb[(n + 1) % 2], ("STb", (n + 1) % 2)
        pkb, pkbk = self.next_ps()
        P.op("pe", lambda e: e.matmul(pkb[:, 0:128], lhsT=TOKr[:, 2, :], rhs=TOKr[:, 3, :], start=True, stop=False), reads=["TOK"], writes=[pkbk])
        yield
        pu, puk = self.next_ps()
        P.op("pe", lambda e: e.matmul(pu[:, 0:128], lhsT=W1Tr[:], rhs=Sc[:], start=True, stop=True), reads=["W1Ta", "W1Tb", Sck], writes=[puk])
        yield
        P.op("dve", lambda e: e.tensor_tensor(out=U2r[:], in0=pu[:, 0:128], in1=U0r[:], op=ALU.add), reads=[puk, "U0"], writes=["U2"])
        yield
        P.op("pe", lambda e: e.matmul(pkb[:, 0:128], lhsT=TOKr[:, 1, :], rhs=U2r[:], start=False, stop=True), reads=["TOK", "U2"], writes=[pkbk])
        yield
        po, pok = self.next_ps()
        fns = [lambda e: e.matmul(po[:, 0:128], lhsT=TLar[:, c, 1, :], rhs=Sc[:], start=True, stop=False)]
        for h in range(2):
            fns.append(lambda e, h=h: e.matmul(po[:, h * 64:(h + 1) * 64], lhsT=ATr[:, 0, h, :], rhs=U2r[:, h * 64:(h + 1) * 64], start=False, stop=False))
            fns.append(lambda e, h=h: e.matmul(po[:, h * 64:(h + 1) * 64], lhsT=ATr[:, 2, h, :], rhs=TOKr[:, 3, h * 64:(h + 1) * 64], start=False, stop=(h == 1)))
        P.op("pe", fns, reads=["TLr", Sck, "ArbT", "ArkT", "U2", "TOK"], writes=[pok])
        yield
        if "norc4" in self.flags:
            return
        tS = ext[0]
        P.op("dve", lambda e: e.tensor_tensor(out=tS[:, 0:128], in0=pkb[:, 0:128], in1=cst[:, 4, :], op=ALU.mult), reads=[pkbk, "rwc"], writes=["tS"])
        yield
        P.op("dve", lambda e: e.scalar_tensor_tensor(out=ST32[:], in0=ST32[:], scalar=pc[:, c:c + 1], in1=tS[:, 0:128], op0=ALU.mult, op1=ALU.add),
             reads=["ST32", "tS", pck], writes=["ST32"])
        yield
        P.op("act", lambda e: e.copy(out=Sn[:], in_=ST32[:]), reads=["ST32"], writes=[Snk])
        yield
        osb = ext[1]
        P.op("act", lambda e: e.copy(out=osb[:, 0:128], in_=po[:, 0:128]), reads=[pok], writes=["osb"])
        yield
        pot, potk = self.next_ps()
        P.op("pe", lambda e: e.transpose(pot[:, 0:128], osb[:, 0:128], self.ident[:]), reads=["osb", "ident"], writes=[potk])
        yield
        lo = sq_ * g.L + t * tw + c * CH
        P.op("dve", lambda e: e.tensor_copy(out=oacc[:, lo:lo + CH], in_=pot[:, 0:128]), reads=[potk], writes=[("oacc", lo)])
        yield

    def rwkv_post(self, hp, g, j, cst, prm, gne, oacc, ms, msk, tmpA, SL, PGG, PGB):
        P = self.P
        rwS = self.rwS
        oacc0, oacc1 = oacc
        oacc = oacc0
        for t in range(g.ntile):
            sl_ = slice(t * TT, (t + 1) * TT)
            ok0 = [("ns", 0, ("oacc", t * TT + q * 128)) for q in range(4)]
            ok1 = [("ns", 1, ("oacc", t * TT + q * 128)) for q in range(4)]
            P.op("dve", lambda e, sl_=sl_: e.tensor_tensor(out=oacc0[:, sl_], in0=oacc0[:, sl_], in1=oacc1[:, sl_], op=ALU.add), reads=ok0 + ok1, writes=ok0)
            okeys = ok0
            gt0 = g.tok0 + t * TT
            bon, bonk = tmpA(); gs, gsk = tmpA(); sq, sqk = tmpA(); mean, meank = tmpA(); on, onk = tmpA()
            P.dma("sp", bon[:], rwS[hp, SL["bon"], :, gt0:gt0 + TT], reads=[("rwS", hp, "bon", gt0)], writes=[bonk])
            P.dma("sp", gs[:], rwS[hp, SL["gs"], :, gt0:gt0 + TT], reads=[("rwS", hp, "gs", gt0)], writes=[gsk])
            ob, sb2 = self.post_b
            P.op("act", lambda e, sl_=sl_: e.activation(out=sb2[:], in_=oacc[:, sl_], func=AF.Square), reads=okeys, writes=["post_sq"])
            P.op("act", lambda e, sl_=sl_: e.copy(out=ob[:], in_=oacc[:, sl_]), reads=okeys, writes=["post_ob"])
            pm, pmk = self.next_ps(); p2, p2k = self.next_ps()
            P.op("pe", lambda e, pm=pm: e.matmul(pm[:], lhsT=self.cstb_[:, 4, :], rhs=ob[:], start=True, stop=True), reads=["post_ob", "rwcb"], writes=[pmk])
            P.op("pe", lambda e, p2=p2: e.matmul(p2[:], lhsT=self.cstb_[:, 4, :], rhs=sb2[:], start=True, stop=True), reads=["post_sq", "rwcb"], writes=[p2k])
            P.op("act", lambda e, mean=mean, pm=pm: e.activation(out=mean[:], in_=pm[:], func=AF.Identity, scale=1.0 / 64.0), reads=[pmk], writes=[meank])
            P.op("dve", lambda e, sq=sq, mean=mean: e.tensor_tensor(out=sq[:], in0=mean[:], in1=mean[:], op=ALU.mult), reads=[meank], writes=[sqk])
            P.op("dve", lambda e, sq=sq, p2=p2: e.scalar_tensor_tensor(out=sq[:], in0=p2[:], scalar=1.0 / 64.0, in1=sq[:], op0=ALU.mult, op1=ALU.subtract),
                 reads=[p2k, sqk], writes=[sqk])
            P.op("act", lambda e, sq=sq: e.activation(out=sq[:], in_=sq[:], func=AF.Sqrt, bias=gne[:, 0:1]), reads=[sqk, "gne"], writes=[sqk])
            P.op("dve", lambda e, sq=sq: e.reciprocal(out=sq[:], in_=sq[:]), reads=[sqk], writes=[sqk])
            P.op("dve", lambda e, on=on, mean=mean, sl_=sl_: e.tensor_tensor(out=on[:], in0=oacc[:, sl_], in1=mean[:], op=ALU.subtract),
                 reads=okeys + [meank], writes=[onk])
            P.op("dve", lambda e, on=on, sq=sq: e.tensor_tensor(out=on[:], in0=on[:], in1=sq[:], op=ALU.mult), reads=[onk, sqk], writes=[onk])
            P.op("dve", lambda e, on=on: e.tensor_scalar(out=on[:], in0=on[:], scalar1=prm[:, PGG, hp:hp + 1], scalar2=prm[:, PGB, hp:hp + 1],
                                                         op0=ALU.mult, op1=ALU.add), reads=[onk, "rwprm"], writes=[onk])
            P.op("dve", lambda e, on=on, bon=bon: e.tensor_tensor(out=on[:], in0=on[:], in1=bon[:], op=ALU.add), reads=[onk, bonk], writes=[onk])
            if g.name == "s":
                out = ms[:, :].rearrange("p (r c) -> p c r", c=64)[:, t * 8:(t + 1) * 8, :]
                a0 = on[:].rearrange("p (c r) -> p c r", r=64); a1 = gs[:].rearrange("p (c r) -> p c r", r=64)
            else:
                out, a0, a1 = ms[:, sl_], on[:], gs[:]
            P.op("dve", lambda e, out=out, a0=a0, a1=a1: e.tensor_tensor(out=out, in0=a0, in1=a1, op=ALU.mult), reads=[onk, gsk], writes=[msk + (t,)])
        P.dma("sp", self.mixT[512 + hp * 128:512 + (hp + 1) * 128, g.tok0:g.tok0 + g.ntok], ms[:],
              reads=[msk + (t,) for t in range(g.ntile)], writes=[("mixT", 4 + hp, g.tok0 + t * TT) for t in range(g.ntile)])


WEIGHT_SHAPES = {
    "w_mod": [4, 1024, 6144], "b_mod": [4, 6144], "ln1_g": [4, 1024], "ln1_b": [4, 1024], "ln2_g": [4, 1024], "ln2_b": [4, 1024],
    "mlp_w1": [4, 1024, 4096], "mlp_w2": [4, 4096, 1024], "w_out": [4, 1024, 1024], "ab_w_in": [2, 1024, 2560],
    "sc_conv": [2, 3, 512], "lru_conv": [2, 4, 512], "lru_conv_b": [2, 512], "lru_wa": [2, 2, 8, 64, 64], "lru_ba": [2, 2, 512],
    "lru_wi": [2, 2, 8, 64, 64], "lru_bi": [2, 2, 512], "lru_lambda": [2, 2, 512], "cd_w_in": [2, 1024, 3584],
    "hy_conv": [2, 3, 1536], "hy_w1": [2, 33, 64], "hy_b1": [2, 64], "hy_w2": [2, 64, 64], "hy_b2": [2, 64], "hy_w3": [2, 64, 2048],
    "hy_freq": [2, 64], "hy_bias": [2, 2, 512], "rw_mu": [2, 4, 512], "rw_mu_x": [2, 2, 1024], "rw_w0": [2, 2, 512],
    "rw_w1": [2, 2, 1024, 64], "rw_w2": [2, 2, 64, 512], "rw_a0": [2, 2, 512], "rw_a1": [2, 2, 1024, 64], "rw_a2": [2, 2, 64, 512],
    "rw_kk": [2, 512], "rw_ka": [2, 512], "rw_rk": [2, 8, 64], "rw_gn_g": [2, 512], "rw_gn_b": [2, 512],
}

CONST_SPECS = [
    ("rw_masks", [6, 128, 128], F32),
    ("TC4096", [32, 128, 32, 128], BF16), ("TS4096", [32, 128, 32, 128], BF16),
    ("IC4096", [8, 128, 32, 512], BF16), ("IS4096", [8, 128, 32, 512], BF16),
    ("TC256", [2, 128, 2, 128], BF16), ("TS256", [2, 128, 2, 128], BF16),
    ("IC256", [1, 128, 2, 256], BF16), ("IS256", [1, 128, 2, 256], BF16),
    ("featT4096", [33, 4096], F32), ("featT256", [33, 256], F32),
    ("ntn4096", [128, 32], F32), ("ntn256", [128, 2], F32),
    ("delta_b", [128, 512], F32), ("m1col", [128, 1], F32),
]
_CONSTS = None


def make_consts():
    global _CONSTS
    if _CONSTS is not None:
        return _CONSTS
    c = {}
    i = np.arange(128)[:, None]; jj = np.arange(128)[None, :]
    bd = (i // 64 == jj // 64)
    c["rw_masks"] = np.stack([jj < i, jj <= i, jj > i, jj >= i, bd, bd / 64.0]).astype(np.float32)
    bf = ml_dtypes.bfloat16
    for L, sfx, TI in ((4096, "4096", 512), (256, "256", 256)):
        nb = L // 128
        k = np.arange(L, dtype=np.float64) + 0.5
        s_ = np.arange(L, dtype=np.float64)
        th = np.mod(np.outer(s_, k), 2.0 * L) * (np.pi / L)
        C = np.cos(th); S = -np.sin(th)
        for nm, M in (("TC", C), ("TS", S)):
            c[nm + sfx] = np.ascontiguousarray(M.reshape(nb, 128, nb, 128).transpose(2, 1, 0, 3)).astype(bf)
        for nm, M in (("IC", C), ("IS", S)):
            c[nm + sfx] = np.ascontiguousarray(M.reshape(L // TI, TI, nb, 128).transpose(0, 3, 2, 1)).astype(bf)
        del C, S, th
        tn = np.linspace(0.0, 1.0, L, dtype=np.float32)
        tr = np.arange(L, dtype=np.float32)
        bands = np.linspace(1e-4, 15.0, 16, dtype=np.float32)
        ang = (np.float32(2.0 * np.pi / L) * tr[:, None] * bands[None, :]).astype(np.float32)
        feats = np.concatenate([tn[:, None], np.cos(ang), -np.sin(ang)], -1).astype(np.float32)
        c["featT" + sfx] = np.ascontiguousarray(feats.T)
        c["ntn" + sfx] = np.ascontiguousarray((-tn).reshape(nb, 128).T)
    deltas = np.abs(np.linspace(np.log(1e-2) / 1.5, np.log(1e-2) / 0.3, 512, dtype=np.float32))
    c["delta_b"] = np.ascontiguousarray(np.broadcast_to(deltas[None, :], (128, 512))).astype(np.float32)
    m1 = np.ones((128, 1), np.float32); m1[0, 0] = 0.0
    c["m1col"] = m1
    _CONSTS = c
    return c


_NC_CACHE = {}


def _get_nc(depth=4, groups=("s", "p")):
    key = (depth, tuple(groups))
    if key not in _NC_CACHE:
        _NC_CACHE[key] = Builder(depth=depth, groups=groups).build()
    return _NC_CACHE[key]


def make_in_maps(inputs):
    f = lambda a: np.ascontiguousarray(np.asarray(a, dtype=np.float32))
    w = {k: f(inputs[k]) for k in WEIGHT_SHAPES}
    ident = np.eye(128, dtype=np.float32)
    w.update(make_consts())
    maps = []
    for i in range(NCORES):
        m = dict(w)
        m["xs"] = f(inputs["x_sample"][i])
        m["xp"] = f(inputs["x_prompt"][4 * i:4 * i + 4]).reshape(1024, D)
        m["st_lru"] = f(inputs["state_lru"][i])
        m["st_rwkv"] = f(inputs["state_rwkv"][i])
        m["cvecs"] = np.ascontiguousarray(np.stack([inputs["c"][i], inputs["c_ctx"]]).astype(np.float32))
        m["ident"] = ident
        maps.append(m)
    return maps


def kernel(**inputs):
    nc = _get_nc()
    maps = make_in_maps(inputs)
    res = run_bass_kernel_spmd(nc, maps, core_ids=list(range(NCORES)))
    r = res.results
    y_prompt = np.concatenate([np.asarray(r[i]["yp"]).reshape(4, 256, D) for i in range(NCORES)], 0)
    y_sample = np.stack([np.asarray(r[i]["ys"]) for i in range(NCORES)], 0)
    nlru = np.concatenate([np.asarray(r[i]["nlru"]) for i in range(NCORES)], 0)
    nrwkv = np.concatenate([np.asarray(r[i]["nrwkv"]) for i in range(NCORES)], 0)
    return (y_prompt.astype(np.float32), y_sample.astype(np.float32), nlru.astype(np.float32), nrwkv.astype(np.float32))
```

```python
import numpy as np
import ml_dtypes
import concourse.bass as bass
import concourse.mybir as mybir
from concourse.bass_utils import run_bass_kernel_spmd

F32 = mybir.dt.float32
BF16 = mybir.dt.bfloat16
AF = mybir.ActivationFunctionType
ALU = mybir.AluOpType

D = 1024
KC = 8
NCORES = 8
TT = 512
ALPHA = 8.0 ** 0.25
LN_EPS = 1e-5


class _Rec:
    def __init__(self):
        self.calls = []

    def __getattr__(self, name):
        def f(*args, **kwargs):
            self.calls.append((name, args, kwargs))
            return self
        return f


class Prog:
    ENGS = ("pe", "act", "dve", "pool", "sp")

    def __init__(self, nc):
        self.nc = nc
        self.stream = {e: [] for e in self.ENGS}
        self.sem = {}
        self.cnt = {}
        self.known = {e: {} for e in self.ENGS}
        self.last_w = {}
        self.readers = {}
        self.dma_ring = {}
        self.ndma_sems = 16
        self.ninstr = 0

    def setup(self, stack):
        for e in ("pe", "act", "dve", "pool"):
            self.sem["c_" + e] = stack.enter_context(self.nc.semaphore("c_" + e))
            self.cnt["c_" + e] = 0
        for q in ("sp", "pool", "act"):
            names = []
            for i in range(self.ndma_sems):
                n = f"d_{q}{i}"
                self.sem[n] = stack.enter_context(self.nc.semaphore(n))
                self.cnt[n] = 0
                names.append(n)
            self.dma_ring[q] = [names, 0]

    def _need(self, eng, tickets):
        best = {}
        for t in tickets:
            if t is None:
                continue
            s, v = t
            if eng == "pe" and s == "c_pe":
                continue
            if self.known[eng].get(s, 0) >= v:
                continue
            if best.get(s, 0) < v:
                best[s] = v
        for s, v in best.items():
            self.known[eng][s] = v
            self.stream[eng].append(("wait", (s, v)))

    GLOBAL_KEYS = {"rwc", "rwcb", "ident", "identb", "rwprm", "rmask", "onesb", "rwder", "w2t", "e12", "wl"}
    GLOBAL_PREF = {"ps", "rwS", "mixT", "nrwkv", "xT", "hyS", "tokS", "FS", "LI", "wrw", "hT"}

    def _ns(self, keys):
        ns = getattr(self, "ns", None)
        if ns is None:
            return keys
        out = []
        for k in keys:
            if (isinstance(k, str) and k in self.GLOBAL_KEYS) or (isinstance(k, tuple) and k[0] in self.GLOBAL_PREF):
                out.append(k)
            else:
                out.append(("ns", ns, k))
        return out

    def _deps(self, reads, writes):
        ts = []
        for k in reads:
            ts.append(self.last_w.get(k))
        for k in writes:
            ts.append(self.last_w.get(k))
            ts.extend(self.readers.get(k, ()))
        return ts

    def _commit(self, ticket, reads, writes):
        for k in reads:
            self.readers.setdefault(k, []).append(ticket)
        for k in writes:
            self.last_w[k] = ticket
            self.readers[k] = []

    def op(self, eng, fns, reads=(), writes=(), serial=False):
        if callable(fns):
            fns = [fns]
        calls = []
        for f in fns:
            rec = _Rec()
            f(rec)
            assert len(rec.calls) == 1
            calls.append(rec.calls[0])
        fns = calls
        reads, writes = self._ns(reads), self._ns(writes)
        self._need(eng, self._deps(reads, writes))
        if serial and self.cnt["c_" + eng] > 0:
            sv = ("c_" + eng, self.cnt["c_" + eng])
            if self.known[eng].get(sv[0], 0) < sv[1]:
                self.known[eng][sv[0]] = sv[1]
                self.stream[eng].append(("wait", sv))
        s = "c_" + eng
        self.cnt[s] += 1
        ticket = (s, self.cnt[s])
        self.stream[eng].append(("ops", (fns, ticket)))
        self.ninstr += len(fns)
        self._commit(ticket, reads, writes)
        return ticket

    def dma(self, q, out, in_, reads=(), writes=(), **kw):
        names, idx = self.dma_ring[q]
        s = names[idx % len(names)]
        self.dma_ring[q][1] = idx + 1
        prev = (s, self.cnt[s]) if self.cnt[s] > 0 else None
        reads, writes = self._ns(reads), self._ns(writes)
        self._need(q, self._deps(reads, writes) + [prev])
        self.cnt[s] += 16
        ticket = (s, self.cnt[s])
        self.stream[q].append(("dma", (out, in_, kw, ticket)))
        self.ninstr += 1
        self._commit(ticket, reads, writes)
        return ticket

    def barrier(self):
        for e in self.ENGS:
            self._need(e, [(s, v) for s, v in self.cnt.items() if v > 0])
        self.last_w.clear()
        self.readers.clear()

    def finish(self):
        self._need("sp", [(s, v) for s, v in self.cnt.items() if v > 0])

    def emit(self, block):
        nc = self.nc
        sem = self.sem

        def run(eng_name):
            def body(e):
                for kind, pl in self.stream[eng_name]:
                    if kind == "wait":
                        e.wait_ge(sem[pl[0]], pl[1])
                    elif kind == "ops":
                        fns, (s, v) = pl
                        ins = None
                        for (name, args, kwargs) in fns:
                            ins = getattr(e, name)(*args, **kwargs)
                        ins.then_inc(sem[s], 1)
                    else:
                        out, in_, kw, (s, v) = pl
                        e.dma_start(out=out, in_=in_, **kw).then_inc(sem[s], 16)
            return body

        block.tensor(run("pe"))
        block.scalar(run("act"))
        block.vector(run("dve"))
        block.gpsimd(run("pool"))
        block.sync(run("sp"))


class Group:
    def __init__(self, name, tok0, ntok, nseq, L, line):
        self.name, self.tok0, self.ntok, self.nseq, self.L, self.line = name, tok0, ntok, nseq, L, line
        self.ntile = ntok // TT


class Builder:
    def __init__(self, depth=4, groups=("s", "p"), dbg=False, flags=()):
        self.flags = set(flags)
        self.depth = depth
        self.dbg = dbg
        self.nc = bass.Bass("TRN2", target_bir_lowering=False)
        self.groups = []
        if "s" in groups:
            self.groups.append(Group("s", 0, 4096, 1, 4096, 64))
        if "p" in groups:
            self.groups.append(Group("p", 4096, 1024, 4, 256, 256))
        self.NT = 5120
        self.uid = 0

    def din(self, name, shape, dt=F32):
        return self.nc.dram_tensor(name, list(shape), dt, kind="ExternalInput").ap()

    def dout(self, name, shape, dt=F32):
        return self.nc.dram_tensor(name, list(shape), dt, kind="ExternalOutput").ap()

    def dscr(self, name, shape, dt=F32):
        return self.nc.dram_tensor(name, list(shape), dt).ap()

    def sb(self, stack, name, shape, dt=F32):
        self.uid += 1
        return stack.enter_context(self.nc.sbuf_tensor(f"{name}_{self.uid}", list(shape), dt))

    def ring(self, stack, name, n, shape, dt=F32):
        return [self.sb(stack, f"{name}{i}", shape, dt) for i in range(n)]

    def dump(self, name, ap, reads, dt=F32):
        if not self.dbg:
            return
        shape = list(ap.shape)
        o = self.dout("dbg_" + name, shape, dt)
        self.P.dma("sp", o, ap, reads=reads, writes=[("dbg", name)])

    def next_ps(self):
        i = self.ps_i % (len(self.ps) - 1)
        self.ps_i += 1
        return self.ps[i], ("ps", i)

    def build(self):
        from contextlib import ExitStack
        nc = self.nc
        P = self.P = Prog(nc)
        I = self.I = {}
        O = self.O = {}
        I["xs"] = self.din("xs", [4096, D])
        I["xp"] = self.din("xp", [1024, D])
        I["st_lru"] = self.din("st_lru", [2, 2, 512])
        I["st_rwkv"] = self.din("st_rwkv", [2, 2, 8, 64, 64])
        I["cvecs"] = self.din("cvecs", [2, D])
        I["ident"] = self.din("ident", [128, 128])
        for nm, shp in WEIGHT_SHAPES.items():
            I[nm] = self.din(nm, shp)
        O["ys"] = self.dout("ys", [4096, D])
        O["yp"] = self.dout("yp", [1024, D])
        O["nlru"] = self.dout("nlru", [4, 2, 2, 512])
        O["nrwkv"] = self.dout("nrwkv", [4, 2, 2, 8, 64, 64])
        self.xT = self.dscr("xT_scr", [D, self.NT])
        self.mixT = self.dscr("mixT_scr", [D, self.NT], BF16)
        self.hyS = self.dscr("hyS_scr", [4, 512, self.NT])
        self.rwS = self.dscr("rwS_scr", [4, 11, 128, self.NT])
        self.tokS = self.dscr("tokS_scr", [128, self.NT // 128, 512], BF16)
        self.FS = self.dscr("FS_scr", [2, 34, 128, 2, 512])
        for nm, shp, dt in CONST_SPECS:
            I[nm] = self.din(nm, shp, dt)

        with ExitStack() as top:
            P.setup(top)
            self.ps = [top.enter_context(nc.psum_tensor(f"ps{i}", [128, 512], F32)) for i in range(8)]
            self.ps_i = 0
            self.ident = self.sb(top, "ident", [128, 128])
            P.dma("sp", self.ident[:], I["ident"][:, :], writes=["ident"])
            self.identb = self.sb(top, "identb", [128, 128], BF16)
            P.op("dve", lambda e: e.tensor_copy(out=self.identb[:], in_=self.ident[:]), reads=["ident"], writes=["identb"])
            self.onesb = self.sb(top, "onesb", [128, 128], BF16)
            P.op("dve", lambda e: e.memset(self.onesb[:], 1.0 / 1024.0), writes=["onesb"])
            self.modsT = self.sb(top, "modsT", [128, 4, 2, 48])
            self.lnp = self.sb(top, "lnp", [128, 4, 4, 8])
            for i, nm in enumerate(("ln1_g", "ln1_b", "ln2_g", "ln2_b")):
                for l in range(4):
                    P.dma("sp", self.lnp[:, l, i, :], I[nm][l].rearrange("(kc p) -> p kc", p=128),
                          writes=["lnp"], allow_slow_non_contiguous=True)

            self.epsT = self.sb(top, "epsT", [128, 1])
            P.op("dve", lambda e: e.memset(self.epsT[:], LN_EPS / (ALPHA * ALPHA)), writes=["epsT"])
            self.mods2 = self.sb(top, "mods2", [128, 4, 2, 32])
            self.phase_mods()
            self.phase_prepass()
            for l in range(self.depth):
                for g in self.groups:
                    if l % 2 == 0:
                        self.phase_ab(l, g)
                    else:
                        self.phase_cd(l, g)
                self.phase_dense(l, last=(l == self.depth - 1))
            P.finish()
            with nc.Block() as block:
                P.emit(block)
        return nc

    def phase_mods(self):
        from contextlib import ExitStack
        P, I = self.P, self.I
        with ExitStack() as st:
            cv = self.sb(st, "cv", [128, 2, 8])
            sl = self.sb(st, "sl", [128, 8, 2])
            bm = self.sb(st, "bm", [128, 4, 48])
            P.dma("sp", cv[:], I["cvecs"].rearrange("g (kc p) -> p g kc", p=128), writes=["cv"],
                  allow_slow_non_contiguous=True)
            P.dma("sp", bm[:], I["b_mod"].rearrange("l (m p) -> p l m", p=128), writes=["bm"],
                  allow_slow_non_contiguous=True)
            P.op("act", lambda e: e.activation(out=sl[:].rearrange("p kc g -> p g kc"), in_=cv[:], func=AF.Silu),
                 reads=["cv"], writes=["sl"])
            wbuf = self.ring(st, "wmod", 2, [128, 8, 1536])
            n = 0
            for l in range(self.depth):
                for s4 in range(4):
                    wb = wbuf[n % 2]
                    key = ("wmod", n % 2)
                    P.dma("sp", wb[:], I["w_mod"][l, :, s4 * 1536:(s4 + 1) * 1536].rearrange("(kc p) n -> p kc n", p=128),
                          writes=[key])
                    pt, pk = self.next_ps()
                    fns = []
                    for m in range(12):
                        for kc in range(8):
                            fns.append(lambda e, m=m, kc=kc, wb=wb, pt=pt: e.matmul(
                                pt[:, m * 2:m * 2 + 2], lhsT=wb[:, kc, m * 128:(m + 1) * 128], rhs=sl[:, kc, :],
                                start=(kc == 0), stop=(kc == 7)))
                    P.op("pe", fns, reads=[key, "sl"], writes=[pk])
                    for g in range(2):
                        P.op("dve", lambda e, g=g, l=l, s4=s4, pt=pt: e.tensor_tensor(
                            out=self.modsT[:, l, g, s4 * 12:(s4 + 1) * 12],
                            in0=pt[:, 0:24].rearrange("p (m g) -> p g m", g=2)[:, g, :],
                            in1=bm[:, l, s4 * 12:(s4 + 1) * 12], op=ALU.add),
                            reads=[pk, "bm"], writes=[("modsT", l, g, s4)])
                    n += 1
            allm = [("modsT", l, g, s4) for l in range(self.depth) for g in range(2) for s4 in range(4)]
            m2 = self.mods2
            dl = slice(0, self.depth)
            P.op("dve", lambda e: e.tensor_scalar_add(out=m2[:, dl, :, 0:8], in0=self.modsT[:, dl, :, 8:16], scalar1=1.0),
                 reads=allm, writes=["mods2a"])
            P.op("dve", lambda e: e.tensor_scalar_mul(out=m2[:, dl, :, 8:16], in0=self.modsT[:, dl, :, 16:24], scalar1=1.0 / ALPHA),
                 reads=allm, writes=["mods2b"])
            P.op("dve", lambda e: e.tensor_scalar_add(out=m2[:, dl, :, 16:24], in0=self.modsT[:, dl, :, 32:40], scalar1=1.0),
                 reads=allm, writes=["mods2c"])
            P.op("dve", lambda e: e.tensor_scalar_mul(out=m2[:, dl, :, 24:32], in0=self.modsT[:, dl, :, 40:48], scalar1=1.0 / ALPHA),
                 reads=allm, writes=["mods2d"])
            P.barrier()

    def xT_tile(self, gt0):
        return self.xT.rearrange("(kc p) t -> p kc t", p=128)[:, :, gt0:gt0 + TT]

    def mixT_tile(self, gt0):
        return self.mixT.rearrange("(kc p) t -> p kc t", p=128)[:, :, gt0:gt0 + TT]

    def phase_prepass(self):
        from contextlib import ExitStack
        P, I = self.P, self.I
        with ExitStack() as st:
            xin = self.ring(st, "xin", 2, [128, 4, D])
            xst = self.ring(st, "xst", 2, [128, 8, TT])
            n = 0
            for g in self.groups:
                src = I["xs"] if g.name == "s" else I["xp"]
                for t in range(g.ntile):
                    xi, xk = xin[n % 2], ("xin", n % 2)
                    xo, ok = xst[n % 2], ("xst", n % 2)
                    P.dma("sp", xi[:], src[t * TT:(t + 1) * TT, :].rearrange("(b p) d -> p b d", p=128), writes=[xk])
                    for kc in range(8):
                        pt, pk = self.next_ps()
                        P.op("pe", [lambda e, b=b, kc=kc, xi=xi, pt=pt: e.transpose(
                            pt[:, b * 128:(b + 1) * 128], xi[:, b, kc * 128:(kc + 1) * 128], self.ident[:]) for b in range(4)],
                            reads=[xk, "ident"], writes=[pk])
                        if kc % 2 == 0:
                            P.op("act", lambda e, kc=kc, xo=xo, pt=pt: e.copy(out=xo[:, kc, :], in_=pt[:]),
                                 reads=[pk], writes=[ok + (kc,)])
                        else:
                            P.op("dve", lambda e, kc=kc, xo=xo, pt=pt: e.tensor_copy(out=xo[:, kc, :], in_=pt[:]),
                                 reads=[pk], writes=[ok + (kc,)])
                    P.dma("sp", self.xT_tile(g.tok0 + t * TT), xo[:], reads=[ok + (kc,) for kc in range(8)],
                          writes=[("xT", g.tok0 + t * TT)])
                    n += 1
            P.barrier()

    def load_h(self, st, l, g, colmajor=False):
        P = self.P
        gi = 0 if g.name == "s" else 1
        from contextlib import ExitStack
        hT = self.sb(st, "hT", [128, 8, g.ntok], BF16)
        xst_ = ExitStack()
        xr = self.ring(xst_, "xr", 2, [128, 8, TT])
        for t in range(g.ntile):
            xt, xk = xr[t % 2], ("xr", t % 2)
            P.dma("sp", xt[:], self.xT_tile(g.tok0 + t * TT), reads=[("xT", g.tok0 + t * TT)], writes=[xk])
            for kc in range(8):
                if colmajor:
                    out = hT[:, kc, :].rearrange("p (c r) -> p r c", r=64)[:, t * 8:(t + 1) * 8, :]
                    in_ = xt[:, kc, :].rearrange("p (r c) -> p r c", c=64)
                else:
                    out = hT[:, kc, t * TT:(t + 1) * TT]
                    in_ = xt[:, kc, :]
                P.op("act", lambda e, out=out, in_=in_, kc=kc: e.activation(
                    out=out, in_=in_, func=AF.Identity,
                    scale=self.mods2[:, l, gi, kc:kc + 1], bias=self.modsT[:, l, gi, kc:kc + 1]),
                    reads=[xk], writes=[("hT", kc, t) if not colmajor else ("hT", kc, "all")])
        P.barrier()
        xst_.close()
        return hT

    def w_chunk(self, wt, key, src_cols):
        self.P.dma("pool", wt[:], src_cols.rearrange("(kc p) n -> p kc n", p=128), writes=[key])

    def mm_group(self, pt, pk, wt, wkey, rhs_fn, rkeys, nk=8):
        self.P.op("pe", [lambda e, kc=kc, rhs=rhs_fn(kc): e.matmul(pt, lhsT=wt[:, kc, :], rhs=rhs, start=(kc == 0), stop=(kc == nk - 1))
                         for kc in range(nk)], reads=[wkey] + rkeys, writes=[pk])

    def phase_ab(self, l, g):
        from contextlib import ExitStack
        P, I = self.P, self.I
        j = l // 2
        nl, ll = TT // g.line, g.line
        segs = [(s * g.L, (s + 1) * g.L) for s in range(TT // g.L)] if g.L < TT else None

        def lines(ap):
            return ap.rearrange("p (a b) -> p a b", b=ll)

        with ExitStack() as st:
            hT = self.load_h(st, l, g)
            hkeys = lambda t: [("hT", kc, t) for kc in range(8)]
            scw = self.sb(st, "scw", [128, 3, 4])
            lcw = self.sb(st, "lcw", [128, 4, 4])
            lcb = self.sb(st, "lcb", [128, 4])
            lba = self.sb(st, "lba", [128, 2, 4])
            lbi = self.sb(st, "lbi", [128, 2, 4])
            lam = self.sb(st, "lam", [128, 2, 4])
            h0 = self.sb(st, "h0", [128, 2, 4])
            for tl, nm, pat in ((scw, "sc_conv", "k (cc p) -> p k cc"), (lcw, "lru_conv", "k (cc p) -> p k cc"),
                                (lba, "lru_ba", "k (cc p) -> p k cc"), (lbi, "lru_bi", "k (cc p) -> p k cc"),
                                (lam, "lru_lambda", "k (cc p) -> p k cc")):
                P.dma("sp", tl[:], I[nm][j].rearrange(pat, p=128), writes=[nm], allow_slow_non_contiguous=True)
            P.dma("sp", lcb[:], I["lru_conv_b"][j].rearrange("(cc p) -> p cc", p=128), writes=["lcb"], allow_slow_non_contiguous=True)
            if g.name == "s":
                P.dma("sp", h0[:], I["st_lru"][j].rearrange("k (cc p) -> p k cc", p=128), writes=["h0"], allow_slow_non_contiguous=True)
            clam = self.sb(st, "clam", [128, 2, 4])
            P.op("act", lambda e: e.activation(out=clam[:], in_=lam[:], func=AF.Exp, scale=-1.0), reads=["lru_lambda"], writes=["clam"])
            P.op("act", lambda e: e.activation(out=clam[:], in_=clam[:], func=AF.Ln, bias=1.0), reads=["clam"], writes=["clam"])
            P.op("dve", lambda e: e.tensor_scalar_mul(out=clam[:], in0=clam[:], scalar1=-8.0), reads=["clam"], writes=["clam"])
            wg = self.sb(st, "wg", [128, 2, 2, 4, 128], BF16)
            P.op("dve", lambda e: e.memset(wg[:], 0.0), writes=["wg"])
            for gate, nm in enumerate(("lru_wa", "lru_wi")):
                for d in range(2):
                    for par in range(2):
                        P.dma("pool", wg[par * 64:(par + 1) * 64, d, gate, :, par * 64:(par + 1) * 64],
                              I[nm][j, d].rearrange("(bc par) i o -> par i bc o", par=2)[par],
                              reads=[], writes=["wg"])
            wr = self.ring(st, "wab", 6, [128, 8, 128], BF16)
            self.wn = getattr(self, "wn", 0)

            def getw(col0):
                wt, wk = wr[self.wn % 6], ("wab", self.wn % 6)
                self.wn += 1
                self.w_chunk(wt, wk, I["ab_w_in"][j, :, col0:col0 + 128])
                return wt, wk

            t32 = self.ring(st, "t32", 6, [128, TT])
            self.tn = 0

            def tmp():
                tl, tk = t32[self.tn % 6], ("t32", self.tn % 6)
                self.tn += 1
                return tl, tk
            mst = self.ring(st, "mst", 2, [128, TT], BF16)
            self.mn = 0

            for cc in range(4):
                wb_, wc_, wv_ = getw(cc * 128), getw(512 + cc * 128), getw(1024 + cc * 128)
                for t in range(g.ntile):
                    hs = lambda kc, t=t: hT[:, kc, t * TT:(t + 1) * TT]
                    pb, pbk = self.next_ps(); pc, pck = self.next_ps(); pv, pvk = self.next_ps()
                    self.mm_group(pb[:], pbk, wb_[0], wb_[1], hs, hkeys(t))
                    self.mm_group(pc[:], pck, wc_[0], wc_[1], hs, hkeys(t))
                    self.mm_group(pv[:], pvk, wv_[0], wv_[1], hs, hkeys(t))
                    sc, sck = tmp(); pp, ppk = tmp(); ac, ack = tmp()
                    P.op("act", lambda e, sc=sc, pc=pc: e.copy(out=sc[:], in_=pc[:]), reads=[pck], writes=[sck])
                    P.op("dve", lambda e, pp=pp, pv=pv, sc=sc: e.tensor_tensor(out=pp[:], in0=pv[:], in1=sc[:], op=ALU.mult),
                         reads=[pvk, sck], writes=[ppk])
                    P.op("dve", lambda e, ac=ac, pp=pp, cc=cc: e.tensor_scalar_mul(out=ac[:], in0=pp[:], scalar1=scw[:, 1, cc:cc + 1]),
                         reads=[ppk, "sc_conv"], writes=[ack])
                    P.op("dve", lambda e, ac=ac, pp=pp, cc=cc: e.scalar_tensor_tensor(
                        out=lines(ac[:])[:, :, 1:], in0=lines(pp[:])[:, :, :ll - 1], scalar=scw[:, 0, cc:cc + 1],
                        in1=lines(ac[:])[:, :, 1:], op0=ALU.mult, op1=ALU.add), reads=[ppk, ack], writes=[ack])
                    P.op("dve", lambda e, ac=ac, pp=pp, cc=cc: e.scalar_tensor_tensor(
                        out=lines(ac[:])[:, :, :ll - 1], in0=lines(pp[:])[:, :, 1:], scalar=scw[:, 2, cc:cc + 1],
                        in1=lines(ac[:])[:, :, :ll - 1], op0=ALU.mult, op1=ALU.add), reads=[ppk, ack], writes=[ack])
                    ms, msk = mst[self.mn % 2], ("mst", self.mn % 2); self.mn += 1
                    P.op("dve", lambda e, ms=ms, pb=pb, ac=ac: e.tensor_tensor(out=ms[:], in0=pb[:], in1=ac[:], op=ALU.mult),
                         reads=[pbk, ack], writes=[msk])
                    gt0 = g.tok0 + t * TT
                    P.dma("sp", self.mixT[cc * 128:(cc + 1) * 128, gt0:gt0 + TT], ms[:], reads=[msk], writes=[("mixT", cc, gt0)])

            XC = self.sb(st, "XC", [128, g.ntok])
            XCB = self.sb(st, "XCB", [128, g.ntok], BF16)
            GG = self.sb(st, "GG", [128, g.ntok], BF16)
            HF = self.sb(st, "HF", [128, g.ntok])
            hbr = self.ring(st, "hb", 2, [128, TT])
            for bc in range(4):
                wg_, wx_ = getw(1536 + bc * 128), getw(2048 + bc * 128)

                def gates(d, t, bc=bc):
                    sl_ = slice(t * TT, (t + 1) * TT)
                    pr, prk = self.next_ps(); pi, pik = self.next_ps()
                    P.op("pe", lambda e: e.matmul(pr[:], lhsT=wg[:, d, 0, bc, :], rhs=XCB[:, sl_], start=True, stop=True),
                         reads=["wg", ("XCB", t)], writes=[prk])
                    P.op("pe", lambda e: e.matmul(pi[:], lhsT=wg[:, d, 1, bc, :], rhs=XCB[:, sl_], start=True, stop=True),
                         reads=["wg", ("XCB", t)], writes=[pik])
                    r_, rk = tmp(); a_, ak = tmp(); i_, ik = tmp(); s_, sk = tmp(); u_, uk = tmp()
                    P.op("act", lambda e: e.activation(out=r_[:], in_=pr[:], func=AF.Sigmoid, bias=lba[:, d, bc:bc + 1]),
                         reads=[prk, "lru_ba"], writes=[rk])
                    P.op("act", lambda e: e.activation(out=i_[:], in_=pi[:], func=AF.Sigmoid, bias=lbi[:, d, bc:bc + 1]),
                         reads=[pik, "lru_bi"], writes=[ik])
                    P.op("act", lambda e: e.activation(out=a_[:], in_=r_[:], func=AF.Exp, scale=clam[:, d, bc:bc + 1]),
                         reads=[rk, "clam"], writes=[ak])
                    P.op("act", lambda e: e.activation(out=s_[:], in_=a_[:], func=AF.Square), reads=[ak], writes=[sk])
                    P.op("act", lambda e: e.activation(out=s_[:], in_=s_[:], func=AF.Sqrt, scale=-1.0, bias=1.0), reads=[sk], writes=[sk])
                    P.op("dve", lambda e: e.tensor_tensor(out=u_[:], in0=i_[:], in1=XC[:, sl_], op=ALU.mult),
                         reads=[ik, ("XC", t)], writes=[uk])
                    P.op("dve", lambda e: e.tensor_tensor(out=u_[:], in0=u_[:], in1=s_[:], op=ALU.mult), reads=[uk, sk], writes=[uk])
                    return a_, ak, u_, uk

                for t in range(g.ntile):
                    sl_ = slice(t * TT, (t + 1) * TT)
                    hs = lambda kc, t=t: hT[:, kc, t * TT:(t + 1) * TT]
                    pg, pgk = self.next_ps(); px, pxk = self.next_ps()
                    self.mm_group(pg[:], pgk, wg_[0], wg_[1], hs, hkeys(t))
                    self.mm_group(px[:], pxk, wx_[0], wx_[1], hs, hkeys(t))
                    P.op("act", lambda e, pg=pg, sl_=sl_: e.activation(out=GG[:, sl_], in_=pg[:], func=AF.Gelu), reads=[pgk], writes=[("GG", t)])
                    xl, xlk = tmp()
                    P.op("act", lambda e, px=px, xl=xl: e.copy(out=xl[:], in_=px[:]), reads=[pxk], writes=[xlk])
                    xck = ("XC", t)
                    P.op("dve", lambda e, xl=xl, sl_=sl_, bc=bc: e.tensor_scalar(
                        out=XC[:, sl_], in0=xl[:], scalar1=lcw[:, 2, bc:bc + 1], scalar2=lcb[:, bc:bc + 1], op0=ALU.mult, op1=ALU.add),
                        reads=[xlk, "lru_conv", "lcb"], writes=[xck])
                    for tap, off in ((0, -2), (1, -1), (3, 1)):
                        if off < 0:
                            o_ = lines(XC[:, sl_])[:, :, -off:]; i_ = lines(xl[:])[:, :, :ll + off]
                        else:
                            o_ = lines(XC[:, sl_])[:, :, :ll - off]; i_ = lines(xl[:])[:, :, off:]
                        P.op("dve", lambda e, o_=o_, i_=i_, tap=tap, bc=bc: e.scalar_tensor_tensor(
                            out=o_, in0=i_, scalar=lcw[:, tap, bc:bc + 1], in1=o_, op0=ALU.mult, op1=ALU.add),
                            reads=[xlk, xck], writes=[xck])
                    P.op("act", lambda e, sl_=sl_: e.copy(out=XCB[:, sl_], in_=XC[:, sl_]), reads=[xck], writes=[("XCB", t)])
                    a_, ak, u_, uk = gates(0, t)
                    for (s0, s1) in (segs or [(0, TT)]):
                        if g.name == "s":
                            init = h0[:, 0, bc:bc + 1] if t == 0 else HF[:, t * TT - 1:t * TT]
                            rk_ = ["h0"] if t == 0 else [("HF", t - 1)]
                        else:
                            init, rk_ = 0.0, []
                        P.op("dve", lambda e, s0=s0, s1=s1, init=init, a_=a_, u_=u_, t=t: e.tensor_tensor_scan(
                            out=HF[:, t * TT + s0:t * TT + s1], data0=a_[:, s0:s1], data1=u_[:, s0:s1], initial=init,
                            op0=ALU.mult, op1=ALU.add), reads=[ak, uk] + rk_, writes=[("HF", t)])
                prev_hb = None
                for t in reversed(range(g.ntile)):
                    sl_ = slice(t * TT, (t + 1) * TT)
                    a_, ak, u_, uk = gates(1, t)
                    hb, hbk = hbr[t % 2], ("hb", t % 2)
                    for (s0, s1) in (segs or [(0, TT)]):
                        if g.name == "s":
                            init = h0[:, 1, bc:bc + 1] if t == g.ntile - 1 else prev_hb[0][:, 0:1]
                            rk_ = ["h0"] if t == g.ntile - 1 else [prev_hb[1]]
                        else:
                            init, rk_ = 0.0, []
                        P.op("dve", lambda e, s0=s0, s1=s1, init=init, a_=a_, u_=u_, hb=hb: e.tensor_tensor_scan(
                            out=hb[:, s0:s1][:, ::-1], data0=a_[:, s0:s1][:, ::-1], data1=u_[:, s0:s1][:, ::-1], initial=init,
                            op0=ALU.mult, op1=ALU.add), reads=[ak, uk] + rk_, writes=[hbk])
                    prev_hb = (hb, hbk)
                    if g.name == "p":
                        for si, (s0, s1) in enumerate(segs):
                            b = t * (TT // g.L) + si
                            P.dma("sp", self.O["nlru"][b, j, 0, bc * 128:(bc + 1) * 128].rearrange("(p o) -> p o", o=1),
                                  HF[:, t * TT + s1 - 1:t * TT + s1], reads=[("HF", t)], writes=[("nlru", b, j, 0, bc)])
                            P.dma("sp", self.O["nlru"][b, j, 1, bc * 128:(bc + 1) * 128].rearrange("(p o) -> p o", o=1),
                                  hb[:, s0:s0 + 1], reads=[hbk], writes=[("nlru", b, j, 1, bc)])
                    y_, yk = tmp()
                    P.op("dve", lambda e, y_=y_, hb=hb, sl_=sl_: e.tensor_tensor(out=y_[:], in0=HF[:, sl_], in1=hb[:], op=ALU.add),
                         reads=[("HF", t), hbk], writes=[yk])
                    ms, msk = mst[self.mn % 2], ("mst", self.mn % 2); self.mn += 1
                    P.op("dve", lambda e, ms=ms, y_=y_, sl_=sl_: e.tensor_tensor(out=ms[:], in0=y_[:], in1=GG[:, sl_], op=ALU.mult),
                         reads=[yk, ("GG", t)], writes=[msk])
                    gt0 = g.tok0 + t * TT
                    P.dma("sp", self.mixT[512 + bc * 128:512 + (bc + 1) * 128, gt0:gt0 + TT], ms[:], reads=[msk],
                          writes=[("mixT", 4 + bc, gt0)])
            P.barrier()


    def ln_tile(self, xt, xk, l, which, scr):
        P = self.P
        vb, sq, mean_sb, m2, rstd = scr
        keys = [xk + (kc,) for kc in range(8)]
        P.op("dve", lambda e: e.tensor_copy(out=vb[:], in_=xt[:]), reads=keys, writes=["ln_vb"])
        P.op("act", lambda e: e.activation(out=sq[:], in_=xt[:], func=AF.Square), reads=keys, writes=["ln_sq"])
        pm, pmk = self.next_ps(); pe2, pe2k = self.next_ps()
        P.op("pe", [lambda e, kc=kc: e.matmul(pm[:], lhsT=self.onesb[:], rhs=vb[:, kc, :], start=(kc == 0), stop=(kc == 7)) for kc in range(8)],
             reads=["onesb", "ln_vb"], writes=[pmk])
        P.op("pe", [lambda e, kc=kc: e.matmul(pe2[:], lhsT=self.onesb[:], rhs=sq[:, kc, :], start=(kc == 0), stop=(kc == 7)) for kc in range(8)],
             reads=["onesb", "ln_sq"], writes=[pe2k])
        P.op("act", lambda e: e.copy(out=mean_sb[:], in_=pm[:]), reads=[pmk], writes=["ln_mean"])
        P.op("dve", lambda e: e.tensor_tensor(out=m2[:], in0=mean_sb[:], in1=mean_sb[:], op=ALU.mult), reads=["ln_mean"], writes=["ln_m2"])
        P.op("dve", lambda e: e.tensor_tensor(out=m2[:], in0=pe2[:], in1=m2[:], op=ALU.subtract), reads=[pe2k, "ln_m2"], writes=["ln_m2"])
        P.op("act", lambda e: e.activation(out=rstd[:], in_=m2[:], func=AF.Sqrt, bias=self.epsT[:, 0:1]), reads=["ln_m2", "epsT"], writes=["ln_rstd"])
        P.op("dve", lambda e: e.reciprocal(out=rstd[:], in_=rstd[:]), reads=["ln_rstd"], writes=["ln_rstd"])
        bc_ = lambda ap: ap[:].rearrange("p (o t) -> p o t", o=1).to_broadcast([128, 8, TT])
        P.op("dve", lambda e: e.tensor_tensor(out=xt[:], in0=xt[:], in1=bc_(mean_sb), op=ALU.subtract), reads=keys + ["ln_mean"], writes=keys)
        P.op("dve", lambda e: e.tensor_tensor(out=xt[:], in0=xt[:], in1=bc_(rstd), op=ALU.mult), reads=keys + ["ln_rstd"], writes=keys)
        gi, bi = (0, 1) if which == 1 else (2, 3)
        for kc in range(8):
            P.op("dve", lambda e, kc=kc: e.tensor_scalar(out=xt[:, kc, :], in0=xt[:, kc, :], scalar1=self.lnp[:, l, gi, kc:kc + 1],
                                                        scalar2=self.lnp[:, l, bi, kc:kc + 1], op0=ALU.mult, op1=ALU.add),
                 reads=[keys[kc], "lnp"], writes=[keys[kc]])

    def phase_dense(self, l, last):
        from contextlib import ExitStack
        P, I, O = self.P, self.I, self.O
        with ExitStack() as st:
            xr = self.ring(st, "dx", 2, [128, 8, TT])
            mr = self.ring(st, "dm", 2, [128, 8, TT], BF16)
            h2 = self.sb(st, "h2", [128, 8, TT], BF16)
            hid = self.sb(st, "hid", [128, 32, TT], BF16)
            rt = self.ring(st, "rt", 3, [128, TT], BF16)
            scr = (self.sb(st, "ln_vb", [128, 8, TT], BF16), self.sb(st, "ln_sq", [128, 8, TT], BF16),
                   self.sb(st, "ln_mean", [128, TT]), self.sb(st, "ln_m2", [128, TT]), self.sb(st, "ln_rstd", [128, TT]))
            ws = self.ring(st, "wsm", 6, [128, 8, 128], BF16)
            w2r = self.ring(st, "w2r", 3, [128, 32, 128], BF16)
            ot = self.ring(st, "ot", 2, [128, D]) if last else None
            tiles = [(g, t) for g in self.groups for t in range(g.ntile)]
            cnt = {'wn': 0, 'w2n': 0, 'on': 0}

            def dense_tile(n, g, t):
                gi = 0 if g.name == "s" else 1
                gt0 = g.tok0 + t * TT
                xt, xk = xr[n % 2], ("dx", n % 2)
                mt, mk = mr[n % 2], ("dm", n % 2)
                xkeys = [xk + (kc,) for kc in range(8)]
                P.dma("sp", xt[:], self.xT_tile(gt0), reads=[("xT", gt0)], writes=xkeys)
                P.dma("sp", mt[:], self.mixT_tile(gt0), reads=[("mixT", c, gt0) for c in range(8)], writes=[mk])
                for m in range(8):
                    wt, wk = ws[cnt['wn'] % 6], ("wsm", cnt['wn'] % 6); cnt['wn'] += 1
                    self.w_chunk(wt, wk, I["w_out"][l, :, m * 128:(m + 1) * 128])
                    pt, pk = self.next_ps()
                    self.mm_group(pt[:], pk, wt, wk, lambda kc: mt[:, kc, :], [mk])
                    P.op("dve", lambda e, m=m, pt=pt: e.scalar_tensor_tensor(
                        out=xt[:, m, :], in0=pt[:], scalar=self.mods2[:, l, gi, 8 + m:9 + m], in1=xt[:, m, :],
                        op0=ALU.mult, op1=ALU.add), reads=[pk, xkeys[m], "mods2b"], writes=[xkeys[m]])
                if n == 0 and l == 0:
                    self.dump('mods2', self.mods2[:], ['mods2a', 'mods2b', 'mods2c', 'mods2d'])
                    self.dump('modsT', self.modsT[:], [])
                    self.dump('v1', xt[:], xkeys)
                    self.dump('mt', mt[:], [mk], BF16)
                if n == 0 and l > 0:
                    self.dump('mt_l%d' % l, mt[:], [mk], BF16)
                self.ln_tile(xt, xk, l, 1, scr)
                if n == 0 and l == 0:
                    self.dump('x1', xt[:], xkeys)
                for kc in range(8):
                    P.op("act", lambda e, kc=kc: e.activation(out=h2[:, kc, :], in_=xt[:, kc, :], func=AF.Identity,
                                                              scale=self.mods2[:, l, gi, 16 + kc:17 + kc],
                                                              bias=self.modsT[:, l, gi, 24 + kc:25 + kc]),
                         reads=[xkeys[kc]], writes=[("h2", kc)])
                h2k = [("h2", kc) for kc in range(8)]
                for hc in range(32):
                    wt, wk = ws[cnt['wn'] % 6], ("wsm", cnt['wn'] % 6); cnt['wn'] += 1
                    self.w_chunk(wt, wk, I["mlp_w1"][l, :, hc * 128:(hc + 1) * 128])
                    pt, pk = self.next_ps()
                    self.mm_group(pt[:], pk, wt, wk, lambda kc: h2[:, kc, :], h2k)
                    r_, rk = rt[hc % 3], ("rt", hc % 3)
                    P.op("act", lambda e, r_=r_, pt=pt: e.activation(out=r_[:], in_=pt[:], func=AF.Relu), reads=[pk], writes=[rk])
                    P.op("dve", lambda e, r_=r_, hc=hc, pt=pt: e.tensor_tensor(out=hid[:, hc, :], in0=pt[:], in1=r_[:], op=ALU.mult),
                         reads=[rk, pk], writes=[("hid", hc)])
                hidk = [("hid", hc) for hc in range(32)]
                for m in range(8):
                    wt, wk = w2r[cnt['w2n'] % 3], ("w2r", cnt['w2n'] % 3); cnt['w2n'] += 1
                    P.dma("pool", wt[:], I["mlp_w2"][l, :, m * 128:(m + 1) * 128].rearrange("(kc p) n -> p kc n", p=128), writes=[wk])
                    pt, pk = self.next_ps()
                    self.mm_group(pt[:], pk, wt, wk, lambda kc: hid[:, kc, :], hidk, nk=32)
                    P.op("dve", lambda e, m=m, pt=pt: e.scalar_tensor_tensor(
                        out=xt[:, m, :], in0=pt[:], scalar=self.mods2[:, l, gi, 24 + m:25 + m], in1=xt[:, m, :],
                        op0=ALU.mult, op1=ALU.add), reads=[pk, xkeys[m], "mods2d"], writes=[xkeys[m]])
                if n == 0 and l == 0:
                    self.dump('v2', xt[:], xkeys)
                    self.dump('hid', hid[:], hidk, BF16)
                self.ln_tile(xt, xk, l, 2, scr)
                if n == 0 and l == 0:
                    self.dump('x2', xt[:], xkeys)
                if not last:
                    P.dma("sp", self.xT_tile(gt0), xt[:], reads=xkeys, writes=[("xT", gt0)])
                else:
                    dst = O["ys"] if g.name == "s" else O["yp"]
                    for b in range(4):
                        o_, ok = ot[cnt['on'] % 2], ("ot", cnt['on'] % 2); cnt['on'] += 1
                        for half in range(2):
                            pt, pk = self.next_ps()
                            P.op("pe", [lambda e, q=q, pt=pt, b=b, half=half: e.transpose(
                                pt[:, q * 128:(q + 1) * 128], xt[:, half * 4 + q, b * 128:(b + 1) * 128], self.ident[:]) for q in range(4)],
                                reads=xkeys[half * 4:half * 4 + 4] + ["ident"], writes=[pk])
                            if half == 0:
                                P.op("act", lambda e, o_=o_, pt=pt: e.copy(out=o_[:, 0:512], in_=pt[:]), reads=[pk], writes=[ok + (0,)])
                            else:
                                P.op("dve", lambda e, o_=o_, pt=pt: e.tensor_copy(out=o_[:, 512:1024], in_=pt[:]), reads=[pk], writes=[ok + (1,)])
                        r0 = t * TT + b * 128
                        P.dma("sp", dst[r0:r0 + 128, :], o_[:], reads=[ok + (0,), ok + (1,)], writes=[("out", g.name, r0)])

            for n, (g, t) in enumerate(tiles):
                dense_tile(n, g, t)
            P.barrier()

    def phase_cd(self, l, g):
        from contextlib import ExitStack
        P, I = self.P, self.I
        j = l // 2
        with ExitStack() as st:
            nblk = g.L // 128
            with ExitStack() as st2:
                rw = self.rwkv_setup(st2, l, g)
                hT = self.load_h(st2, l, g, colmajor=(g.name == "s"))
                with ExitStack() as st3:
                    self.cd_hyena_proj(st3, l, g, hT)
                    P.barrier()
                if "norwkv" not in self.flags:
                    self.rwkv_partA(st2, l, g, hT, rw)
                P.barrier()
                if l == 1 and g.name == "p" and self.dbg:
                    for hp_ in range(4):
                        self.dump("rwS%d" % hp_, self.rwS[hp_, :, :, 4096:5120], [])
            if "norwkv" not in self.flags:
                with ExitStack() as st4:
                    rw = self.rwkv_setup(st4, l, g)
                    self.rwkv_partB(st4, l, g, rw)
                    P.barrier()
            if "nohyconv" not in self.flags:
                with ExitStack() as st5:
                    self.cd_hyena_conv(st5, l, g)
                    P.barrier()

    def hkeys(self, g, t):
        if g.name == "s":
            return [("hT", kc, "all") for kc in range(8)]
        return [("hT", kc, t) for kc in range(8)]

    def to_tok(self, src_bf, skey, dst, dkeys_fn, nb):
        P = self.P
        pt, pk = self.next_ps()
        pv = pt[:].bitcast(BF16)
        P.op("pe", [lambda e, b=b: e.transpose(pv[:, b * 128:(b + 1) * 128], src_bf[:, b * 128:(b + 1) * 128], self.identb[:]) for b in range(nb)],
             reads=[skey, "identb"], writes=[pk])
        P.op("act", lambda e: e.copy(out=dst, in_=pv[:, 0:nb * 128].rearrange("p (b c) -> p b c", c=128)), reads=[pk], writes=dkeys_fn)

    def cd_hyena_proj(self, st, l, g, hT):
        P, I = self.P, self.I
        j = l // 2
        nl, ll = TT // g.line, g.line
        lines = lambda ap: ap.rearrange("p (a b) -> p a b", b=ll)
        hyw = self.sb(st, "hyw", [128, 3, 12])
        P.dma("sp", hyw[:], I["hy_conv"][j].rearrange("k (q p) -> p k q", p=128), writes=["hyw"], allow_slow_non_contiguous=True)
        wr = self.ring(st, "whp", 3, [128, 8, 128], BF16)
        xl_r = self.ring(st, "hxl", 2, [128, TT])
        u_r = self.ring(st, "hu", 2, [128, TT])
        ub_r = self.ring(st, "hub", 2, [128, TT], BF16)
        tokr = self.ring(st, "tokst", 2, [128, 4, 128], BF16)
        cnt = {"n": 0}
        nblk = g.L // 128

        def tile(q, t, wt, wk):
            which, cc = q // 4, q % 4
            n = cnt["n"]; cnt["n"] += 1
            pt, pk = self.next_ps()
            self.mm_group(pt[:], pk, wt, wk, lambda kc: hT[:, kc, t * TT:(t + 1) * TT], self.hkeys(g, t))
            xl, xlk = xl_r[n % 2], ("hxl", n % 2)
            u, uk = u_r[n % 2], ("hu", n % 2)
            P.op("act", lambda e: e.copy(out=xl[:], in_=pt[:]), reads=[pk], writes=[xlk])
            P.op("dve", lambda e: e.tensor_scalar_mul(out=u[:], in0=xl[:], scalar1=hyw[:, 1, q:q + 1]), reads=[xlk, "hyw"], writes=[uk])
            P.op("dve", lambda e: e.scalar_tensor_tensor(out=lines(u[:])[:, :, 1:], in0=lines(xl[:])[:, :, :ll - 1], scalar=hyw[:, 0, q:q + 1],
                                                         in1=lines(u[:])[:, :, 1:], op0=ALU.mult, op1=ALU.add), reads=[xlk, uk], writes=[uk])
            P.op("dve", lambda e: e.scalar_tensor_tensor(out=lines(u[:])[:, :, :ll - 1], in0=lines(xl[:])[:, :, 1:], scalar=hyw[:, 2, q:q + 1],
                                                         in1=lines(u[:])[:, :, :ll - 1], op0=ALU.mult, op1=ALU.add), reads=[xlk, uk], writes=[uk])
            gt0 = g.tok0 + t * TT
            P.dma("sp", self.hyS[which, cc * 128:(cc + 1) * 128, gt0:gt0 + TT], u[:], reads=[uk], writes=[("hyS", which, cc, gt0)])
            if which == 0:
                ub, ubk = ub_r[n % 2], ("hub", n % 2)
                P.op("act", lambda e: e.copy(out=ub[:], in_=u[:]), reads=[uk], writes=[ubk])
                tk_, tkk = tokr[n % 2], ("tokst", n % 2)
                self.to_tok(ub, ubk, tk_[:], [tkk], 4)
                gb0 = g.tok0 // 128 + t * 4
                P.dma("sp", self.tokS[:, gb0:gb0 + 4, cc * 128:(cc + 1) * 128], tk_[:], reads=[tkk], writes=[("tokS", gb0, cc)])

        for q in range(12):
            wt, wk = wr[q % 3], ("whp", q % 3)
            self.w_chunk(wt, wk, I["cd_w_in"][j, :, q * 128:(q + 1) * 128])
            for t in range(g.ntile):
                tile(q, t, wt, wk)

    def cd_hyena_conv(self, st, l, g):
        from contextlib import ExitStack
        P, I = self.P, self.I
        j = l // 2
        L, nseq = g.L, g.nseq
        nblk = L // 128
        N2 = 2 * L
        sfx = "4096" if L == 4096 else "256"
        TI = min(512, L)
        ntt = L // TI
        KG = min(8, nblk)
        HW = 512
        kb0 = 0 if g.name == "s" else 32
        hbias = self.sb(st, "hbias", [128, 2, 4])
        P.dma("sp", hbias[:], I["hy_bias"][j].rearrange("o (cc p) -> p o cc", p=128), writes=["hbias"], allow_slow_non_contiguous=True)
        eps6 = self.sb(st, "eps6", [128, 1])
        P.op("dve", lambda e: e.memset(eps6[:], 1e-6), writes=["eps6"])
        ones32 = self.sb(st, "ones32", [128, 128])
        P.op("dve", lambda e: e.memset(ones32[:], 1.0), writes=["ones32"])
        tcn = {"n": 0}

        def mk_table(tb):
            def table(src_ap, shape):
                n = tcn["n"]; tcn["n"] += 1
                tl, tk = tb[n % len(tb)], ("dft", n % len(tb))
                v = tl[:, 0:shape[0] * shape[1]].rearrange("p (a b) -> p a b", b=shape[1])
                P.dma("sp", v, src_ap, writes=[tk])
                return v, tk
            return table

        with ExitStack() as sF:
            GSD = self.sb(sF, "GSD", [128, 2 * nblk * HW], BF16)
            GS = GSD[:, 0:nblk * HW].rearrange("p (b c) -> p b c", c=HW)
            GD = GSD[:, nblk * HW:2 * nblk * HW].rearrange("p (b c) -> p b c", c=HW)
            rnb = self.sb(sF, "rnb", [128, HW])
            def gen_filters(o, half):
                c0 = half * HW
                with ExitStack() as s2:
                    fT = self.sb(s2, "fT", [33, L])
                    P.dma("sp", fT[:], I["featT" + sfx][:, :], writes=["fT"])
                    prm = self.sb(s2, "hyprm", [64, 6])
                    P.dma("sp", prm[:, 0:1], I["hy_freq"][j].rearrange("(p o) -> p o", o=1), writes=["hyprm0"])
                    P.dma("sp", prm[:, 1:2], I["hy_b1"][j].rearrange("(p o) -> p o", o=1), writes=["hyprm1"])
                    P.dma("sp", prm[:, 2:3], I["hy_b2"][j].rearrange("(p o) -> p o", o=1), writes=["hyprm2"])
                    P.op("dve", lambda e: e.tensor_tensor(out=prm[:, 3:5], in0=prm[:, 1:3], in1=prm[:, 0:1].to_broadcast([64, 2]), op=ALU.mult),
                         reads=["hyprm0", "hyprm1", "hyprm2"], writes=["hyprm3"])
                    w1 = self.sb(s2, "hw1", [33, 64]); w2 = self.sb(s2, "hw2", [64, 64]); w3 = self.sb(s2, "hw3", [64, 2048])
                    P.dma("sp", w1[:], I["hy_w1"][j], writes=["hw1"]); P.dma("sp", w2[:], I["hy_w2"][j], writes=["hw2"])
                    P.dma("sp", w3[:], I["hy_w3"][j], writes=["hw3"])
                    ntn = self.sb(s2, "ntn", [128, nblk]); dlt = self.sb(s2, "dlt", [128, 512]); m1 = self.sb(s2, "m1c", [128, 1])
                    P.dma("sp", ntn[:], I["ntn" + sfx][:, :], writes=["ntn"]); P.dma("sp", dlt[:], I["delta_b"][:, :], writes=["dlt"])
                    P.dma("sp", m1[:], I["m1col"][:, :], writes=["m1c"])
                    H1 = self.sb(s2, "H1", [64, L]); H2 = self.sb(s2, "H2", [64, L])
                    tmp = self.ring(s2, "ftmp", 4, [128, 512])
                    hw_ = slice(0, HW)
                    MAGIC = 12582912.0
                    TWO_PI = float(2.0 * np.pi)

                    def sin_layer(dst, wmat, kdim, src, bcol, nm):
                        for c0 in range(0, L, 512):
                            cw = min(512, L - c0)
                            pt, pk = self.next_ps()
                            P.op("pe", lambda e: e.matmul(pt[0:64, 0:cw], lhsT=wmat[:], rhs=src[0:kdim, c0:c0 + cw], start=True, stop=True),
                                 reads=[nm[0], nm[1]], writes=[pk])
                            a, b = tmp[0], tmp[1]
                            P.op("act", lambda e: e.activation(out=a[0:64, 0:cw], in_=pt[0:64, 0:cw], func=AF.Identity,
                                                               scale=prm[:, 0:1], bias=prm[:, bcol:bcol + 1]), reads=[pk, "hyprm3", "hyprm0"], writes=["ft0"])
                            P.op("dve", lambda e: e.tensor_scalar(out=b[0:64, 0:cw], in0=a[0:64, 0:cw], scalar1=1.0 / TWO_PI, scalar2=MAGIC,
                                                                  op0=ALU.mult, op1=ALU.add), reads=["ft0"], writes=["ft1"])
                            P.op("dve", lambda e: e.tensor_scalar_add(out=b[0:64, 0:cw], in0=b[0:64, 0:cw], scalar1=-MAGIC), reads=["ft1"], writes=["ft1"])
                            P.op("dve", lambda e: e.scalar_tensor_tensor(out=a[0:64, 0:cw], in0=b[0:64, 0:cw], scalar=-TWO_PI, in1=a[0:64, 0:cw],
                                                                         op0=ALU.mult, op1=ALU.add), reads=["ft0", "ft1"], writes=["ft0"])
                            P.op("act", lambda e: e.activation(out=dst[:, c0:c0 + cw], in_=a[0:64, 0:cw], func=AF.Sin), reads=["ft0"], writes=[nm[2]])

                    sin_layer(H1, w1, 33, fT, 3, ("hw1", "fT", "H1"))
                    sin_layer(H2, w2, 64, H1, 4, ("hw2", "H1", "H2"))
                    acc, acck = self.ps[7], ("ps", 7)
                    for blk in range(nblk):
                        dec, g0, g1, sq = tmp[0], tmp[1], tmp[2], tmp[3]
                        P.op("act", lambda e, blk=blk: e.activation(out=dec[:, hw_], in_=dlt[:, c0:c0 + HW], func=AF.Exp, scale=ntn[:, blk:blk + 1]),
                             reads=["dlt", "ntn"], writes=["ft0"])
                        for dr, gt, gk in ((0, g0, "ft1"), (1, g1, "ft2")):
                            pt, pk = self.next_ps()
                            ch = o * 2 + dr
                            P.op("pe", lambda e, blk=blk, ch=ch, pt=pt: e.matmul(pt[:, hw_], lhsT=H2[:, blk * 128:(blk + 1) * 128],
                                                                               rhs=w3[:, ch * 512 + c0:ch * 512 + c0 + HW], start=True, stop=True),
                                 reads=["H2", "hw3"], writes=[pk])
                            P.op("dve", lambda e, gt=gt, pt=pt: e.tensor_tensor(out=gt[:, hw_], in0=pt[:, hw_], in1=dec[:, hw_], op=ALU.mult),
                                 reads=[pk, "ft0"], writes=[gk])
                        if blk == 0:
                            P.op("dve", lambda e: e.tensor_scalar_mul(out=g1[:, hw_], in0=g1[:, hw_], scalar1=m1[:, 0:1]), reads=["ft2", "m1c"], writes=["ft2"])
                        P.op("dve", lambda e: e.tensor_tensor(out=sq[:, hw_], in0=g0[:, hw_], in1=g0[:, hw_], op=ALU.mult), reads=["ft1"], writes=["ft3"])
                        P.op("pe", lambda e, blk=blk: e.matmul(acc[:, hw_], lhsT=ones32[:], rhs=sq[:, hw_], start=(blk == 0), stop=False),
                             reads=["ft3", "ones32"], writes=[acck])
                        P.op("dve", lambda e: e.tensor_tensor(out=sq[:, hw_], in0=g1[:, hw_], in1=g1[:, hw_], op=ALU.mult), reads=["ft2"], writes=["ft3"])
                        P.op("pe", lambda e, blk=blk: e.matmul(acc[:, hw_], lhsT=ones32[:], rhs=sq[:, hw_], start=False, stop=(blk == nblk - 1)),
                             reads=["ft3", "ones32"], writes=[acck])
                        P.op("dve", lambda e, blk=blk: e.tensor_tensor(out=GS[:, blk, :], in0=g0[:, hw_], in1=g1[:, hw_], op=ALU.add),
                             reads=["ft1", "ft2"], writes=[("GS", blk)])
                        P.op("dve", lambda e, blk=blk: e.tensor_tensor(out=GD[:, blk, :], in0=g0[:, hw_], in1=g1[:, hw_], op=ALU.subtract),
                             reads=["ft1", "ft2"], writes=[("GD", blk)])
                    P.op("act", lambda e: e.activation(out=rnb[:], in_=acc[:, hw_], func=AF.Sqrt, bias=eps6[:, 0:1]), reads=[acck, "eps6"], writes=["rnb"])
                    P.op("dve", lambda e: e.reciprocal(out=rnb[:], in_=rnb[:]), reads=["rnb"], writes=["rnb"])
                    P.op("dve", lambda e: e.tensor_scalar_mul(out=rnb[:], in0=rnb[:], scalar1=2.0 / N2), reads=["rnb"], writes=["rnb"])
                    P.barrier()


            for o in range(2):
                gen_filters(o, 0)
                with ExitStack() as s2:
                    tb = self.ring(s2, "dftF", 4, [128, KG * 512], BF16)
                    table = mk_table(tb)
                    fo_r = self.ring(s2, "fo", 2, [128, 2, 512])
                    for kb in range(nblk):
                        C, ck = table(I["TC" + sfx][kb], (nblk, 128))
                        S, sk = table(I["TS" + sfx][kb], (nblk, 128))
                        fr, frk = self.next_ps(); fi, fik = self.next_ps()
                        P.op("pe", [lambda e, sb_=sb_: e.matmul(fr[:], lhsT=C[:, sb_, :], rhs=GS[:, sb_, :], start=(sb_ == 0), stop=(sb_ == nblk - 1))
                                    for sb_ in range(nblk)], reads=[ck] + [("GS", b) for b in range(nblk)], writes=[frk])
                        P.op("pe", [lambda e, sb_=sb_: e.matmul(fi[:], lhsT=S[:, sb_, :], rhs=GD[:, sb_, :], start=(sb_ == 0), stop=(sb_ == nblk - 1))
                                    for sb_ in range(nblk)], reads=[sk] + [("GD", b) for b in range(nblk)], writes=[fik])
                        fo, fok = fo_r[kb % 2], ("fo", kb % 2)
                        P.op("dve", lambda e: e.tensor_tensor(out=fo[:, 0, :], in0=fr[:], in1=rnb[:], op=ALU.mult), reads=[frk, "rnb"], writes=[fok + (0,)])
                        P.op("dve", lambda e: e.tensor_tensor(out=fo[:, 1, :], in0=fi[:], in1=rnb[:], op=ALU.mult), reads=[fik, "rnb"], writes=[fok + (1,)])
                        P.dma("sp", self.FS[o, kb0 + kb], fo[:], reads=[fok + (0,), fok + (1,)], writes=[("FS", o, kb0 + kb)])
                    P.barrier()
            P.barrier()

        src_tok = self.sb(st, "srctok", [128, nseq * nblk, 512], BF16)
        P.dma("sp", src_tok[:], self.tokS[:, g.tok0 // 128:g.tok0 // 128 + nseq * nblk, :],
              writes=[("tok", b, cc) for b in range(nseq * nblk) for cc in range(4)])
        msbox = {}

        def conv(o, post):
            with ExitStack() as s2:
                Yr = self.sb(s2, "Yr", [128, nseq * nblk, HW], BF16)
                Yi = self.sb(s2, "Yi", [128, nseq * nblk, HW], BF16)
                tb = self.ring(s2, "dft", 6 if o == 0 else 4, [128, KG * 512], BF16)
                table = mk_table(tb)
                fl_r = self.ring(s2, "fl", 2, [128, 2, 512])
                tmp = self.ring(s2, "ctmp", 4, [128, 512])
                for kb in range(nblk):
                    C, ck = table(I["TC" + sfx][kb], (nblk, 128))
                    S, sk = table(I["TS" + sfx][kb], (nblk, 128))
                    fl, flk = fl_r[kb % 2], ("fl", kb % 2)
                    P.dma("sp", fl[:], self.FS[o, kb0 + kb], reads=[("FS", o, kb0 + kb)], writes=[flk])
                    frs, fis = fl[:, 0, :], fl[:, 1, :]
                    for sq_ in range(nseq):
                        zr, zrk = self.next_ps(); zi, zik = self.next_ps()
                        tkeys = [("tok", sq_ * nblk + b, cc) for b in range(nblk) for cc in range(4)]
                        P.op("pe", [lambda e, sb_=sb_: e.matmul(zr[:], lhsT=C[:, sb_, :], rhs=src_tok[:, sq_ * nblk + sb_, :], start=(sb_ == 0),
                                                                stop=(sb_ == nblk - 1)) for sb_ in range(nblk)], reads=[ck] + tkeys, writes=[zrk])
                        P.op("pe", [lambda e, sb_=sb_: e.matmul(zi[:], lhsT=S[:, sb_, :], rhs=src_tok[:, sq_ * nblk + sb_, :], start=(sb_ == 0),
                                                                stop=(sb_ == nblk - 1)) for sb_ in range(nblk)], reads=[sk] + tkeys, writes=[zik])
                        t1, t2, t3, t4 = tmp[0], tmp[1], tmp[2], tmp[3]
                        yidx = sq_ * nblk + kb
                        P.op("dve", lambda e: e.tensor_tensor(out=t1[:], in0=zr[:], in1=frs, op=ALU.mult), reads=[zrk, flk], writes=["ct2"])
                        P.op("dve", lambda e: e.tensor_tensor(out=t2[:], in0=zi[:], in1=fis, op=ALU.mult), reads=[zik, flk], writes=["ct3"])
                        P.op("dve", lambda e: e.tensor_tensor(out=t3[:], in0=zr[:], in1=fis, op=ALU.mult), reads=[zrk, flk], writes=["ct4"])
                        P.op("dve", lambda e: e.tensor_tensor(out=t4[:], in0=zi[:], in1=frs, op=ALU.mult), reads=[zik, flk], writes=["ct5"])
                        P.op("pool", lambda e, yidx=yidx: e.tensor_tensor(out=Yr[:, yidx, :], in0=t1[:], in1=t2[:], op=ALU.subtract),
                             reads=["ct2", "ct3"], writes=[("Yr", yidx)])
                        P.op("pool", lambda e, yidx=yidx: e.tensor_tensor(out=Yi[:, yidx, :], in0=t3[:], in1=t4[:], op=ALU.add),
                             reads=["ct4", "ct5"], writes=[("Yi", yidx)])
                for sq_ in range(nseq):
                    for tt in range(ntt):
                        banks = [self.next_ps() for _ in range(4)]
                        ngrp = nblk // KG
                        for kg in range(ngrp):
                            Ci, cik = table(I["IC" + sfx][tt, :, kg * KG:(kg + 1) * KG, :], (KG, TI))
                            Si, sik = table(I["IS" + sfx][tt, :, kg * KG:(kg + 1) * KG, :], (KG, TI))
                            for cc in range(4):
                                bk, bkk = banks[cc]
                                fns = []
                                for kq in range(KG):
                                    kb = kg * KG + kq
                                    yidx = sq_ * nblk + kb
                                    fns.append(lambda e, kq=kq, yidx=yidx, bk=bk, cc=cc, kb=kb: e.matmul(
                                        bk[:, 0:TI], lhsT=Yr[:, yidx, cc * 128:(cc + 1) * 128], rhs=Ci[:, kq, :], start=(kb == 0), stop=False))
                                    fns.append(lambda e, kq=kq, yidx=yidx, bk=bk, cc=cc, kb=kb: e.matmul(
                                        bk[:, 0:TI], lhsT=Yi[:, yidx, cc * 128:(cc + 1) * 128], rhs=Si[:, kq, :], start=False, stop=(kb == nblk - 1)))
                                P.op("pe", fns, reads=[cik, sik] + [("Yr", sq_ * nblk + kg * KG + q) for q in range(KG)]
                                     + [("Yi", sq_ * nblk + kg * KG + q) for q in range(KG)], writes=[bkk])
                        for cc in range(4):
                            post(sq_, tt, cc, banks[cc][0], banks[cc][1])
                P.barrier()

        io = {"n": 0}
        iobuf = self.ring(st, "hio", 4, [128, 512])
        zb_r = self.ring(st, "zb", 2, [128, 512], BF16)

        def ld(which, cc, tok_lo, width):
            n = io["n"]; io["n"] += 1
            tl, tk = iobuf[n % 4], ("hio", n % 4)
            src = self.hyS[which, cc * 128:(cc + 1) * 128, tok_lo:tok_lo + width]
            P.dma("sp", tl[:, 0:width], src, reads=[("hyS", which, cc, (tok_lo // TT) * TT)], writes=[tk])
            return tl, tk

        def post1(sq_, tt, cc, bk, bkk):
            tok_lo = g.tok0 + sq_ * L + tt * TI
            hv, hvk = ld(0, cc, tok_lo, TI)
            hx, hxk = ld(1, cc, tok_lo, TI)
            P.op("dve", lambda e: e.scalar_tensor_tensor(out=hv[:, 0:TI], in0=hv[:, 0:TI], scalar=hbias[:, 0, cc:cc + 1], in1=bk[:, 0:TI],
                                                         op0=ALU.mult, op1=ALU.add), reads=[hvk, bkk, "hbias"], writes=[hvk])
            P.op("dve", lambda e: e.tensor_tensor(out=hv[:, 0:TI], in0=hv[:, 0:TI], in1=hx[:, 0:TI], op=ALU.mult), reads=[hvk, hxk], writes=[hvk])
            P.dma("sp", self.hyS[3, cc * 128:(cc + 1) * 128, tok_lo:tok_lo + TI], hv[:, 0:TI], reads=[hvk],
                  writes=[("hyS", 3, cc, (tok_lo // TT) * TT)])
            n = io["n"]
            zb, zbk = zb_r[n % 2], ("zb", n % 2)
            P.op("act", lambda e: e.copy(out=zb[:, 0:TI], in_=hv[:, 0:TI]), reads=[hvk], writes=[zbk])
            nb = TI // 128
            b0 = sq_ * nblk + tt * nb
            self.to_tok(zb, zbk, src_tok[:, b0:b0 + nb, cc * 128:(cc + 1) * 128], [("tok", b0 + b, cc) for b in range(nb)], nb)

        def post2(sq_, tt, cc, bk, bkk):
            tok_lo = g.tok0 + sq_ * L + tt * TI
            z, zk = ld(3, cc, tok_lo, TI)
            hx, hxk = ld(2, cc, tok_lo, TI)
            P.op("dve", lambda e: e.scalar_tensor_tensor(out=z[:, 0:TI], in0=z[:, 0:TI], scalar=hbias[:, 1, cc:cc + 1], in1=bk[:, 0:TI],
                                                         op0=ALU.mult, op1=ALU.add), reads=[zk, bkk, "hbias"], writes=[zk])
            if g.name == "s":
                out = msbox["MS"][:, cc, :].rearrange("p (r c) -> p c r", c=64)[:, tt * 8:(tt + 1) * 8, :]
                a0 = z[:, 0:TI].rearrange("p (c r) -> p c r", r=64)
                a1 = hx[:, 0:TI].rearrange("p (c r) -> p c r", r=64)
            else:
                lo = sq_ * L + tt * TI
                out = msbox["MS"][:, cc, lo:lo + TI]
                a0, a1 = z[:, 0:TI], hx[:, 0:TI]
            P.op("dve", lambda e: e.tensor_tensor(out=out, in0=a0, in1=a1, op=ALU.mult), reads=[zk, hxk], writes=[("MS", cc, sq_, tt)])

        conv(0, post1)
        MS = self.sb(st, "MSh", [128, 4, g.ntok], BF16)
        msbox["MS"] = MS
        conv(1, post2)
        for cc in range(4):
            P.dma("sp", self.mixT[cc * 128:(cc + 1) * 128, g.tok0:g.tok0 + g.ntok], MS[:, cc, :],
                  reads=[("MS", cc, s_, t_) for s_ in range(nseq) for t_ in range(ntt)],
                  writes=[("mixT", cc, g.tok0 + t * TT) for t in range(g.ntile)])

    def rwkv_setup(self, st, l, g):
        from contextlib import ExitStack
        P, I, O = self.P, self.I, self.O
        j = l // 2
        nl, ll = TT // g.line, g.line
        lines = lambda ap: ap.rearrange("p (a b) -> p a b", b=ll)
        L, nseq = g.L, g.nseq
        CH = 128
        rwS = self.rwS
        SL = {"r": 0, "v": 1, "kk": 2, "gs": 3, "kd0": 4, "kd1": 5, "b0": 6, "b1": 7, "lw0": 8, "lw1": 9, "bon": 10}
        cst = self.sb(st, "rwc", [128, 6, 128])
        P.dma("sp", cst[:], I["rw_masks"].rearrange("a p c -> p a c"), writes=["rwc"])
        cstb = self.sb(st, "rwcb", [128, 6, 128], BF16)
        P.op("dve", lambda e: e.tensor_copy(out=cstb[:], in_=cst[:]), reads=["rwc"], writes=["rwcb"])
        prm = self.sb(st, "rwprm", [128, 16, 4])
        names = [("rw_mu", 4), ("rw_w0", 2), ("rw_a0", 2)]
        k0 = 0
        for nm, cntk in names:
            P.dma("sp", prm[:, k0:k0 + cntk, :], I[nm][j].rearrange("k (hp p) -> p k hp", p=128), writes=["rwprm"], allow_slow_non_contiguous=True)
            k0 += cntk
        for nm in ("rw_kk", "rw_ka", "rw_gn_g", "rw_gn_b"):
            P.dma("sp", prm[:, k0, :], I[nm][j].rearrange("(hp p) -> p hp", p=128), writes=["rwprm"], allow_slow_non_contiguous=True)
            k0 += 1
        P.dma("sp", prm[:, k0, :], I["rw_rk"][j].rearrange("(hp h2) n -> (h2 n) hp", h2=2), writes=["rwprm"], allow_slow_non_contiguous=True)
        PMU, PW0, PA0, PKK, PKA, PGG, PGB, PRK = 0, 4, 6, 8, 9, 10, 11, 12
        der = self.sb(st, "rwder", [128, 9, 4])
        P.op("dve", lambda e: e.tensor_scalar_mul(out=der[:, 0:4, :], in0=prm[:, 0:4, :], scalar1=0.5), reads=["rwprm"], writes=["rwder"])
        P.op("dve", lambda e: e.tensor_scalar(out=der[:, 4:8, :], in0=prm[:, 0:4, :], scalar1=-1.0, scalar2=1.0, op0=ALU.mult, op1=ALU.add),
             reads=["rwprm"], writes=["rwder"])
        P.op("dve", lambda e: e.tensor_scalar(out=der[:, 8, :], in0=prm[:, PKA, :], scalar1=-1.0, scalar2=1.0, op0=ALU.mult, op1=ALU.add),
             reads=["rwprm"], writes=["rwder"])
        e12 = self.sb(st, "e12", [128, 1]); gne = self.sb(st, "gne", [128, 1])
        P.op("dve", lambda e: e.memset(e12[:], 1e-12), writes=["e12"])
        P.op("dve", lambda e: e.memset(gne[:], 64e-5), writes=["gne"])
        rw = dict(cst=cst, cstb=cstb, prm=prm, der=der, e12=e12, gne=gne, SL=SL, PMU=PMU, PW0=PW0, PA0=PA0, PKK=PKK, PKA=PKA, PGG=PGG, PGB=PGB, PRK=PRK)
        return rw

    def rwkv_partA(self, st, l, g, hT, rw):
        from contextlib import ExitStack
        P, I, O = self.P, self.I, self.O
        j = l // 2
        nl, ll = TT // g.line, g.line
        lines = lambda ap: ap.rearrange("p (a b) -> p a b", b=ll)
        L, nseq = g.L, g.nseq
        rwS = self.rwS
        cst, cstb, prm, der, e12, gne, SL = rw["cst"], rw["cstb"], rw["prm"], rw["der"], rw["e12"], rw["gne"], rw["SL"]
        PMU, PW0, PA0, PKK, PKA, PGG, PGB, PRK = [rw[k] for k in ("PMU", "PW0", "PA0", "PKK", "PKA", "PGG", "PGB", "PRK")]
        wl = self.sb(st, "wl", [128, 8, 4, 128], BF16)
        w2t = self.sb(st, "w2t", [128, 2, 512], BF16)
        with ExitStack() as s2:
            wraw = self.sb(s2, "wraw", [128, 8, 2, 2, 64])
            mux = self.sb(s2, "mux", [128, 2, 8]); omm = self.sb(s2, "omm", [128, 2, 8])
            for ty, nm in enumerate(("rw_w1", "rw_a1")):
                for d in range(2):
                    P.dma("sp", wraw[:, :, ty, d, :], I[nm][j, d].rearrange("(kc p) n -> p kc n", p=128), writes=[("wraw", ty, d)])
            P.dma("sp", mux[:], I["rw_mu_x"][j].rearrange("k (kc p) -> p k kc", p=128), writes=["mux"], allow_slow_non_contiguous=True)
            P.op("dve", lambda e: e.tensor_scalar(out=omm[:], in0=mux[:], scalar1=-1.0, scalar2=1.0, op0=ALU.mult, op1=ALU.add), reads=["mux"], writes=["omm"])
            for ty in range(2):
                for d in range(2):
                    for var, sc in ((0, omm), (1, mux)):
                        P.op("dve", lambda e, ty=ty, d=d, var=var, sc=sc: e.tensor_tensor(
                            out=wl[:, :, ty * 2 + var, d * 64:(d + 1) * 64], in0=wraw[:, :, ty, d, :],
                            in1=sc[:, ty, :].rearrange("p (k o) -> p k o", o=1).to_broadcast([128, 8, 64]), op=ALU.mult),
                            reads=[("wraw", ty, d), "mux", "omm"], writes=["wl"])
            for ty, nm in enumerate(("rw_w2", "rw_a2")):
                for d in range(2):
                    P.dma("pool", w2t[d * 64:(d + 1) * 64, ty, :], I[nm][j, d], writes=["w2t"])
            P.barrier()
        LWI = self.sb(st, "LWI", [128, g.ntok], BF16)
        LAI = self.sb(st, "LAI", [128, g.ntok], BF16)
        tA = self.ring(st, "rta", 4, [128, TT])
        cn = {"t": 0}

        def tmpA():
            n = cn["t"]; cn["t"] += 1
            return tA[n % 4], ("rta", n % 4)

        def lora_tile(t):
            hs = lambda kc: hT[:, kc, t * TT:(t + 1) * TT]
            for ty, dst in ((0, LWI), (1, LAI)):
                pa, pak = self.next_ps(); pb, pbk = self.next_ps()
                P.op("pe", [lambda e, kc=kc: e.matmul(pa[:], lhsT=wl[:, kc, ty * 2, :], rhs=hs(kc), start=(kc == 0), stop=(kc == 7)) for kc in range(8)],
                     reads=["wl"] + self.hkeys(g, t), writes=[pak])
                P.op("pe", [lambda e, kc=kc: e.matmul(pb[:], lhsT=wl[:, kc, ty * 2 + 1, :], rhs=hs(kc), start=(kc == 0), stop=(kc == 7)) for kc in range(8)],
                     reads=["wl"] + self.hkeys(g, t), writes=[pbk])
                xb, xbk = tmpA(); ac, ack = tmpA()
                P.op("act", lambda e: e.copy(out=xb[:], in_=pb[:]), reads=[pbk], writes=[xbk])
                P.op("act", lambda e: e.copy(out=ac[:], in_=pa[:]), reads=[pak], writes=[ack])
                P.op("dve", lambda e: e.scalar_tensor_tensor(out=lines(ac[:])[:, :, 1:], in0=lines(xb[:])[:, :, :ll - 1], scalar=0.5,
                                                             in1=lines(ac[:])[:, :, 1:], op0=ALU.mult, op1=ALU.add), reads=[xbk, ack], writes=[ack])
                P.op("dve", lambda e: e.scalar_tensor_tensor(out=lines(ac[:])[:, :, :ll - 1], in0=lines(xb[:])[:, :, 1:], scalar=0.5,
                                                             in1=lines(ac[:])[:, :, :ll - 1], op0=ALU.mult, op1=ALU.add), reads=[xbk, ack], writes=[ack])
                fn = AF.Tanh if ty == 0 else AF.Identity
                P.op("act", lambda e: e.activation(out=dst[:, t * TT:(t + 1) * TT], in_=ac[:], func=fn), reads=[ack], writes=[("LI", ty, t)])
        if "norwL" not in self.flags:
            for t in range(g.ntile):
                lora_tile(t)

        wr = self.ring(st, "wrw", 6, [128, 8, 128], BF16)
        wcn = {"n": 0}

        def getw(col0):
            n = wcn["n"]; wcn["n"] += 1
            wt, wk = wr[n % 6], ("wrw", n % 6)
            self.w_chunk(wt, wk, I["cd_w_in"][j, :, col0:col0 + 128])
            return wt, wk

        def store(slot, hp, t, tl, tk):
            gt0 = g.tok0 + t * TT
            P.dma("sp", rwS[hp, SL[slot], :, gt0:gt0 + TT], tl[:], reads=[tk], writes=[("rwS", hp, slot, gt0)])

        roleA_sets = [{nm: self.sb(st, "ra_" + nm, [128, TT]) for nm in ("xl0", "xl1", "y0", "y1", "y2", "y3", "kq", "sq", "lw", "a", "kd0", "kd1", "bs")}
                      for _ in range(2)]
        sqb_sets = [self.sb(st, "ra_sqb", [128, TT], BF16) for _ in range(2)]
        thr = {"i": 0}

        def role(nm):
            return roleA_sets[thr["i"]][nm], ("ra", nm)

        def stageA_tile(hp, t, ws):
            sqb = sqb_sets[thr["i"]]
            hs = lambda kc: hT[:, kc, t * TT:(t + 1) * TT]
            sl_ = slice(t * TT, (t + 1) * TT)
            outs = []
            for n4 in range(4):
                pt, pk = self.next_ps()
                self.mm_group(pt[:], pk, ws[n4][0], ws[n4][1], hs, self.hkeys(g, t))
                yield
                xl, xlk = role("xl%d" % (n4 % 2)); y, yk = role("y%d" % n4)
                P.op("act", lambda e, xl=xl, pt=pt: e.copy(out=xl[:], in_=pt[:]), reads=[pk], writes=[xlk])
                yield
                P.op("dve", lambda e, xl=xl, y=y, n4=n4: e.tensor_scalar_mul(out=y[:], in0=xl[:], scalar1=der[:, 4 + n4, hp:hp + 1]),
                     reads=[xlk, "rwder"], writes=[yk])
                yield
                P.op("dve", lambda e, xl=xl, y=y, n4=n4: e.scalar_tensor_tensor(out=lines(y[:])[:, :, 1:], in0=lines(xl[:])[:, :, :ll - 1],
                                                                                scalar=der[:, n4, hp:hp + 1], in1=lines(y[:])[:, :, 1:], op0=ALU.mult, op1=ALU.add),
                     reads=[xlk, yk], writes=[yk])
                yield
                P.op("dve", lambda e, xl=xl, y=y, n4=n4: e.scalar_tensor_tensor(out=lines(y[:])[:, :, :ll - 1], in0=lines(xl[:])[:, :, 1:],
                                                                                scalar=der[:, n4, hp:hp + 1], in1=lines(y[:])[:, :, :ll - 1], op0=ALU.mult, op1=ALU.add),
                     reads=[xlk, yk], writes=[yk])
                yield
                outs.append((y, yk))
            (r_, rk), (k_, kk_k), (v_, vk), (g_, gk) = outs
            P.op("act", lambda e: e.activation(out=g_[:], in_=g_[:], func=AF.Sigmoid), reads=[gk], writes=[gk])
            yield
            store("gs", hp, t, g_, gk); store("r", hp, t, r_, rk); store("v", hp, t, v_, vk)
            yield
            kq, kqk = role("kq"); sq, sqk = role("sq")
            P.op("dve", lambda e: e.tensor_scalar_mul(out=kq[:], in0=k_[:], scalar1=prm[:, PKK, hp:hp + 1]), reads=[kk_k, "rwprm"], writes=[kqk])
            yield
            P.op("act", lambda e: e.activation(out=sqb[:], in_=kq[:], func=AF.Square), reads=[kqk], writes=["sqb"])
            yield
            pn, pnk = self.next_ps()
            P.op("pe", lambda e: e.matmul(pn[:], lhsT=cstb[:, 4, :], rhs=sqb[:], start=True, stop=True), reads=["rwcb", "sqb"], writes=[pnk])
            yield
            P.op("act", lambda e: e.activation(out=sq[:], in_=pn[:], func=AF.Sqrt, bias=e12[:, 0:1]), reads=[pnk, "e12"], writes=[sqk])
            yield
            P.op("dve", lambda e: e.reciprocal(out=sq[:], in_=sq[:]), reads=[sqk], writes=[sqk])
            yield
            P.op("dve", lambda e: e.tensor_tensor(out=kq[:], in0=kq[:], in1=sq[:], op=ALU.mult), reads=[kqk, sqk], writes=[kqk])
            yield
            store("kk", hp, t, kq, kqk)
            yield
            kds = []
            for d in range(2):
                pw, pwk = self.next_ps(); pa, pak = self.next_ps()
                P.op("pe", lambda e, d=d, pw=pw: e.matmul(pw[:], lhsT=w2t[d * 64:(d + 1) * 64, 0, hp * 128:(hp + 1) * 128],
                                                          rhs=LWI[d * 64:(d + 1) * 64, sl_], start=True, stop=True), reads=["w2t", ("LI", 0, t)], writes=[pwk], serial=True)
                yield
                P.op("pe", lambda e, d=d, pa=pa: e.matmul(pa[:], lhsT=w2t[d * 64:(d + 1) * 64, 1, hp * 128:(hp + 1) * 128],
                                                          rhs=LAI[d * 64:(d + 1) * 64, sl_], start=True, stop=True), reads=["w2t", ("LI", 1, t)], writes=[pak], serial=True)
                yield
                lw, lwk = role("lw"); a_, ak = role("a"); kd, kdk = role("kd%d" % d)
                P.op("act", lambda e, d=d, lw=lw, pw=pw: e.activation(out=lw[:], in_=pw[:], func=AF.Sigmoid, bias=prm[:, PW0 + d, hp:hp + 1]),
                     reads=[pwk, "rwprm"], writes=[lwk])
                yield
                P.op("dve", lambda e, lw=lw: e.tensor_scalar_mul(out=lw[:], in0=lw[:], scalar1=-float(np.exp(-0.5))), reads=[lwk], writes=[lwk])
                yield
                store("lw%d" % d, hp, t, lw, lwk)
                yield
                P.op("act", lambda e, d=d, a_=a_, pa=pa: e.activation(out=a_[:], in_=pa[:], func=AF.Sigmoid, bias=prm[:, PA0 + d, hp:hp + 1]),
                     reads=[pak, "rwprm"], writes=[ak])
                yield
                P.op("dve", lambda e, a_=a_, kd=kd: e.tensor_scalar(out=kd[:], in0=a_[:], scalar1=prm[:, PKA, hp:hp + 1], scalar2=der[:, 8, hp:hp + 1],
                                                                    op0=ALU.mult, op1=ALU.add), reads=[ak, "rwprm", "rwder"], writes=[kdk])
                yield
                P.op("dve", lambda e, kd=kd: e.tensor_tensor(out=kd[:], in0=kd[:], in1=k_[:], op=ALU.mult), reads=[kdk, kk_k], writes=[kdk])
                yield
                store("kd%d" % d, hp, t, kd, kdk)
                yield
                P.op("dve", lambda e, a_=a_: e.tensor_tensor(out=a_[:], in0=a_[:], in1=kq[:], op=ALU.mult), reads=[ak, kqk], writes=[ak])
                yield
                store("b%d" % d, hp, t, a_, ak)
                yield
                kds.append((kd, kdk))
            bs, bsk = role("bs")
            P.op("dve", lambda e: e.tensor_tensor(out=bs[:], in0=kds[0][0][:], in1=kds[1][0][:], op=ALU.add), reads=[kds[0][1], kds[1][1]], writes=[bsk])
            yield
            P.op("dve", lambda e: e.scalar_tensor_tensor(out=sqb[:], in0=r_[:], scalar=prm[:, PRK, hp:hp + 1], in1=bs[:], op0=ALU.mult, op1=ALU.mult),
                 reads=[rk, bsk, "rwprm"], writes=["sqb"])
            yield
            pbn, pbnk = self.next_ps()
            P.op("pe", lambda e: e.matmul(pbn[:], lhsT=cstb[:, 4, :], rhs=sqb[:], start=True, stop=True), reads=["rwcb", "sqb"], writes=[pbnk])
            yield
            P.op("dve", lambda e: e.tensor_tensor(out=bs[:], in0=pbn[:], in1=v_[:], op=ALU.mult), reads=[pbnk, vk], writes=[bsk])
            yield
            store("bon", hp, t, bs, bsk)
            yield

        def a_thread(hp, ws, tiles):
            for t in tiles:
                yield from stageA_tile(hp, t, ws)

        for hp in range(4):
            ws = [getw(1536 + n4 * 512 + hp * 128) for n4 in range(4)]
            if "norwA" not in self.flags:
                gens = [(i, a_thread(hp, ws, list(range(i, g.ntile, 2)))) for i in range(2)]
                while gens:
                    for item in list(gens):
                        thr["i"] = item[0]
                        P.ns = ("A", item[0])
                        try:
                            next(item[1])
                        except StopIteration:
                            gens.remove(item)
                P.ns = None

    def rwkv_partB(self, st, l, g, rw):
        from contextlib import ExitStack
        P, I, O = self.P, self.I, self.O
        j = l // 2
        L, nseq = g.L, g.nseq
        CH = 128
        rwS = self.rwS
        cst, cstb, prm, der, e12, gne, SL = rw["cst"], rw["cstb"], rw["prm"], rw["der"], rw["e12"], rw["gne"], rw["SL"]
        PGG, PGB = rw["PGG"], rw["PGB"]
        tA = self.ring(st, "rtb", 8, [128, TT])
        cn = {"t": 0}

        def tmpA():
            n = cn["t"]; cn["t"] += 1
            return tA[n % 8], ("rtb", n % 8)
        def stageB(hp, sq_, d, bst):
            (ldr, TLar, Bt, Kt, Bh, Kh, Vb, clt, ext, rmask, N2r, X2r, TTr, ATr, TOKr, W1Tr, Zr_, U0r, U2r, ST32, STb, oacc, snat, pcr) = bst
            nch = L // CH
            ntile = max(1, L // TT)
            tw = min(TT, L)
            cpt = tw // CH
            hb_ = 0
            if g.name == "s":
                P.op("dve", lambda e: e.memset(snat[:], 0.0), writes=["snat"])
                yield
                for h2 in range(2):
                    P.dma("sp", snat[h2 * 64:(h2 + 1) * 64, h2 * 64:(h2 + 1) * 64], I["st_rwkv"][j, d, hp * 2 + h2], writes=["snat"])
                    yield
                pt, pk = self.next_ps()
                P.op("pe", lambda e: e.transpose(pt[:, 0:128], snat[:], self.ident[:]), reads=["snat", "ident"], writes=[pk])
                yield
                P.op("dve", lambda e: e.tensor_copy(out=ST32[:], in_=pt[:, 0:128]), reads=[pk], writes=["ST32"])
                yield
            else:
                P.op("dve", lambda e: e.memset(ST32[:], 0.0), writes=["ST32"])
                yield
            P.op("act", lambda e: e.copy(out=STb[0][:], in_=ST32[:]), reads=["ST32"], writes=[("STb", 0)])
            yield
            stn = {"n": 0}
            order = list(range(ntile)) if d == 0 else list(reversed(range(ntile)))
            for ti, t in enumerate(order):
                tok_lo = g.tok0 + sq_ * L + t * tw
                gt0 = (tok_lo // TT) * TT
                ld = {}
                for q, nm in enumerate(("r", "lw%d" % d, "kd%d" % d, "v", "kk", "b%d" % d)):
                    tl, tk = ldr[q], ("ldr", q)
                    P.dma("sp", tl[:, 0:tw], rwS[hp, SL[nm], :, tok_lo:tok_lo + tw], reads=[("rwS", hp, nm, gt0)], writes=[tk])
                    yield
                    ld[q] = (tl, tk)
                (r_, rk), (lw, lwk), (kd, kdk), (v_, vk), (kk, kkk), (b_, bk) = [ld[q] for q in range(6)]
                if "nosb2" in self.flags:
                    continue
                W = slice(0, tw)
                c3 = lambda ap: ap[:, W].rearrange("p (c i) -> p c i", i=CH)
                cl, e1, e2, e3, e4 = clt, ext[0], ext[1], ext[2], ext[3]
                if d == 0:
                    P.op("dve", lambda e: e.tensor_tensor_scan(out=cl[:, W], data0=rmask[:, 0, W], data1=lw[:, W], initial=0.0, op0=ALU.mult, op1=ALU.add),
                         reads=[lwk, "rmask"], writes=["cl"])
                    yield
                    cend = c3(cl)[:, :, CH - 1:CH]
                else:
                    P.op("dve", lambda e: e.tensor_tensor_scan(out=cl[:, W][:, ::-1], data0=rmask[:, 1, W][:, ::-1], data1=lw[:, W][:, ::-1], initial=0.0,
                                                               op0=ALU.mult, op1=ALU.add), reads=[lwk, "rmask"], writes=["cl"])
                    yield
                    cend = c3(cl)[:, :, 0:1]
                P.op("act", lambda e: e.activation(out=e1[:, W], in_=cl[:, W], func=AF.Exp), reads=["cl"], writes=["e1"])
                yield
                P.op("act", lambda e: e.activation(out=e2[:, W], in_=cl[:, W], func=AF.Exp, scale=-1.0), reads=["cl"], writes=["e2"])
                yield
                P.op("dve", lambda e: e.tensor_tensor(out=e3[:, W], in0=cl[:, W], in1=lw[:, W], op=ALU.subtract), reads=["cl", lwk], writes=["e3"])
                yield
                P.op("act", lambda e: e.activation(out=e3[:, W], in_=e3[:, W], func=AF.Exp), reads=["e3"], writes=["e3"])
                yield
                P.op("dve", lambda e: e.tensor_tensor(out=c3(e4), in0=cend.to_broadcast([128, cpt, CH]), in1=c3(cl), op=ALU.subtract),
                     reads=["cl"], writes=["e4"])
                yield
                P.op("act", lambda e: e.activation(out=e4[:, W], in_=e4[:, W], func=AF.Exp), reads=["e4"], writes=["e4"])
                yield
                pc, pck = pcr[ti % 2], ("pcr", ti % 2)
                P.op("act", lambda e: e.activation(out=pc[:, 0:cpt], in_=cend.rearrange("p c o -> p (c o)"), func=AF.Exp), reads=["cl"], writes=[pck])
                yield
                P.op("dve", lambda e: e.scalar_tensor_tensor(out=TLar[:, 0:cpt, 0, :], in0=c3(kk), scalar=-1.0, in1=c3(e3), op0=ALU.mult, op1=ALU.mult),
                     reads=[kkk, "e3"], writes=["TLa"])
                yield
                P.op("dve", lambda e: e.tensor_tensor(out=TLar[:, 0:cpt, 1, :], in0=c3(r_), in1=c3(e1), op=ALU.mult), reads=[rk, "e1"], writes=["TLr"])
                yield
                P.op("dve", lambda e: e.tensor_tensor(out=Bt[:, W], in0=b_[:, W], in1=e2[:, W], op=ALU.mult), reads=[bk, "e2"], writes=["Bt"])
                yield
                P.op("dve", lambda e: e.tensor_tensor(out=Kt[:, W], in0=kd[:, W], in1=e2[:, W], op=ALU.mult), reads=[kdk, "e2"], writes=["Kt"])
                yield
                P.op("pool", lambda e: e.tensor_tensor(out=Bh[:, W], in0=b_[:, W], in1=e4[:, W], op=ALU.mult), reads=[bk, "e4"], writes=["Bh"])
                yield
                P.op("pool", lambda e: e.tensor_tensor(out=Kh[:, W], in0=kd[:, W], in1=e4[:, W], op=ALU.mult), reads=[kdk, "e4"], writes=["Kh"])
                yield
                P.op("act", lambda e: e.copy(out=Vb[:, W], in_=v_[:, W]), reads=[vk], writes=["Vb"])
                yield
                if "nosb3" in self.flags:
                    continue
                corder = list(range(cpt)) if d == 0 else list(reversed(range(cpt)))
                for c in corder:
                    yield from self.rwkv_chunk(hp, sq_, d, t, c, tw, cst, cstb, bst, pc, pck, stn, g)
            if g.name == "p":
                pt, pk = self.next_ps()
                P.op("pe", lambda e: e.transpose(pt[:, 0:128], ST32[:], self.ident[:]), reads=["ST32", "ident"], writes=[pk])
                yield
                P.op("dve", lambda e: e.tensor_copy(out=snat[:], in_=pt[:, 0:128]), reads=[pk], writes=["snat"])
                yield
                for h2 in range(2):
                    P.dma("sp", O["nrwkv"][sq_, j, d, hp * 2 + h2], snat[h2 * 64:(h2 + 1) * 64, h2 * 64:(h2 + 1) * 64], reads=["snat"],
                          writes=[("nrwkv", sq_, j, d, hp, h2)])
                    yield

        with ExitStack() as sB:
            rmask = self.sb(sB, "rmask", [128, 2, TT])
            P.op("dve", lambda e: e.memset(rmask[:], 1.0), writes=["rmask"])
            P.op("dve", lambda e: e.memset(rmask[:, 0, :].rearrange("p (c i) -> p c i", i=CH)[:, :, 0:1], 0.0), writes=["rmask"])
            P.op("dve", lambda e: e.memset(rmask[:, 1, :].rearrange("p (c i) -> p c i", i=CH)[:, :, CH - 1:CH], 0.0), writes=["rmask"])
            bsts = {}
            self.TTb_d = {}
            for d in range(2):
                ldr = self.ring(sB, "ldr", 6, [128, TT])
                TLar = self.sb(sB, "TLar", [128, 4, 2, 128], BF16)
                Bt = self.sb(sB, "Bt", [128, TT], BF16); Kt = self.sb(sB, "Kt", [128, TT], BF16)
                Bh = self.sb(sB, "Bh", [128, TT], BF16); Kh = self.sb(sB, "Kh", [128, TT], BF16); Vb = self.sb(sB, "Vb", [128, TT], BF16)
                clt = self.sb(sB, "clt", [128, TT]); ext = self.ring(sB, "ext", 4, [128, TT])
                N2r = self.ring(sB, "N2", 2, [128, 2, 128]); X2r = self.ring(sB, "X2", 2, [128, 2, 128])
                TTr = self.ring(sB, "TT", 2, [128, 2, 128])
                self.TTb_d[d] = self.sb(sB, "TTb", [128, 2, 128], BF16)
                ATr = self.sb(sB, "ATr", [128, 3, 2, 128], BF16)
                TOKr = self.sb(sB, "TOK", [128, 4, 128], BF16)
                W1Tr = self.sb(sB, "W1T", [128, 128], BF16)
                Zr_ = self.sb(sB, "Zrw", [128, 128], BF16); U0r = self.sb(sB, "U0", [128, 128]); U2r = self.sb(sB, "U2", [128, 128], BF16)
                ST32 = self.sb(sB, "ST32", [128, 128]); STb = self.ring(sB, "STb", 2, [128, 128], BF16)
                oacc = self.sb(sB, "oacc", [128, g.ntok])
                snat = self.sb(sB, "snat", [128, 128]); pcr = self.ring(sB, "pcr", 2, [128, 4])
                bsts[d] = (ldr, TLar, Bt, Kt, Bh, Kh, Vb, clt, ext, rmask, N2r, X2r, TTr, ATr, TOKr, W1Tr, Zr_, U0r, U2r, ST32, STb, oacc, snat, pcr)
            msr = self.ring(sB, "msd", 2, [128, g.ntok], BF16)
            self.post_b = (self.sb(sB, "post_ob", [128, TT], BF16), self.sb(sB, "post_sq", [128, TT], BF16))
            self.cstb_ = cstb

            def thread(hp, d):
                for sq_ in range(nseq):
                    yield from stageB(hp, sq_, d, bsts[d])

            def run_threads(gens):
                gens = list(gens)
                while gens:
                    for item in list(gens):
                        P.ns = item[0]
                        try:
                            next(item[1])
                        except StopIteration:
                            gens.remove(item)
                P.ns = None

            for hp in range(4):
                if "norwB" not in self.flags:
                    run_threads([(d, thread(hp, d)) for d in range(2)])
                if "norwP" not in self.flags:
                    self.rwkv_post(hp, g, j, cst, prm, gne, (bsts[0][21], bsts[1][21]), msr[hp % 2], ("msd", hp % 2), tmpA, SL, PGG, PGB)

    def rwkv_chunk(self, hp, sq_, d, t, c, tw, cst, cstb, bst, pc, pck, stn, g):
        P = self.P
        (ldr, TLar, Bt, Kt, Bh, Kh, Vb, clt, ext, rmask, N2r, X2r, TTr, ATr, TOKr, W1Tr, Zr_, U0r, U2r, ST32, STb, oacc, snat, pcr) = bst
        CH = 128
        cs = slice(c * CH, (c + 1) * CH)
        mN, mX, mI = (0, 2, 3) if d == 0 else (2, 0, 1)
        h2v = lambda ap, w: ap[:, 0:2 * w].rearrange("p (h i) -> p h i", h=2)
        bc2 = lambda m: cst[:, m, :].rearrange("p (o i) -> p o i", o=1).to_broadcast([128, 2, 128])
        tl_keys = ["TLa", "TLr"]
        pA, pAk = self.next_ps()
        pB, pBk = self.next_ps()
        pC, pCk = self.next_ps()
        N_, X_, T_ = N2r[0], X2r[0], TTr[0]
        for h in range(2):
            tl2 = TLar[h * 64:(h + 1) * 64, c, :, :].rearrange("p a i -> p (a i)")
            P.op("pe", lambda e, h=h: e.matmul(pA[:, h * 128:(h + 1) * 128], lhsT=TLar[h * 64:(h + 1) * 64, c, 0, :], rhs=Bt[h * 64:(h + 1) * 64, cs],
                                               start=True, stop=True), reads=["TLa", "Bt"], writes=[pAk], serial=True)
            P.op("pe", lambda e, h=h, tl2=tl2: e.matmul(pB[:, h * 256:(h + 1) * 256], lhsT=Bt[h * 64:(h + 1) * 64, cs], rhs=tl2, start=True, stop=True),
                 reads=["TLa", "TLr", "Bt"], writes=[pBk])
            P.op("pe", lambda e, h=h, tl2=tl2: e.matmul(pC[:, h * 256:(h + 1) * 256], lhsT=Kt[h * 64:(h + 1) * 64, cs], rhs=tl2, start=True, stop=True),
                 reads=["TLa", "TLr", "Kt"], writes=[pCk])
        P.op("dve", lambda e: e.tensor_tensor(out=N_[:], in0=h2v(pA, 128), in1=bc2(mN), op=ALU.mult), reads=[pAk, "rwc"], writes=[("N2", 0)])
        yield
        pB4 = pB[:, 0:512].rearrange("p (h a i) -> p h a i", h=2, a=2)
        P.op("dve", lambda e: e.tensor_tensor(out=X_[:], in0=pB4[:, :, 0, :], in1=bc2(mX), op=ALU.mult), reads=[pBk, "rwc"], writes=[("X2", 0)])
        yield
        P.op("dve", lambda e: e.tensor_tensor(out=ATr[:, 0, :, :], in0=pB4[:, :, 1, :], in1=bc2(mI), op=ALU.mult), reads=[pBk, "rwc"], writes=["ArbT"])
        yield
        pC4 = pC[:, 0:512].rearrange("p (h a i) -> p h a i", h=2, a=2)
        P.op("dve", lambda e: e.tensor_tensor(out=ATr[:, 1, :, :], in0=pC4[:, :, 0, :], in1=bc2(mX), op=ALU.mult), reads=[pCk, "rwc"], writes=["AakT"])
        yield
        P.op("dve", lambda e: e.tensor_tensor(out=ATr[:, 2, :, :], in0=pC4[:, :, 1, :], in1=bc2(mI), op=ALU.mult), reads=[pCk, "rwc"], writes=["ArkT"])
        yield
        self._pe_serial_next = True
        idb = self.ident[:].rearrange("p (o i) -> p o i", o=1).to_broadcast([128, 2, 128])
        P.op("pool", lambda e: e.tensor_tensor(out=T_[:], in0=X_[:], in1=idb, op=ALU.add), reads=[("X2", 0), "ident"], writes=[("TT", 0)])
        yield
        cur = 0
        for r in range(1, 7):
            nx = 1 - cur
            Nc, Xc, Tc = N2r[cur], X2r[cur], TTr[cur]
            Nn, Xn, Tn = N2r[nx], X2r[nx], TTr[nx]
            pn, pnk = self.next_ps()
            P.op("pe", [lambda e, h=h, Nc=Nc, Xc=Xc, pn=pn: e.matmul(pn[:, h * 128:(h + 1) * 128], lhsT=Xc[:, h, :], rhs=Nc[:, h, :], start=True, stop=True)
                        for h in range(2)], reads=[("X2", cur), ("N2", cur)], writes=[pnk])
            yield
            P.op("act", lambda e, Nn=Nn, pn=pn: e.copy(out=Nn[:], in_=h2v(pn, 128)), reads=[pnk], writes=[("N2", nx)])
            yield
            if r < 6:
                px, pxk = self.next_ps()
                P.op("pe", [lambda e, h=h, Nc=Nc, Xc=Xc, px=px: e.matmul(px[:, h * 128:(h + 1) * 128], lhsT=Nc[:, h, :], rhs=Xc[:, h, :], start=True, stop=True)
                            for h in range(2)], reads=[("X2", cur), ("N2", cur)], writes=[pxk])
                yield
                P.op("act", lambda e, Xn=Xn, px=px: e.copy(out=Xn[:], in_=h2v(px, 128)), reads=[pxk], writes=[("X2", nx)])
                yield
            pt_, ptk = self.next_ps()
            P.op("pe", [lambda e, h=h, Nn=Nn, Tc=Tc, pt_=pt_: e.matmul(pt_[:, h * 128:(h + 1) * 128], lhsT=Nn[:, h, :], rhs=Tc[:, h, :], start=True, stop=True)
                        for h in range(2)], reads=[("N2", nx), ("TT", cur)], writes=[ptk])
            yield
            P.op("dve", lambda e, Tn=Tn, Tc=Tc, pt_=pt_: e.tensor_tensor(out=Tn[:], in0=h2v(pt_, 128), in1=Tc[:], op=ALU.add),
                 reads=[ptk, ("TT", cur)], writes=[("TT", nx)])
            yield
            cur = nx
        Tf, Tfk = self.TTb_d[d], "TTb"
        P.op("pool", lambda e: e.tensor_copy(out=Tf[:], in_=TTr[cur][:]), reads=[("TT", cur)], writes=["TTb"])
        yield
        if "norc2" in self.flags:
            return
        ptk_, ptkk = self.next_ps()
        pv = ptk_[:].bitcast(BF16)
        srcs = [(TLar[:, c, 0, :], "TLa"), (Bh[:, cs], "Bh"), (Kh[:, cs], "Kh"), (Vb[:, cs], "Vb")]
        P.op("pe", [lambda e, q=q, s=s: e.transpose(pv[:, q * 128:(q + 1) * 128], s[0], self.identb[:]) for q, s in enumerate(srcs)],
             reads=["TLa", "Bh", "Kh", "Vb", "identb"], writes=[ptkk])
        yield
        P.op("act", lambda e: e.copy(out=TOKr[:], in_=pv[:, 0:512].rearrange("p (q i) -> p q i", q=4)), reads=[ptkk], writes=["TOK"])
        yield
        pw, pwk = self.next_ps()
        P.op("pe", [lambda e, h=h: e.matmul(pw[:, h * 128:(h + 1) * 128], lhsT=TOKr[:, 0, :], rhs=Tf[:, h, :], start=True, stop=True) for h in range(2)],
             reads=["TOK", Tfk], writes=[pwk])
        yield
        P.op("act", lambda e: e.copy(out=W1Tr[0:64, :], in_=pw[0:64, 0:128]), reads=[pwk], writes=["W1Ta"])
        yield
        P.op("dve", lambda e: e.tensor_copy(out=W1Tr[64:128, :], in_=pw[64:128, 128:256]), reads=[pwk], writes=["W1Tb"])
        yield
        pz, pzk = self.next_ps()
        P.op("pe", [lambda e, h=h: e.matmul(pz[:, h * 64:(h + 1) * 64], lhsT=ATr[:, 1, h, :], rhs=TOKr[:, 3, h * 64:(h + 1) * 64], start=True, stop=True)
                    for h in range(2)], reads=["AakT", "TOK"], writes=[pzk])
        yield
        P.op("act", lambda e: e.copy(out=Zr_[:], in_=pz[:, 0:128]), reads=[pzk], writes=["Zrw"])
        yield
        pu0, pu0k = self.next_ps()
        P.op("pe", [lambda e, h=h: e.matmul(pu0[:, h * 64:(h + 1) * 64], lhsT=Tf[:, h, :], rhs=Zr_[:, h * 64:(h + 1) * 64], start=True, stop=True)
                    for h in range(2)], reads=[Tfk, "Zrw"], writes=[pu0k])
        yield
        P.op("dve", lambda e: e.tensor_copy(out=U0r[:], in_=pu0[:, 0:128]), reads=[pu0k], writes=["U0"])
        yield
        if "norc3" in self.flags:
            return
        n = stn["n"]; stn["n"] += 1
        Sc, Sck = STb[n % 2], ("STb", n % 2)
        Sn, Snk = STb[(n + 1) % 2], ("STb", (n + 1) % 2)
        pkb, pkbk = self.next_ps()
        P.op("pe", lambda e: e.matmul(pkb[:, 0:128], lhsT=TOKr[:, 2, :], rhs=TOKr[:, 3, :], start=True, stop=False), reads=["TOK"], writes=[pkbk])
        yield
        pu, puk = self.next_ps()
        P.op("pe", lambda e: e.matmul(pu[:, 0:128], lhsT=W1Tr[:], rhs=Sc[:], start=True, stop=True), reads=["W1Ta", "W1Tb", Sck], writes=[puk])
        yield
        P.op("dve", lambda e: e.tensor_tensor(out=U2r[:], in0=pu[:, 0:128], in1=U0r[:], op=ALU.add), reads=[puk, "U0"], writes=["U2"])
        yield
        P.op("pe", lambda e: e.matmul(pkb[:, 0:128], lhsT=TOKr[:, 1, :], rhs=U2r[:], start=False, stop=True), reads=["TOK", "U2"], writes=[pkbk])
        yield
        po, pok = self.next_ps()
        fns = [lambda e: e.matmul(po[:, 0:128], lhsT=TLar[:, c, 1, :], rhs=Sc[:], start=True, stop=False)]
        for h in range(2):
            fns.append(lambda e, h=h: e.matmul(po[:, h * 64:(h + 1) * 64], lhsT=ATr[:, 0, h, :], rhs=U2r[:, h * 64:(h + 1) * 64], start=False, stop=False))
            fns.append(lambda e, h=h: e.matmul(po[:, h * 64:(h + 1) * 64], lhsT=ATr[:, 2, h, :], rhs=TOKr[:, 3, h * 64:(h + 1) * 64], start=False, stop=(h == 1)))
        P.op("pe", fns, reads=["TLr", Sck, "ArbT", "ArkT", "U2", "TOK"], writes=[pok])
        yield
        if "norc4" in self.flags:
            return
        tS = ext[0]
        P.op("dve", lambda e: e.tensor_tensor(out=tS[:, 0:128], in0=pkb[:, 0:128], in1=cst[:, 4, :], op=ALU.mult), reads=[pkbk, "rwc"], writes=["tS"])
        yield
        P.op("dve", lambda e: e.scalar_tensor_tensor(out=ST32[:], in0=ST32[:], scalar=pc[:, c:c + 1], in1=tS[:, 0:128], op0=ALU.mult, op1=ALU.add),
             reads=["ST32", "tS", pck], writes=["ST32"])
        yield
        P.op("act", lambda e: e.copy(out=Sn[:], in_=ST32[:]), reads=["ST32"], writes=[Snk])
        yield
        osb = ext[1]
        P.op("act", lambda e: e.copy(out=osb[:, 0:128], in_=po[:, 0:128]), reads=[pok], writes=["osb"])
        yield
        pot, potk = self.next_ps()
        P.op("pe", lambda e: e.transpose(pot[:, 0:128], osb[:, 0:128], self.ident[:]), reads=["osb", "ident"], writes=[potk])
        yield
        lo = sq_ * g.L + t * tw + c * CH
        P.op("dve", lambda e: e.tensor_copy(out=oacc[:, lo:lo + CH], in_=pot[:, 0:128]), reads=[potk], writes=[("oacc", lo)])
        yield

    def rwkv_post(self, hp, g, j, cst, prm, gne, oacc, ms, msk, tmpA, SL, PGG, PGB):
        P = self.P
        rwS = self.rwS
        oacc0, oacc1 = oacc
        oacc = oacc0
        for t in range(g.ntile):
            sl_ = slice(t * TT, (t + 1) * TT)
            ok0 = [("ns", 0, ("oacc", t * TT + q * 128)) for q in range(4)]
            ok1 = [("ns", 1, ("oacc", t * TT + q * 128)) for q in range(4)]
            P.op("dve", lambda e, sl_=sl_: e.tensor_tensor(out=oacc0[:, sl_], in0=oacc0[:, sl_], in1=oacc1[:, sl_], op=ALU.add), reads=ok0 + ok1, writes=ok0)
            okeys = ok0
            gt0 = g.tok0 + t * TT
            bon, bonk = tmpA(); gs, gsk = tmpA(); sq, sqk = tmpA(); mean, meank = tmpA(); on, onk = tmpA()
            P.dma("sp", bon[:], rwS[hp, SL["bon"], :, gt0:gt0 + TT], reads=[("rwS", hp, "bon", gt0)], writes=[bonk])
            P.dma("sp", gs[:], rwS[hp, SL["gs"], :, gt0:gt0 + TT], reads=[("rwS", hp, "gs", gt0)], writes=[gsk])
            ob, sb2 = self.post_b
            P.op("act", lambda e, sl_=sl_: e.activation(out=sb2[:], in_=oacc[:, sl_], func=AF.Square), reads=okeys, writes=["post_sq"])
            P.op("act", lambda e, sl_=sl_: e.copy(out=ob[:], in_=oacc[:, sl_]), reads=okeys, writes=["post_ob"])
            pm, pmk = self.next_ps(); p2, p2k = self.next_ps()
            P.op("pe", lambda e, pm=pm: e.matmul(pm[:], lhsT=self.cstb_[:, 4, :], rhs=ob[:], start=True, stop=True), reads=["post_ob", "rwcb"], writes=[pmk])
            P.op("pe", lambda e, p2=p2: e.matmul(p2[:], lhsT=self.cstb_[:, 4, :], rhs=sb2[:], start=True, stop=True), reads=["post_sq", "rwcb"], writes=[p2k])
            P.op("act", lambda e, mean=mean, pm=pm: e.activation(out=mean[:], in_=pm[:], func=AF.Identity, scale=1.0 / 64.0), reads=[pmk], writes=[meank])
            P.op("dve", lambda e, sq=sq, mean=mean: e.tensor_tensor(out=sq[:], in0=mean[:], in1=mean[:], op=ALU.mult), reads=[meank], writes=[sqk])
            P.op("dve", lambda e, sq=sq, p2=p2: e.scalar_tensor_tensor(out=sq[:], in0=p2[:], scalar=1.0 / 64.0, in1=sq[:], op0=ALU.mult, op1=ALU.subtract),
                 reads=[p2k, sqk], writes=[sqk])
            P.op("act", lambda e, sq=sq: e.activation(out=sq[:], in_=sq[:], func=AF.Sqrt, bias=gne[:, 0:1]), reads=[sqk, "gne"], writes=[sqk])
            P.op("dve", lambda e, sq=sq: e.reciprocal(out=sq[:], in_=sq[:]), reads=[sqk], writes=[sqk])
            P.op("dve", lambda e, on=on, mean=mean, sl_=sl_: e.tensor_tensor(out=on[:], in0=oacc[:, sl_], in1=mean[:], op=ALU.subtract),
                 reads=okeys + [meank], writes=[onk])
            P.op("dve", lambda e, on=on, sq=sq: e.tensor_tensor(out=on[:], in0=on[:], in1=sq[:], op=ALU.mult), reads=[onk, sqk], writes=[onk])
            P.op("dve", lambda e, on=on: e.tensor_scalar(out=on[:], in0=on[:], scalar1=prm[:, PGG, hp:hp + 1], scalar2=prm[:, PGB, hp:hp + 1],
                                                         op0=ALU.mult, op1=ALU.add), reads=[onk, "rwprm"], writes=[onk])
            P.op("dve", lambda e, on=on, bon=bon: e.tensor_tensor(out=on[:], in0=on[:], in1=bon[:], op=ALU.add), reads=[onk, bonk], writes=[onk])
            if g.name == "s":
                out = ms[:, :].rearrange("p (r c) -> p c r", c=64)[:, t * 8:(t + 1) * 8, :]
                a0 = on[:].rearrange("p (c r) -> p c r", r=64); a1 = gs[:].rearrange("p (c r) -> p c r", r=64)
            else:
                out, a0, a1 = ms[:, sl_], on[:], gs[:]
            P.op("dve", lambda e, out=out, a0=a0, a1=a1: e.tensor_tensor(out=out, in0=a0, in1=a1, op=ALU.mult), reads=[onk, gsk], writes=[msk + (t,)])
        P.dma("sp", self.mixT[512 + hp * 128:512 + (hp + 1) * 128, g.tok0:g.tok0 + g.ntok], ms[:],
              reads=[msk + (t,) for t in range(g.ntile)], writes=[("mixT", 4 + hp, g.tok0 + t * TT) for t in range(g.ntile)])


WEIGHT_SHAPES = {
    "w_mod": [4, 1024, 6144], "b_mod": [4, 6144], "ln1_g": [4, 1024], "ln1_b": [4, 1024], "ln2_g": [4, 1024], "ln2_b": [4, 1024],
    "mlp_w1": [4, 1024, 4096], "mlp_w2": [4, 4096, 1024], "w_out": [4, 1024, 1024], "ab_w_in": [2, 1024, 2560],
    "sc_conv": [2, 3, 512], "lru_conv": [2, 4, 512], "lru_conv_b": [2, 512], "lru_wa": [2, 2, 8, 64, 64], "lru_ba": [2, 2, 512],
    "lru_wi": [2, 2, 8, 64, 64], "lru_bi": [2, 2, 512], "lru_lambda": [2, 2, 512], "cd_w_in": [2, 1024, 3584],
    "hy_conv": [2, 3, 1536], "hy_w1": [2, 33, 64], "hy_b1": [2, 64], "hy_w2": [2, 64, 64], "hy_b2": [2, 64], "hy_w3": [2, 64, 2048],
    "hy_freq": [2, 64], "hy_bias": [2, 2, 512], "rw_mu": [2, 4, 512], "rw_mu_x": [2, 2, 1024], "rw_w0": [2, 2, 512],
    "rw_w1": [2, 2, 1024, 64], "rw_w2": [2, 2, 64, 512], "rw_a0": [2, 2, 512], "rw_a1": [2, 2, 1024, 64], "rw_a2": [2, 2, 64, 512],
    "rw_kk": [2, 512], "rw_ka": [2, 512], "rw_rk": [2, 8, 64], "rw_gn_g": [2, 512], "rw_gn_b": [2, 512],
}

CONST_SPECS = [
    ("rw_masks", [6, 128, 128], F32),
    ("TC4096", [32, 128, 32, 128], BF16), ("TS4096", [32, 128, 32, 128], BF16),
    ("IC4096", [8, 128, 32, 512], BF16), ("IS4096", [8, 128, 32, 512], BF16),
    ("TC256", [2, 128, 2, 128], BF16), ("TS256", [2, 128, 2, 128], BF16),
    ("IC256", [1, 128, 2, 256], BF16), ("IS256", [1, 128, 2, 256], BF16),
    ("featT4096", [33, 4096], F32), ("featT256", [33, 256], F32),
    ("ntn4096", [128, 32], F32), ("ntn256", [128, 2], F32),
    ("delta_b", [128, 512], F32), ("m1col", [128, 1], F32),
]
_CONSTS = None


def make_consts():
    global _CONSTS
    if _CONSTS is not None:
        return _CONSTS
    c = {}
    i = np.arange(128)[:, None]; jj = np.arange(128)[None, :]
    bd = (i // 64 == jj // 64)
    c["rw_masks"] = np.stack([jj < i, jj <= i, jj > i, jj >= i, bd, bd / 64.0]).astype(np.float32)
    bf = ml_dtypes.bfloat16
    for L, sfx, TI in ((4096, "4096", 512), (256, "256", 256)):
        nb = L // 128
        k = np.arange(L, dtype=np.float64) + 0.5
        s_ = np.arange(L, dtype=np.float64)
        th = np.mod(np.outer(s_, k), 2.0 * L) * (np.pi / L)
        C = np.cos(th); S = -np.sin(th)
        for nm, M in (("TC", C), ("TS", S)):
            c[nm + sfx] = np.ascontiguousarray(M.reshape(nb, 128, nb, 128).transpose(2, 1, 0, 3)).astype(bf)
        for nm, M in (("IC", C), ("IS", S)):
            c[nm + sfx] = np.ascontiguousarray(M.reshape(L // TI, TI, nb, 128).transpose(0, 3, 2, 1)).astype(bf)
        del C, S, th
        tn = np.linspace(0.0, 1.0, L, dtype=np.float32)
        tr = np.arange(L, dtype=np.float32)
        bands = np.linspace(1e-4, 15.0, 16, dtype=np.float32)
        ang = (np.float32(2.0 * np.pi / L) * tr[:, None] * bands[None, :]).astype(np.float32)
        feats = np.concatenate([tn[:, None], np.cos(ang), -np.sin(ang)], -1).astype(np.float32)
        c["featT" + sfx] = np.ascontiguousarray(feats.T)
        c["ntn" + sfx] = np.ascontiguousarray((-tn).reshape(nb, 128).T)
    deltas = np.abs(np.linspace(np.log(1e-2) / 1.5, np.log(1e-2) / 0.3, 512, dtype=np.float32))
    c["delta_b"] = np.ascontiguousarray(np.broadcast_to(deltas[None, :], (128, 512))).astype(np.float32)
    m1 = np.ones((128, 1), np.float32); m1[0, 0] = 0.0
    c["m1col"] = m1
    _CONSTS = c
    return c


_NC_CACHE = {}


def _get_nc(depth=4, groups=("s", "p")):
    key = (depth, tuple(groups))
    if key not in _NC_CACHE:
        _NC_CACHE[key] = Builder(depth=depth, groups=groups).build()
    return _NC_CACHE[key]


def make_in_maps(inputs):
    f = lambda a: np.ascontiguousarray(np.asarray(a, dtype=np.float32))
    w = {k: f(inputs[k]) for k in WEIGHT_SHAPES}
    ident = np.eye(128, dtype=np.float32)
    w.update(make_consts())
    maps = []
    for i in range(NCORES):
        m = dict(w)
        m["xs"] = f(inputs["x_sample"][i])
        m["xp"] = f(inputs["x_prompt"][4 * i:4 * i + 4]).reshape(1024, D)
        m["st_lru"] = f(inputs["state_lru"][i])
        m["st_rwkv"] = f(inputs["state_rwkv"][i])
        m["cvecs"] = np.ascontiguousarray(np.stack([inputs["c"][i], inputs["c_ctx"]]).astype(np.float32))
        m["ident"] = ident
        maps.append(m)
    return maps


def kernel(**inputs):
    nc = _get_nc()
    maps = make_in_maps(inputs)
    res = run_bass_kernel_spmd(nc, maps, core_ids=list(range(NCORES)))
    r = res.results
    y_prompt = np.concatenate([np.asarray(r[i]["yp"]).reshape(4, 256, D) for i in range(NCORES)], 0)
    y_sample = np.stack([np.asarray(r[i]["ys"]) for i in range(NCORES)], 0)
    nlru = np.concatenate([np.asarray(r[i]["nlru"]) for i in range(NCORES)], 0)
    nrwkv = np.concatenate([np.asarray(r[i]["nrwkv"]) for i in range(NCORES)], 0)
    return (y_prompt.astype(np.float32), y_sample.astype(np.float32), nlru.astype(np.float32), nrwkv.astype(np.float32))
```
